# Optimizing a Trainium2 kernel written in Bass

```python
import math
import jax, jax.numpy as jnp
from jax import lax
import numpy as np

D_MODEL = 1024
BATCH = 8
SEQ = 2048
DEPTH = 2

GRID_W = 64
CTX_LEN = 256
HEAD_DIM = 64
ROPE_THETA = 10000.0
Q_BLOCK = 128

MLA_HEADS = 8
MLA_Q_RANK = 256
MLA_KV_RANK = 128
MLA_NOPE = 64
MLA_ROPE = 32
MLA_V = 64
SWA_HEADS = 8
SWA_KV_HEADS = 2
SWA_GROUP = SWA_HEADS // SWA_KV_HEADS
WINDOW = 128
DIFF_HEADS = D_MODEL // (2 * HEAD_DIM)
D_FF = 2816

ALPHA = (2 * DEPTH) ** 0.25
BETA = (8 * DEPTH) ** -0.25
LN_EPS = 1e-5
RMS_EPS = 1e-6
NEG_INF = -1e30
N_EVEN = (DEPTH + 1) // 2
N_ODD = DEPTH // 2

EVEN_Q_SIZES = (MLA_Q_RANK, SWA_HEADS * HEAD_DIM)
EVEN_KV_SIZES = (MLA_KV_RANK, MLA_ROPE, SWA_KV_HEADS * HEAD_DIM, SWA_KV_HEADS * HEAD_DIM)
EVEN_Q_COLS = sum(EVEN_Q_SIZES)
EVEN_IN = EVEN_Q_COLS + sum(EVEN_KV_SIZES)
EVEN_OUT = MLA_HEADS * MLA_V + SWA_HEADS * HEAD_DIM
DIFF_QK_COLS = DIFF_HEADS * 2 * HEAD_DIM
ODD_IN = 3 * DIFF_QK_COLS
ODD_OUT = DIFF_HEADS * 2 * HEAD_DIM

kernel_name = "hybrid_mla_swa_diffattn_dit_block"


def layer_norm(x, g, b):
    xf = x.astype(jnp.float32)
    mu = jnp.mean(xf, -1, keepdims=True)
    var = jnp.mean(jnp.square(xf - mu), -1, keepdims=True)
    return ((xf - mu) * lax.rsqrt(var + LN_EPS) * g + b).astype(x.dtype)


def rms_norm(x, g):
    xf = x.astype(jnp.float32)
    return (xf * lax.rsqrt(jnp.mean(jnp.square(xf), -1, keepdims=True) + RMS_EPS) * g).astype(x.dtype)


def split_cols(t, sizes):
    idx = np.cumsum(np.array(sizes))[:-1].tolist()
    return jnp.split(t, idx, axis=-1)


def axial_rope_tables(rows, dim):
    row = jnp.repeat(jnp.arange(rows), GRID_W).astype(jnp.float32)
    col = jnp.tile(jnp.arange(GRID_W), rows).astype(jnp.float32)
    quarter = dim // 4
    freqs = ROPE_THETA ** (-jnp.arange(quarter, dtype=jnp.float32) / quarter)
    ar = row[:, None] * freqs
    ac = col[:, None] * freqs
    cos = jnp.concatenate([jnp.cos(ar), jnp.cos(ar), jnp.cos(ac), jnp.cos(ac)], -1)
    sin = jnp.concatenate([jnp.sin(ar), jnp.sin(ar), jnp.sin(ac), jnp.sin(ac)], -1)
    return cos, sin


def apply_rope(x, cos, sin):
    shape = (cos.shape[0],) + (1,) * (x.ndim - 3) + (cos.shape[1],)
    cos = cos.reshape(shape)
    sin = sin.reshape(shape)
    a, b, c_, d = jnp.split(x, 4, axis=-1)
    rot = jnp.concatenate([-b, a, -d, c_], -1)
    return (x * cos + rot * sin).astype(x.dtype)


def sweep_query_blocks(fn, q):
    B, S = q.shape[:2]
    nb = S // Q_BLOCK
    blocks = jnp.moveaxis(q.reshape((B, nb, Q_BLOCK) + q.shape[2:]), 1, 0)
    out = jnp.moveaxis(lax.map(fn, blocks), 0, 1)
    return out.reshape((B, S) + out.shape[3:])


def softmax_attend(q, k, v, scale):
    s = jnp.einsum('bqhd,bkhd->bhqk', q, k).astype(jnp.float32) * scale
    p = jax.nn.softmax(s, axis=-1).astype(v.dtype)
    return jnp.einsum('bhqk,bkhd->bqhd', p, v)


def mla_queries(q_a, q_norm, w_qb, cos, sin):
    B, T = q_a.shape[:2]
    q = (rms_norm(q_a, q_norm) @ w_qb).reshape(B, T, MLA_HEADS, MLA_NOPE + MLA_ROPE)
    if cos is None:
        return q
    return jnp.concatenate([q[..., :MLA_NOPE], apply_rope(q[..., MLA_NOPE:], cos, sin)], -1)


def mla_keys_values(kv_a, k_rope, kv_norm, w_kvb, cos, sin):
    B, T = kv_a.shape[:2]
    kv = (rms_norm(kv_a, kv_norm) @ w_kvb).reshape(B, T, MLA_HEADS, MLA_NOPE + MLA_V)
    if cos is not None:
        k_rope = apply_rope(k_rope, cos, sin)
    k_rope = jnp.broadcast_to(k_rope[:, :, None, :], (B, T, MLA_HEADS, MLA_ROPE))
    return jnp.concatenate([kv[..., :MLA_NOPE], k_rope], -1), kv[..., MLA_NOPE:]


def window_attend(q, k, v, k_ctx, v_ctx, sink):
    B, S = q.shape[:2]
    L = k_ctx.shape[1]
    nb = S // WINDOW
    qb = q.reshape(B, nb, WINDOW, SWA_KV_HEADS, SWA_GROUP, HEAD_DIM)

    def band(t):
        tb = t.reshape(B, nb, WINDOW, SWA_KV_HEADS, HEAD_DIM)
        tp = jnp.pad(tb, ((0, 0), (1, 1), (0, 0), (0, 0), (0, 0)))
        return jnp.concatenate([tp[:, :-2], tp[:, 1:-1], tp[:, 2:]], axis=2)

    kb, vb = band(k), band(v)
    qpos = jnp.arange(nb)[:, None] * WINDOW + jnp.arange(WINDOW)[None, :]
    kpos = (jnp.arange(nb)[:, None] - 1) * WINDOW + jnp.arange(3 * WINDOW)[None, :]
    valid = ((jnp.abs(qpos[:, :, None] - kpos[:, None, :]) <= WINDOW)
             & (kpos[:, None, :] >= 0) & (kpos[:, None, :] < S))
    scale = HEAD_DIM ** -0.5
    s_band = jnp.einsum('bnqhgd,bnkhd->bnhgqk', qb, kb).astype(jnp.float32) * scale
    s_band = jnp.where(valid[None, :, None, None], s_band, NEG_INF)
    s_ctx = jnp.einsum('bnqhgd,bchd->bnhgqc', qb, k_ctx).astype(jnp.float32) * scale
    sink_col = jnp.broadcast_to(sink.reshape(SWA_KV_HEADS, SWA_GROUP, 1, 1).astype(jnp.float32),
                                s_ctx.shape[:-1] + (1,))
    p = jax.nn.softmax(jnp.concatenate([s_ctx, s_band, sink_col], -1), axis=-1)
    p_ctx = p[..., :L].astype(v.dtype)
    p_band = p[..., L:L + 3 * WINDOW].astype(v.dtype)
    o = (jnp.einsum('bnhgqc,bchd->bnqhgd', p_ctx, v_ctx)
         + jnp.einsum('bnhgqk,bnkhd->bnqhgd', p_band, vb))
    return o.reshape(B, S, SWA_HEADS * HEAD_DIM)


def sink_attend(q, k, v, sink):
    B, T = q.shape[:2]
    qg = q.reshape(B, T, SWA_KV_HEADS, SWA_GROUP, HEAD_DIM)
    s = jnp.einsum('bqhgd,bkhd->bhgqk', qg, k).astype(jnp.float32) * HEAD_DIM ** -0.5
    sink_col = jnp.broadcast_to(sink.reshape(SWA_KV_HEADS, SWA_GROUP, 1, 1).astype(jnp.float32),
                                s.shape[:-1] + (1,))
    p = jax.nn.softmax(jnp.concatenate([s, sink_col], -1), axis=-1)[..., :-1].astype(v.dtype)
    return jnp.einsum('bhgqk,bkhd->bqhgd', p, v).reshape(B, T, SWA_HEADS * HEAD_DIM)


def even_mixer(h_lat, h_ctx, w_in, q_norm, w_qb, kv_norm, w_kvb, sink, w_out, rope32, rope64, need_ctx):
    B, S = h_lat.shape[:2]
    L = h_ctx.shape[1]
    cos32, sin32 = rope32
    cos64, sin64 = rope64
    qa_l, qs_l, kva_l, kr_l, ks_l, vs_l = split_cols(h_lat @ w_in, EVEN_Q_SIZES + EVEN_KV_SIZES)
    kva_c, kr_c, ks_c, vs_c = split_cols(h_ctx @ w_in[:, EVEN_Q_COLS:], EVEN_KV_SIZES)
    mla_scale = (MLA_NOPE + MLA_ROPE) ** -0.5
    q_mla = mla_queries(qa_l, q_norm, w_qb, cos32, sin32)
    k_mla_l, v_mla_l = mla_keys_values(kva_l, kr_l, kv_norm, w_kvb, cos32, sin32)
    k_mla_c, v_mla_c = mla_keys_values(kva_c, kr_c, kv_norm, w_kvb, None, None)
    k_all = jnp.concatenate([k_mla_c, k_mla_l], axis=1)
    v_all = jnp.concatenate([v_mla_c, v_mla_l], axis=1)
    o_mla = sweep_query_blocks(lambda qb: softmax_attend(qb, k_all, v_all, mla_scale), q_mla)
    o_mla = o_mla.reshape(B, S, MLA_HEADS * MLA_V)
    q_swa = apply_rope(qs_l.reshape(B, S, SWA_HEADS, HEAD_DIM), cos64, sin64)
    k_swa = apply_rope(ks_l.reshape(B, S, SWA_KV_HEADS, HEAD_DIM), cos64, sin64)
    v_swa = vs_l.reshape(B, S, SWA_KV_HEADS, HEAD_DIM)
    k_swa_c = ks_c.reshape(B, L, SWA_KV_HEADS, HEAD_DIM)
    v_swa_c = vs_c.reshape(B, L, SWA_KV_HEADS, HEAD_DIM)
    o_swa = window_attend(q_swa, k_swa, v_swa, k_swa_c, v_swa_c, sink)
    out_l = jnp.concatenate([o_mla, o_swa], -1) @ w_out
    if not need_ctx:
        return out_l, None
    qa_c, qs_c = split_cols(h_ctx @ w_in[:, :EVEN_Q_COLS], EVEN_Q_SIZES)
    q_mla_c = mla_queries(qa_c, q_norm, w_qb, None, None)
    o_mla_c = softmax_attend(q_mla_c, k_mla_c, v_mla_c, mla_scale).reshape(B, L, MLA_HEADS * MLA_V)
    o_swa_c = sink_attend(qs_c.reshape(B, L, SWA_HEADS, HEAD_DIM), k_swa_c, v_swa_c, sink)
    out_c = jnp.concatenate([o_mla_c, o_swa_c], -1) @ w_out
    return out_l, out_c


def diff_attend(q, k, v, lam, lambda_init, subln_g):
    s = jnp.einsum('bqhtd,bkhtd->bthqk', q, k).astype(jnp.float32) * HEAD_DIM ** -0.5
    p = jax.nn.softmax(s, axis=-1)
    a = (p[:, 0] - lam * p[:, 1]).astype(v.dtype)
    o = jnp.einsum('bhqk,bkhe->bqhe', a, v)
    o = rms_norm(o, subln_g) * (1.0 - lambda_init)
    return o.reshape(o.shape[0], o.shape[1], DIFF_HEADS * 2 * HEAD_DIM)


def odd_mixer(h_lat, h_ctx, w_in, lam_q1, lam_k1, lam_q2, lam_k2, subln_g, w_out, rope64,
              lambda_init, need_ctx):
    B, S = h_lat.shape[:2]
    L = h_ctx.shape[1]
    cos, sin = rope64
    q_l, k_l, v_l = jnp.split(h_lat @ w_in, 3, axis=-1)
    q_l = apply_rope(q_l.reshape(B, S, DIFF_HEADS, 2, HEAD_DIM), cos, sin)
    k_l = apply_rope(k_l.reshape(B, S, DIFF_HEADS, 2, HEAD_DIM), cos, sin)
    v_l = v_l.reshape(B, S, DIFF_HEADS, 2 * HEAD_DIM)
    k_c, v_c = jnp.split(h_ctx @ w_in[:, DIFF_QK_COLS:], 2, axis=-1)
    k_c = k_c.reshape(B, L, DIFF_HEADS, 2, HEAD_DIM)
    v_c = v_c.reshape(B, L, DIFF_HEADS, 2 * HEAD_DIM)
    lam = (jnp.exp(jnp.sum(lam_q1 * lam_k1).astype(jnp.float32))
           - jnp.exp(jnp.sum(lam_q2 * lam_k2).astype(jnp.float32)) + lambda_init)
    k_all = jnp.concatenate([k_c, k_l], axis=1)
    v_all = jnp.concatenate([v_c, v_l], axis=1)
    o_l = sweep_query_blocks(lambda qb: diff_attend(qb, k_all, v_all, lam, lambda_init, subln_g), q_l)
    out_l = o_l @ w_out
    if not need_ctx:
        return out_l, None
    q_c = (h_ctx @ w_in[:, :DIFF_QK_COLS]).reshape(B, L, DIFF_HEADS, 2, HEAD_DIM)
    out_c = diff_attend(q_c, k_c, v_c, lam, lambda_init, subln_g) @ w_out
    return out_l, out_c


def conv_ffn(h, w_gate, w_up, conv_w, conv_b, w_down):
    g = h @ w_gate
    gp = jnp.pad(g, ((0, 0), (1, 1), (0, 0)))
    g = gp[:, :-2] * conv_w[0] + gp[:, 1:-1] * conv_w[1] + gp[:, 2:] * conv_w[2] + conv_b
    return (jax.nn.silu(g) * (h @ w_up)) @ w_down


def modulation(cond, w_mod, b_mod):
    return jnp.split(jax.nn.silu(cond) @ w_mod + b_mod, 6, axis=-1)


def setup_inputs(seed: int = 0) -> dict:
    key = jax.random.key(seed)
    ks = iter(jax.random.split(key, 40))

    def nrm(shape, scale):
        return jax.random.normal(next(ks), shape, jnp.float32) * scale

    def gain(shape):
        return 1.0 + nrm(shape, 0.01)

    D, F = D_MODEL, D_FF
    return {
        "x": nrm((BATCH, SEQ, D), 1.0),
        "c": nrm((BATCH, D), 1.0),
        "ctx": nrm((BATCH, CTX_LEN, D), 1.0),
        "c_ctx": nrm((D,), 1.0),
        "w_mod": nrm((DEPTH, D, 6 * D), 0.5 * D ** -0.5),
        "b_mod": nrm((DEPTH, 6 * D), 0.01),
        "ln1_g": gain((DEPTH, D)),
        "ln1_b": nrm((DEPTH, D), 0.01),
        "ln2_g": gain((DEPTH, D)),
        "ln2_b": nrm((DEPTH, D), 0.01),
        "ffn_w_gate": nrm((DEPTH, D, F), D ** -0.5),
        "ffn_w_up": nrm((DEPTH, D, F), D ** -0.5),
        "ffn_conv_w": nrm((DEPTH, 3, F), 3 ** -0.5),
        "ffn_conv_b": nrm((DEPTH, F), 0.01),
        "ffn_w_down": nrm((DEPTH, F, D), BETA * F ** -0.5),
        "ab_w_in": nrm((N_EVEN, D, EVEN_IN), D ** -0.5),
        "mla_q_norm": gain((N_EVEN, MLA_Q_RANK)),
        "mla_w_qb": nrm((N_EVEN, MLA_Q_RANK, MLA_HEADS * (MLA_NOPE + MLA_ROPE)), MLA_Q_RANK ** -0.5),
        "mla_kv_norm": gain((N_EVEN, MLA_KV_RANK)),
        "mla_w_kvb": nrm((N_EVEN, MLA_KV_RANK, MLA_HEADS * (MLA_NOPE + MLA_V)), MLA_KV_RANK ** -0.5),
        "swa_sink": nrm((N_EVEN, SWA_HEADS), 0.5),
        "ab_w_out": nrm((N_EVEN, EVEN_OUT, D), BETA * EVEN_OUT ** -0.5),
        "diff_w_in": nrm((N_ODD, D, ODD_IN), D ** -0.5),
        "diff_lam_q1": nrm((N_ODD, HEAD_DIM), 0.1),
        "diff_lam_k1": nrm((N_ODD, HEAD_DIM), 0.1),
        "diff_lam_q2": nrm((N_ODD, HEAD_DIM), 0.1),
        "diff_lam_k2": nrm((N_ODD, HEAD_DIM), 0.1),
        "diff_subln_g": gain((N_ODD, 2 * HEAD_DIM)),
        "diff_w_out": nrm((N_ODD, ODD_OUT, D), BETA * ODD_OUT ** -0.5),
    }


def reference(x, c, ctx, c_ctx, w_mod, b_mod, ln1_g, ln1_b, ln2_g, ln2_b,
              ffn_w_gate, ffn_w_up, ffn_conv_w, ffn_conv_b, ffn_w_down,
              ab_w_in, mla_q_norm, mla_w_qb, mla_kv_norm, mla_w_kvb, swa_sink, ab_w_out,
              diff_w_in, diff_lam_q1, diff_lam_k1, diff_lam_q2, diff_lam_k2, diff_subln_g, diff_w_out):
    S = x.shape[1]
    ROWS = S // GRID_W
    rope32 = axial_rope_tables(ROWS, MLA_ROPE)
    rope64 = axial_rope_tables(ROWS, HEAD_DIM)
    xl, xc = x, ctx
    for i in range(DEPTH):
        need_ctx = i < DEPTH - 1
        j = i // 2
        sh1_l, sc1_l, gt1_l, sh2_l, sc2_l, gt2_l = [t[:, None, :] for t in modulation(c, w_mod[i], b_mod[i])]
        sh1_c, sc1_c, gt1_c, sh2_c, sc2_c, gt2_c = modulation(c_ctx, w_mod[i], b_mod[i])
        h_l = xl * (1.0 + sc1_l) + sh1_l
        h_c = xc * (1.0 + sc1_c) + sh1_c
        if i % 2 == 0:
            o_l, o_c = even_mixer(h_l, h_c, ab_w_in[j], mla_q_norm[j], mla_w_qb[j], mla_kv_norm[j],
                                  mla_w_kvb[j], swa_sink[j], ab_w_out[j], rope32, rope64, need_ctx)
        else:
            lambda_init = 0.8 - 0.6 * math.exp(-0.3 * i)
            o_l, o_c = odd_mixer(h_l, h_c, diff_w_in[j], diff_lam_q1[j], diff_lam_k1[j], diff_lam_q2[j],
                                 diff_lam_k2[j], diff_subln_g[j], diff_w_out[j], rope64, lambda_init, need_ctx)
        xl = layer_norm(ALPHA * xl + gt1_l * o_l, ln1_g[i], ln1_b[i])
        h_l = xl * (1.0 + sc2_l) + sh2_l
        f_l = conv_ffn(h_l, ffn_w_gate[i], ffn_w_up[i], ffn_conv_w[i], ffn_conv_b[i], ffn_w_down[i])
        xl = layer_norm(ALPHA * xl + gt2_l * f_l, ln2_g[i], ln2_b[i])
        if need_ctx:
            xc = layer_norm(ALPHA * xc + gt1_c * o_c, ln1_g[i], ln1_b[i])
            h_c = xc * (1.0 + sc2_c) + sh2_c
            f_c = conv_ffn(h_c, ffn_w_gate[i], ffn_w_up[i], ffn_conv_w[i], ffn_conv_b[i], ffn_w_down[i])
            xc = layer_norm(ALPHA * xc + gt2_c * f_c, ln2_g[i], ln2_b[i])
    return xl
```

```python
import math
import os
import numpy as np
import ml_dtypes
import concourse.bass as bass
import concourse.mybir as mybir
from concourse.bass_utils import run_bass_kernel_spmd

F32 = mybir.dt.float32
BF16 = mybir.dt.bfloat16
AF = mybir.ActivationFunctionType
ALU = mybir.AluOpType

D = 1024
DFF = 2816
NTOK = 2304
NT = 18
NCTX_T = 2
ALPHA = 4.0 ** 0.25
LN_EPS = 1e-5
RMS_EPS = 1e-6
NFC = 22
NPASS = 2
FCP = NFC // NPASS


class _Rec:
    def __init__(self):
        self.call = None

    def __getattr__(self, name):
        def f(*a, **k):
            self.call = (name, a, k)
            return self
        return f


def _record(fn):
    r = _Rec()
    fn(r)
    assert r.call is not None
    return r.call


class Sched:
    ENGS = ("pe", "act", "dve", "pool", "sp")

    def __init__(self, nc, same_engine_sync=True):
        self.nc = nc
        self.streams = {e: [] for e in self.ENGS}
        self.cnt = {e: 0 for e in self.ENGS}
        self.waited = {}
        self.lastw = {}
        self.readers = {}
        self.semcnt = {}
        self.same = same_engine_sync

    def _deps(self, reads, writes):
        deps = {}

        def add(k, v):
            if deps.get(k, 0) < v:
                deps[k] = v

        for r in reads:
            t = self.lastw.get(r)
            if t is not None:
                add(*t)
        for w in writes:
            t = self.lastw.get(w)
            if t is not None:
                add(*t)
            for k, v in self.readers.get(w, {}).items():
                add(k, v)
        return deps

    def _commit(self, tok, reads, writes):
        k, v = tok
        for r in reads:
            d = self.readers.setdefault(r, {})
            if d.get(k, 0) < v:
                d[k] = v
        for w in writes:
            self.lastw[w] = tok
            self.readers[w] = {}

    def _waits(self, eng, deps):
        waits = []
        for k, v in deps.items():
            if k == eng and (eng == "pe" or not self.same):
                continue
            if self.waited.get((eng, k), 0) >= v:
                continue
            self.waited[(eng, k)] = v
            waits.append((k, v))
        return waits

    def op(self, eng, fn, reads=(), writes=()):
        deps = self._deps(reads, writes)
        waits = self._waits(eng, deps)
        self.cnt[eng] += 1
        tok = (eng, self.cnt[eng])
        self.semcnt[eng] = self.cnt[eng]
        self.streams[eng].append((waits, _record(fn), (eng, 1)))
        self._commit(tok, reads, writes)
        return tok

    def dma(self, q, semkey, fn, reads=(), writes=()):
        deps = self._deps(reads, writes)
        prev = self.semcnt.get(semkey, 0)
        if prev and deps.get(semkey, 0) < prev:
            deps[semkey] = prev
        waits = self._waits(q, deps)
        self.semcnt[semkey] = prev + 16
        tok = (semkey, prev + 16)
        self.streams[q].append((waits, _record(fn), (semkey, 16)))
        self._commit(tok, reads, writes)
        return tok

    def wait_all(self, eng):
        waits = []
        for k, v in self.semcnt.items():
            if k == eng:
                continue
            if self.waited.get((eng, k), 0) >= v:
                continue
            self.waited[(eng, k)] = v
            waits.append((k, v))
        self.streams[eng].append((waits, None, None))

    def barrier(self):
        for e in self.ENGS:
            self.wait_all(e)

    def emit(self):
        nc = self.nc
        sems = {}
        for i, k in enumerate(self.semcnt):
            sems[k] = nc.alloc_semaphore(name=f"sm{i}")
        streams = self.streams

        def run(engname, eng):
            for waits, fn, inc in streams[engname]:
                for k, v in waits:
                    eng.wait_ge(sems[k], v)
                if fn is None:
                    continue
                name, a_, k_ = fn
                ins = getattr(eng, name)(*a_, **k_)
                if inc is not None:
                    ins.then_inc(sems[inc[0]], inc[1])

        with nc.Block() as block:
            @block.tensor
            def _(e):
                run("pe", e)

            @block.scalar
            def _(e):
                run("act", e)

            @block.vector
            def _(e):
                run("dve", e)

            @block.gpsimd
            def _(e):
                run("pool", e)

            @block.sync
            def _(e):
                run("sp", e)
        return len(sems)


class Arena:
    def __init__(self, nc, S):
        self.nc = nc
        self.S = S
        self.base = (nc.sbuf_base + 63) // 64 * 64
        self.top = nc.sbuf_top
        self.cur = self.base
        self.n = 0
        self.peak = 0

    def alloc(self, name, shape, dt):
        per = 1
        for s in shape[1:]:
            per *= s
        per *= 2 if dt == BF16 else 4
        off = self.cur
        self.cur = (off + per + 63) // 64 * 64
        assert self.cur <= self.top, f"SBUF overflow allocating {name}: {self.cur} > {self.top}"
        self.peak = max(self.peak, self.cur)
        self.n += 1
        return self.nc.alloc_sbuf_tensor_at(f"{name}_{self.n}", list(shape), dt, offset=off)

    def mark(self):
        return self.cur

    def release(self, m):
        self.S.barrier()
        self.cur = m


def build_program(layers=(0, 1), dbg=(), stop_after=None):
    nc = bass.Bass("TRN2", target_bir_lowering=False)
    S = Sched(nc)
    A = Arena(nc, S)
    first_layer, last_layer = layers[0], layers[-1]

    def din(name, shape, dt=F32):
        return nc.dram_tensor(name, list(shape), dt, kind="ExternalInput").ap()

    def dscr(name, shape, dt=F32):
        return nc.dram_tensor(name, list(shape), dt, kind="Internal").ap()

    x_all = din("x_all", [NTOK, D])
    cT_d = din("cT", [128, 8, 2])
    wmod_d = din("wmod", [2, 12, 128, 8, 512])
    bmodpp_d = din("bmod_pp", [2, 128, 48])
    bmod_d = din("bmod", [2, 6144])
    lnv_d = din("lnv", [2, 4, D])
    wgu_d = din("wgu", [2, 11, 128, 8, 512])
    convp_d = din("convp", [2, 128, NFC, 4])
    wdown_d = din("wdown", [2, NPASS, 128, FCP, D])
    w0_d = din("w0", [128, 16, 8, 128])
    wqb_d = din("wqb", [128, 8, 2, 2, 96])
    wkvb_d = din("wkvb", [128, 1024])
    qnorm_d = din("qnorm_pp", [128, 2])
    kvnorm_d = din("kvnorm_pp", [128, 1])
    sink_d = din("sink", [8])
    wout0_d = din("wout0", [128, 8, D])
    w1h_d = din("w1h", [8, 128, 3, 8, 128])
    rmat_d = din("rmat", [128, 128], BF16)
    wout1_d = din("wout1", [128, 8, D])
    lamv_d = din("lamv", [4, 64])
    subg_d = din("subg", [128])
    identf_d = din("ident_f", [128, 128])
    identb_d = din("ident_b", [128, 128], BF16)
    masks_d = din("masks", [2, 128, 512], BF16)
    rope64_d = din("rope64", [2, 128, NTOK])
    rope32_d = din("rope32", [2, 128, NTOK])
    out_d = nc.dram_tensor("out", [2048, D], F32, kind="ExternalOutput").ap()
    xres = [dscr("xres_a", [NTOK, D]), dscr("xres_b", [NTOK, D])]
    ybuf = dscr("ybuf", [NTOK, D])
    gts_d = dscr("gts", [2, 2, 2, 128, D])
    dbg_out = {}

    P2 = [nc.alloc_psum_tensor(f"pp{i}", [128, 1024], F32) for i in range(4)]

    def bank(i):
        return P2[i // 2][:, (i % 2) * 512:(i % 2 + 1) * 512]

    ident_f = A.alloc("ident_f", [128, 128], F32)
    ident_b = A.alloc("ident_b", [128, 128], BF16)
    ones_f = A.alloc("ones_f", [128, 128], F32)
    epsb = A.alloc("epsb", [128, 2], F32)
    masks = A.alloc("masks", [128, 2, 512], BF16)
    mpp2 = A.alloc("mpp", [128, 2, 4, 8, 2], F32)
    hT = A.alloc("hT", [128, 8, NTOK], BF16)
    gt_t = A.alloc("gt_t", [128, 2, D], F32)
    lng_t = A.alloc("lng_t", [128, D], F32)
    lnb_t = A.alloc("lnb_t", [128, D], F32)

    class _Stop(Exception):
        pass

    def checkpoint(name, tensors):
        if name in dbg or stop_after == name:
            S.barrier()
            for nm, (t_ap, shape, dt) in tensors.items():
                d = nc.dram_tensor("dbg_" + nm, list(shape), dt, kind="ExternalOutput").ap()
                cdma(lambda e: e.dma_start(out=d, in_=t_ap))
        if stop_after == name:
            raise _Stop()

    cq = [0]

    def cdma(fn, reads=(), writes=(), q="sp"):
        cq[0] += 1
        return S.dma(q, ("c", cq[0] % 4), fn, reads=reads, writes=writes)

    cdma(lambda e: e.dma_start(out=ident_f[:], in_=identf_d), writes=["ident_f"])
    cdma(lambda e: e.dma_start(out=ident_b[:], in_=identb_d), writes=["ident_b"])
    cdma(lambda e: e.dma_start(out=masks[:], in_=masks_d.rearrange("m p n -> p m n")), writes=["masks"])
    S.op("dve", lambda e: e.memset(ones_f[:], 1.0), writes=["ones_f"])
    S.op("dve", lambda e: e.memset(epsb[:, 0:1], LN_EPS), writes=["epsb"])
    S.op("dve", lambda e: e.memset(epsb[:, 1:2], RMS_EPS), writes=["epsb"])

    def ttiles(a, b):
        return range(a // 128, (b + 127) // 128)

    def modulation_units(l, need_c_gates, pbank_pp, pbank_gt):
        st = {}
        nvar = 2 if need_c_gates else 1
        vi_of = {0: 0, 1: 1, 3: 2, 4: 3}

        def setup():
            st["m0"] = A.mark()
            cT = st["cT"] = A.alloc("cT", [128, 8, 2], F32)
            scf = st["scf"] = A.alloc("scf", [128, 8, 2], F32)
            scb = st["scb"] = A.alloc("scb", [128, 8, 2], BF16)
            cbc = st["cbc"] = A.alloc("cbc", [128, 2, 8, 128], BF16)
            bpp = st["bpp"] = A.alloc("bpp", [128, 48], F32)
            bmb = st["bmb"] = A.alloc("bmb", [128, 2, D], F32)
            st["gtb"] = A.alloc("gtb", [128, 2, 2, D], F32)
            st["wsl"] = [A.alloc(f"wm{i}", [128, 8, 512], BF16) for i in range(3)]
            cdma(lambda e: e.dma_start(out=cT[:], in_=cT_d), writes=[("cT", l)])
            cdma(lambda e: e.dma_start(out=bpp[:], in_=bmodpp_d[l]), writes=[("bpp", l)])
            for gi, vec in enumerate((2, 5)):
                cdma(lambda e, gi=gi, vec=vec: e.dma_start(out=bmb[:, gi, :], in_=bmod_d[l, vec * 1024:(vec + 1) * 1024].partition_broadcast(128)),
                     writes=[("bmb", l, gi)])
            S.op("act", lambda e: e.activation(out=scf[:], in_=cT[:], func=AF.Silu), reads=[("cT", l)], writes=[("scf", l)])
            S.op("dve", lambda e: e.tensor_copy(out=scb[:], in_=scf[:]), reads=[("scf", l)], writes=[("scb", l)])
            for v in range(nvar):
                for k in range(8):
                    S.op("dve", lambda e, v=v, k=k: e.tensor_scalar(out=cbc[:, v, k, :], in0=ones_f[:], scalar1=scf[:, k, v:v + 1], scalar2=None, op0=ALU.mult),
                         reads=[("scf", l), "ones_f"], writes=[("cbc", l, v)])

        def block(blk, n):
            def f():
                wsl, scb, cbc, bpp, bmb, gtb = st["wsl"], st["scb"], st["cbc"], st["bpp"], st["bmb"], st["gtb"]
                sl = n % 3
                vec, half = blk // 2, blk % 2
                S.dma("pool", ("wm", sl), lambda e: e.dma_start(out=wsl[sl][:], in_=wmod_d[l, blk], max_dma_last_dim=8192), writes=[("wm", l, sl)])
                if vec in vi_of:
                    vi = vi_of[vec]
                    pb = ("pb", pbank_pp)
                    for j in range(4):
                        for k in range(8):
                            S.op("pe", lambda e, j=j, k=k: e.matmul(bank(pbank_pp)[:, j * 2:j * 2 + 2], lhsT=wsl[sl][:, k, j * 128:(j + 1) * 128],
                                                                    rhs=scb[:, k, :], start=(k == 0), stop=(k == 7)),
                                 reads=[("wm", l, sl), ("scb", l)], writes=[pb])
                    c0 = half * 4
                    psv = bank(pbank_pp)[:, 0:8].rearrange("p (j v) -> p j v", v=2)
                    for v in range(2):
                        if vec in (1, 4):
                            S.op("dve", lambda e, v=v: e.scalar_tensor_tensor(
                                out=mpp2[:, l, vi, c0:c0 + 4, v], in0=psv[:, :, v], scalar=1.0, in1=bpp[:, vec * 8 + c0:vec * 8 + c0 + 4], op0=ALU.add, op1=ALU.add),
                                reads=[pb, ("bpp", l)], writes=[("mpp", l, vi, half, v)])
                        else:
                            S.op("dve", lambda e, v=v: e.tensor_tensor(
                                out=mpp2[:, l, vi, c0:c0 + 4, v], in0=psv[:, :, v], in1=bpp[:, vec * 8 + c0:vec * 8 + c0 + 4], op=ALU.add),
                                reads=[pb, ("bpp", l)], writes=[("mpp", l, vi, half, v)])
                else:
                    gi = 0 if vec == 2 else 1
                    pb = ("pb", pbank_gt)
                    for v in range(nvar):
                        for k in range(8):
                            S.op("pe", lambda e, k=k, v=v: e.matmul(bank(pbank_gt), lhsT=cbc[:, v, k, :], rhs=wsl[sl][:, k, :], start=(k == 0), stop=(k == 7)),
                                 reads=[("wm", l, sl), ("cbc", l, v)], writes=[pb])
                        S.op("dve", lambda e, v=v: e.tensor_tensor(
                            out=gtb[:, gi, v, half * 512:(half + 1) * 512], in0=bank(pbank_gt), in1=bmb[:, gi, half * 512:(half + 1) * 512], op=ALU.add),
                            reads=[pb, ("bmb", l, gi)], writes=[("gtb", l, gi, v, half)])
            return f

        order = (0, 1, 2, 3, 6, 7, 8, 9, 4, 5, 10, 11)
        blocks = [block(blk, n) for n, blk in enumerate(order)]

        def finish():
            gtb = st["gtb"]
            for gi in range(2):
                for v in range(nvar):
                    cdma(lambda e, gi=gi, v=v: e.dma_start(out=gts_d[l, gi, v], in_=gtb[:, gi, v, :]),
                         reads=[("gtb", l, gi, v, 0), ("gtb", l, gi, v, 1)], writes=[("gts", l, gi, v)])
            checkpoint(f"mod{l}", {"mpp": (mpp2[:, l], [128, 4, 8, 2], F32), "gtb": (gtb[:], [128, 2, 2, D], F32)})
            A.release(st["m0"])

        return setup, blocks, finish

    def load_sublayer_consts(l, sub, need_c):
        for v in range(2 if need_c else 1):
            cdma(lambda e, v=v: e.dma_start(out=gt_t[:, v, :], in_=gts_d[l, sub, v]), reads=[("gts", l, sub, v)], writes=[("gt_t", v)])
        cdma(lambda e: e.dma_start(out=lng_t[:], in_=lnv_d[l, 2 * sub].partition_broadcast(128)), writes=["lng_t"])
        cdma(lambda e: e.dma_start(out=lnb_t[:], in_=lnv_d[l, 2 * sub + 1].partition_broadcast(128)), writes=["lnb_t"])

    def phase_P(l, src, tiles, sub, fillers=()):
        m0 = A.mark()
        fillers = list(fillers)
        xt = [A.alloc(f"xtP{i}", [128, D], F32) for i in range(3)]
        vi_sh, vi_sc = 2 * sub, 2 * sub + 1
        vars_ = sorted(set(1 if t < NCTX_T else 0 for t in tiles))
        shb = {}
        for v in vars_:
            shb[v] = A.alloc(f"shb{v}", [128, 4, 128], F32)
            for kk in range(4):
                k = 2 * kk + 1
                S.op("dve", lambda e, v=v, kk=kk, k=k: e.tensor_scalar(out=shb[v][:, kk, :], in0=ones_f[:], scalar1=mpp2[:, l, vi_sh, k, v:v + 1], scalar2=None, op0=ALU.mult),
                     reads=["ones_f", ("mpp", l, vi_sh, k // 4, v)], writes=[("shb", v, kk)])

        def xload(n):
            if n < len(tiles):
                t_ = tiles[n]
                sl_ = n % 3
                S.dma("sp", ("xt", sl_), lambda e: e.dma_start(out=xt[sl_][:], in_=src[t_ * 128:(t_ + 1) * 128, :]),
                      reads=[(src.tensor.name, t_)], writes=[("xtP", sl_)])
        xload(0)
        xload(1)
        for n, t in enumerate(tiles):
            sl = n % 3
            v = 1 if t < NCTX_T else 0
            xload(n + 2)
            pp = P2[n % 2]
            for k in range(8):
                S.op("pe", lambda e, k=k: e.transpose(out=pp[:, k * 128:(k + 1) * 128], in_=xt[sl][:, k * 128:(k + 1) * 128], identity=ident_f[:]),
                     reads=[("xtP", sl), "ident_f"], writes=[("pb", 2 * (n % 2) + k // 4)])
            for k in range(8):
                rk = [("pb", 2 * (n % 2) + k // 4), ("mpp", l, vi_sc, k // 4, v), ("mpp", l, vi_sh, k // 4, v)]
                if True:
                    S.op("act", lambda e, k=k: e.activation(out=hT[:, k, t * 128:(t + 1) * 128], in_=pp[:, k * 128:(k + 1) * 128], func=AF.Identity,
                                                            scale=mpp2[:, l, vi_sc, k, v:v + 1], bias=mpp2[:, l, vi_sh, k, v:v + 1]),
                         reads=rk, writes=[("hT", t, k)])
                else:
                    S.op("dve", lambda e, k=k: e.scalar_tensor_tensor(out=hT[:, k, t * 128:(t + 1) * 128], in0=pp[:, k * 128:(k + 1) * 128],
                                                                      scalar=mpp2[:, l, vi_sc, k, v:v + 1], in1=shb[v][:, k // 2, :], op0=ALU.mult, op1=ALU.add),
                         reads=rk + [("shb", v, k // 2)], writes=[("hT", t, k)])
            if fillers:
                fillers.pop(0)()
        for f in fillers:
            f()
        A.release(m0)

    def hT_reads(a, b, k):
        return [("hT", t, k) for t in ttiles(a, b)]

    def phase_R(tiles, lhs_fn, lhs_reads_fn, nchunk, w_t, w_key, xsrc, first, last, dst, dst_row0, ytmp, fillers=()):
        m0 = A.mark()
        xt = [A.alloc(f"xtR{i}", [128, D], F32) for i in range(3)]
        NTMP = 4
        tmp = [A.alloc(f"tmpR{i}", [128, D], F32) for i in range(NTMP)]
        st = A.alloc("stR", [128, NTMP, 2, 6], F32)
        mv = A.alloc("mvR", [128, NTMP, 4], F32)
        rsrc = xsrc if first else ytmp

        def xload(n):
            if n < len(tiles):
                t_ = tiles[n]
                sl_ = n % 3
                S.dma("sp", ("xt", sl_), lambda e: e.dma_start(out=xt[sl_][:], in_=rsrc[t_ * 128:(t_ + 1) * 128, :]),
                      reads=[(rsrc.tensor.name, t_)], writes=[("xtR", sl_)])
        xload(0)
        xload(1)

        def stageA(n):
            t = tiles[n]
            sl = n % 3
            s2 = n % NTMP
            v = 1 if t < NCTX_T else 0
            xload(n + 2)
            p2i = n % 3
            pp = P2[p2i]
            for hf in range(2):
                for c in range(nchunk):
                    S.op("pe", lambda e, c=c, hf=hf: e.matmul(pp[:, hf * 512:(hf + 1) * 512], lhsT=lhs_fn(c, t), rhs=w_t[:, c, hf * 512:(hf + 1) * 512],
                                                              start=(c == 0), stop=(c == nchunk - 1)),
                         reads=lhs_reads_fn(c, t) + [w_key], writes=[("pb", 2 * p2i + hf)])
            pbk = [("pb", 2 * p2i), ("pb", 2 * p2i + 1)]
            S.op("dve", lambda e: e.tensor_tensor(out=tmp[s2][:], in0=pp[:], in1=gt_t[:, v, :], op=ALU.mult),
                 reads=pbk + [("gt_t", v)], writes=[("tmpR", s2)])
            S.op("dve", lambda e: e.scalar_tensor_tensor(out=tmp[s2][:], in0=xt[sl][:], scalar=(ALPHA if first else 1.0), in1=tmp[s2][:],
                                                         op0=ALU.mult, op1=ALU.add),
                 reads=[("xtR", sl), ("tmpR", s2)], writes=[("tmpR", s2)])
            if not last:
                S.dma("sp", ("yst", s2), lambda e: e.dma_start(out=ytmp[t * 128:(t + 1) * 128, :], in_=tmp[s2][:]),
                      reads=[("tmpR", s2)], writes=[(ytmp.tensor.name, t)])
                return
            for c in range(2):
                S.op("dve", lambda e, c=c: e.bn_stats(out=st[:, s2, c, :], in_=tmp[s2][:, c * 512:(c + 1) * 512]),
                     reads=[("tmpR", s2)], writes=[("stR", s2, c)])
            S.op("dve", lambda e: e.bn_aggr(out=mv[:, s2, 0:2], in_=st[:, s2, :, :].rearrange("p a b -> p (a b)")),
                 reads=[("stR", s2, 0), ("stR", s2, 1)], writes=[("mvR", s2)])
            S.op("act", lambda e: e.activation(out=mv[:, s2, 2:3], in_=mv[:, s2, 1:2], func=AF.Ln, bias=epsb[:, 0:1]),
                 reads=[("mvR", s2), "epsb"], writes=[("mvR", s2)])
            S.op("act", lambda e: e.activation(out=mv[:, s2, 2:3], in_=mv[:, s2, 2:3], func=AF.Exp, scale=-0.5),
                 reads=[("mvR", s2)], writes=[("mvR", s2)])

        def stageB(n):
            t = tiles[n]
            s2 = n % NTMP
            S.op("dve", lambda e: e.scalar_tensor_tensor(out=mv[:, s2, 3:4], in0=mv[:, s2, 0:1], scalar=-1.0, in1=mv[:, s2, 2:3], op0=ALU.mult, op1=ALU.mult),
                 reads=[("mvR", s2)], writes=[("mvR", s2)])
            S.op("act", lambda e: e.activation(out=tmp[s2][:], in_=tmp[s2][:], func=AF.Identity, scale=mv[:, s2, 2:3], bias=mv[:, s2, 3:4]),
                 reads=[("mvR", s2), ("tmpR", s2)], writes=[("tmpR", s2)])
            eng2 = "dve" if pool_free else "pool"
            S.op(eng2, lambda e: e.tensor_tensor(out=tmp[s2][:], in0=tmp[s2][:], in1=lng_t[:], op=ALU.mult),
                 reads=[("tmpR", s2), "lng_t"], writes=[("tmpR", s2)])
            S.op(eng2, lambda e: e.tensor_tensor(out=tmp[s2][:], in0=tmp[s2][:], in1=lnb_t[:], op=ALU.add),
                 reads=[("tmpR", s2), "lnb_t"], writes=[("tmpR", s2)])
            r0 = t * 128 - dst_row0
            if pool_free:
                S.dma("sp", ("yst", s2), lambda e: e.dma_start(out=dst[r0:r0 + 128, :], in_=tmp[s2][:]),
                      reads=[("tmpR", s2)], writes=[(dst.tensor.name, t)])
            else:
                S.dma("pool", ("ystp", s2), lambda e: e.dma_start(out=dst[r0:r0 + 128, :], in_=tmp[s2][:]),
                      reads=[("tmpR", s2)], writes=[(dst.tensor.name, t)])

        fillers = list(fillers)
        pool_free = bool(fillers)
        for n in range(len(tiles)):
            stageA(n)
            if last and n >= 1:
                stageB(n - 1)
            if fillers:
                fillers.pop(0)()
        if last:
            stageB(len(tiles) - 1)
        for f in fillers:
            f()
        A.release(m0)

    def run_attention(jobs, ET, fillers=()):
        steps = [(job, i) for job in jobs for i in range(len(job["keys"]))]
        fillers = list(fillers)
        fill_every = max(1, len(steps) // max(1, len(fillers))) if fillers else 0
        NSB = 4
        LA = NSB - 1

        def qk(si):
            job, ki = steps[si]
            j = job["keys"][ki]
            b = si % NSB
            for (c0, ncol, lhsT, rhs, view) in job["qk"](j):
                out = bank(b)[:, c0:c0 + ncol]
                if view is not None:
                    out = view(out)
                S.op("pe", lambda e, out=out, lhsT=lhsT, rhs=rhs: e.matmul(out, lhsT=lhsT, rhs=rhs, start=True, stop=True, skip_group_check=True),
                     reads=job["qk_reads"](j), writes=[("pb", b)])

        def ex(si):
            job, ki = steps[si]
            j = job["keys"][ki]
            b = si % NSB
            eb = si % len(ET)
            W = job["W"]
            o = ET[eb][:, 0:W]
            S.op("act", lambda e, o=o, b=b, W=W, job=job: e.activation(out=o, in_=bank(b)[:, 0:W], func=AF.Exp, scale=job["scale"]),
                 reads=[("pb", b)], writes=[("ET", eb)])
            mk = job["mask"](j) if job.get("mask") else None
            if mk is not None:
                S.op("dve", lambda e, o=o, mk=mk: e.tensor_tensor(out=o, in0=o, in1=mk, op=ALU.mult), reads=[("ET", eb), "masks"], writes=[("ET", eb)])

        def pv(si):
            job, ki = steps[si]
            j = job["keys"][ki]
            eb = si % len(ET)
            nkeys = len(job["keys"])
            for (ap_, key_, fib, ec0, rhs) in job["pv"](j):
                S.op("pe", lambda e, ap_=ap_, ec0=ec0, rhs=rhs, fib=fib: e.matmul(
                    ap_, lhsT=ET[eb][:, ec0:ec0 + 128], rhs=rhs, start=(ki == 0 and fib), stop=(ki == nkeys - 1), skip_group_check=True),
                    reads=[("ET", eb)] + job["pv_reads"](j), writes=[key_])
            if ki == nkeys - 1:
                return job["fin"]()
            return None

        pending = []
        for si in range(min(LA, len(steps))):
            qk(si)
        for si in range(len(steps)):
            if si + LA < len(steps):
                qk(si + LA)
            ex(si)
            pending = [(d - 1, f) for d, f in pending]
            due = [f for d, f in pending if d <= 0]
            pending = [(d, f) for d, f in pending if d > 0]
            for f in due:
                f()
            late = pv(si)
            if late is not None:
                if callable(late):
                    late = [(2, late)]
                pending.extend(late)
            if fillers and (si + 1) % fill_every == 0:
                fillers.pop(0)()
        for _, f in sorted(pending, key=lambda x: x[0]):
            f()
        for f in fillers:
            f()

    def rsqrt_small(dst, src, scale, eps_col, rkeys, wkey):
        S.op("act", lambda e: e.activation(out=dst, in_=src, func=AF.Ln, scale=scale, bias=epsb[:src.shape[0], eps_col:eps_col + 1]),
             reads=rkeys + ["epsb"], writes=[wkey])
        S.op("act", lambda e: e.activation(out=dst, in_=dst, func=AF.Exp, scale=-0.5), reads=[wkey], writes=[wkey])

    BLK_ALL = [(0, 512), (512, 1024), (1024, 1536), (1536, 2048), (2048, 2304)]
    BLK_LAT = [(256, 768), (768, 1280), (1280, 1792), (1792, 2304)]

    def layer0_attention(src, dst, mod0, mod1):
        setup0, blocks0, finish0 = mod0
        setup0()
        for f in blocks0[:4]:
            f()
        phase_P(0, src, list(range(NT)), 0, fillers=blocks0[4:])
        finish0()
        load_sublayer_consts(0, 0, True)
        checkpoint("P0", {"hT": (hT[:], [128, 8, NTOK], BF16)})
        mA = A.mark()
        qan = A.alloc("qan", [128, 2, NTOK], BF16)
        kvan = A.alloc("kvan", [128, NTOK], BF16)
        krT = A.alloc("krT", [96, NTOK], BF16)
        rope32 = A.alloc("rope32", [128, 2, NTOK], F32)
        wqb = A.alloc("wqb", [128, 8, 2, 2, 96], BF16)
        wkvb = A.alloc("wkvb", [128, 1024], BF16)
        qn_pp = A.alloc("qn_pp", [128, 2], F32)
        kvn_pp = A.alloc("kvn_pp", [128, 1], F32)
        esink = A.alloc("esink", [128, 8], F32)
        ET = [A.alloc(f"ET{i}", [128, 512], BF16) for i in range(4)]
        mB = A.mark()
        qsT = A.alloc("qsT", [128, 4, NTOK], BF16)
        ksT2 = [A.alloc(f"ksT{i}", [128, NTOK], BF16) for i in range(2)]
        Vs = A.alloc("Vs", [128, NT, 2, 65], BF16)
        mC = A.mark()
        w0 = A.alloc("w0", [128, 16, 8, 128], BF16)
        rope64 = A.alloc("rope64", [128, 2, NTOK], F32)
        t1 = [A.alloc(f"t1_{i}", [128, 512], F32) for i in range(2)]
        t2 = [A.alloc(f"t2_{i}", [128, 512], F32) for i in range(2)]
        qaf = A.alloc("qaf", [128, 2, 512], F32)
        qsq = A.alloc("qsq", [128, 2, 512], F32)
        rbc = A.alloc("rbc", [128, 512], F32)

        for r in range(2):
            cdma(lambda e, r=r: e.dma_start(out=rope32[:, r, :], in_=rope32_d[r]), writes=[("rope32", r)])
            cdma(lambda e, r=r: e.dma_start(out=rope64[:, r, :], in_=rope64_d[r]), writes=[("rope64", r)])
        cdma(lambda e: e.dma_start(out=qn_pp[:], in_=qnorm_d), writes=["qn_pp"])
        cdma(lambda e: e.dma_start(out=kvn_pp[:], in_=kvnorm_d), writes=["kvn_pp"])
        cdma(lambda e: e.dma_start(out=esink[:], in_=sink_d.partition_broadcast(128)), writes=["esink"])
        S.op("act", lambda e: e.activation(out=esink[:], in_=esink[:], func=AF.Exp), reads=["esink"], writes=["esink"])
        for g4 in range(4):
            S.dma("pool", ("w0", g4), lambda e, g4=g4: e.dma_start(out=w0[:, g4 * 4:(g4 + 1) * 4], in_=w0_d[:, g4 * 4:(g4 + 1) * 4], max_dma_last_dim=8192),
                  writes=[("w0", g4)])
        S.dma("pool", ("wq", 0), lambda e: e.dma_start(out=wqb[:], in_=wqb_d, max_dma_last_dim=8192), writes=["wqb"])
        S.dma("pool", ("wq", 1), lambda e: e.dma_start(out=wkvb[:], in_=wkvb_d, max_dma_last_dim=8192), writes=["wkvb"])
        S.op("dve", lambda e: e.memset(Vs[:, :, :, 64:65], 1.0), writes=["Vs1"])

        pbc = [0]

        def nextbank():
            pbc[0] = (pbc[0] + 1) % 8
            return pbc[0]

        def fm_proj(bk, chunk, a, b, m=128):
            for k in range(8):
                S.op("pe", lambda e, k=k: e.matmul(bank(bk)[0:m, 0:b - a], lhsT=w0[:, chunk, k, 0:m], rhs=hT[:, k, a:b], start=(k == 0), stop=(k == 7)),
                     reads=[("w0", chunk // 4)] + hT_reads(a, b, k), writes=[("pb", bk)])

        def rope_evac(bk_raw, bk_rot, tab, tabkey, p0, p1, a, b, dst_ap, wkey, i, dsts=None):
            n = b - a
            S.op("dve", lambda e: e.tensor_tensor(out=t1[i][p0:p1, 0:n], in0=bank(bk_raw)[p0:p1, 0:n], in1=tab[p0:p1, 0, a:b], op=ALU.mult),
                 reads=[("pb", bk_raw), (tabkey, 0)], writes=[("t1", i)])
            S.op("dve", lambda e: e.tensor_tensor(out=t2[i][p0:p1, 0:n], in0=bank(bk_rot)[p0:p1, 0:n], in1=tab[p0:p1, 1, a:b], op=ALU.mult),
                 reads=[("pb", bk_rot), (tabkey, 1)], writes=[("t2", i)])
            if dsts is None:
                dsts = [(p0, p1, dst_ap)]
            for (q0, q1, d_ap) in dsts:
                S.op("pool", lambda e, q0=q0, q1=q1, d_ap=d_ap: e.tensor_tensor(out=d_ap, in0=t1[i][q0:q1, 0:n], in1=t2[i][q0:q1, 0:n], op=ALU.add),
                     reads=[("t1", i), ("t2", i)], writes=[wkey])

        S.op("dve", lambda e: e.memset(ksT2[0][64:128, :], 0.0), writes=["ksTz"])
        S.op("dve", lambda e: e.memset(ksT2[1][0:64, :], 0.0), writes=["ksTz"])
        for bi, (a, b) in enumerate(BLK_ALL):
            n = b - a
            tl = list(ttiles(a, b))
            for c in range(4):
                br, bt = nextbank(), nextbank()
                fm_proj(br, 2 + c, a, b)
                fm_proj(bt, 6 + c, a, b)
                rope_evac(br, bt, rope64, "rope64", 0, 128, a, b, qsT[:, c, a:b], ("qsT", c, bi), (bi * 8 + c) % 2)
            br, bt = nextbank(), nextbank()
            fm_proj(br, 13, a, b)
            fm_proj(bt, 14, a, b)
            rope_evac(br, bt, rope64, "rope64", 0, 128, a, b, None, ("ksT", bi), 0,
                      dsts=[(0, 64, ksT2[0][0:64, a:b]), (64, 128, ksT2[1][64:128, a:b])])
            br, bt = nextbank(), nextbank()
            fm_proj(br, 11, a, b, m=96)
            fm_proj(bt, 12, a, b, m=96)
            rope_evac(br, bt, rope32, "rope32", 64, 96, a, b, krT[64:96, a:b], ("krT", bi), 1)
            bq = [nextbank(), nextbank()]
            for c in range(2):
                fm_proj(bq[c], c, a, b)
                S.op("act", lambda e, c=c, bq=bq: e.activation(out=qaf[:, c, 0:n], in_=bank(bq[c])[:, 0:n], func=AF.Copy), reads=[("pb", bq[c])], writes=[("qaf", c)])
                S.op("act", lambda e, c=c, bq=bq: e.activation(out=qsq[:, c, 0:n], in_=bank(bq[c])[:, 0:n], func=AF.Square), reads=[("pb", bq[c])], writes=[("qsq", c)])
            bs = nextbank()
            for c in range(2):
                S.op("pe", lambda e, c=c, bs=bs: e.matmul(bank(bs)[:, 0:n], lhsT=ones_f[:], rhs=qsq[:, c, 0:n], start=(c == 0), stop=(c == 1)),
                     reads=["ones_f", ("qsq", c)], writes=[("pb", bs)])
            rsqrt_small(rbc[:, 0:n], bank(bs)[:, 0:n], 1.0 / 256.0, 1, [("pb", bs)], "rbc")
            for c in range(2):
                S.op("dve", lambda e, c=c: e.scalar_tensor_tensor(out=qan[:, c, a:b], in0=qaf[:, c, 0:n], scalar=qn_pp[:, c:c + 1], in1=rbc[:, 0:n], op0=ALU.mult, op1=ALU.mult),
                     reads=[("qaf", c), "qn_pp", "rbc"], writes=[("qan", c, bi)])
            bq0 = nextbank()
            fm_proj(bq0, 10, a, b)
            S.op("act", lambda e, bq0=bq0: e.activation(out=qaf[:, 0, 0:n], in_=bank(bq0)[:, 0:n], func=AF.Copy), reads=[("pb", bq0)], writes=[("qaf", 0)])
            S.op("act", lambda e, bq0=bq0: e.activation(out=qsq[:, 0, 0:n], in_=bank(bq0)[:, 0:n], func=AF.Square), reads=[("pb", bq0)], writes=[("qsq", 0)])
            bs = nextbank()
            S.op("pe", lambda e, bs=bs: e.matmul(bank(bs)[:, 0:n], lhsT=ones_f[:], rhs=qsq[:, 0, 0:n], start=True, stop=True),
                 reads=["ones_f", ("qsq", 0)], writes=[("pb", bs)])
            rsqrt_small(rbc[:, 0:n], bank(bs)[:, 0:n], 1.0 / 128.0, 1, [("pb", bs)], "rbc")
            S.op("dve", lambda e: e.scalar_tensor_tensor(out=kvan[:, a:b], in0=qaf[:, 0, 0:n], scalar=kvn_pp[:, 0:1], in1=rbc[:, 0:n], op0=ALU.mult, op1=ALU.mult),
                 reads=[("qaf", 0), "kvn_pp", "rbc"], writes=[("kvan", bi)])
            bv = nextbank()
            for ti, t in enumerate(tl):
                for k in range(8):
                    S.op("pe", lambda e, k=k, t=t, ti=ti, bv=bv: e.matmul(bank(bv)[:, ti * 128:(ti + 1) * 128], lhsT=hT[:, k, t * 128:(t + 1) * 128], rhs=w0[:, 15, k, :],
                                                                         start=(k == 0), stop=(k == 7)),
                         reads=[("w0", 3), ("hT", t, k)], writes=[("pb", bv)])
            nt_ = len(tl)
            S.op("act", lambda e, bv=bv, t0=tl[0], nt_=nt_: e.activation(
                out=Vs[:, t0:t0 + nt_, :, 0:64], in_=bank(bv)[:, 0:nt_ * 128].rearrange("p (t g d) -> p t g d", g=2, d=64), func=AF.Copy),
                reads=[("pb", bv)], writes=[("Vs", bi)])
        checkpoint("proj0", {"qsT": (qsT[:], [128, 4, NTOK], BF16), "krT": (krT[64:96, :], [32, NTOK], BF16),
                             "qan": (qan[:], [128, 2, NTOK], BF16), "kvan": (kvan[:], [128, NTOK], BF16), "Vs": (Vs[:], [128, NT, 2, 65], BF16)})
        A.release(mC)

        otk = [A.alloc(f"otk{i}", [128, 256], BF16) for i in range(2)]
        den = [A.alloc(f"den{i}", [128, 8], F32) for i in range(2)]
        oT = hT
        _p3 = P2[3][:].bitcast(BF16)
        class _TPB:
            def __getitem__(self, idx):
                _, sl_ = idx
                a0 = sl_.start
                bk = a0 // 256
                off = bk * 1024 + (a0 - bk * 256)
                return _p3[:, off:off + (sl_.stop - sl_.start)]
        tpb = _TPB()
        jobs = []
        jn = [0]
        for g in range(2):
            for n_ in range(NT):
                if n_ < NCTX_T:
                    keys = [0, 1]
                else:
                    keys = [0, 1] + [j for j in (n_ - 1, n_, n_ + 1) if NCTX_T <= j < NT]
                aset = jn[0] % 2
                jn[0] += 1
                accb = 4 + aset

                def acc(qt, c, accb=accb):
                    return bank(accb)[:, qt * 65:(qt + 1) * 65], ("pb", accb), qt == 0

                def mask(j, n_=n_):
                    if n_ < NCTX_T:
                        return None
                    if j == n_ - 1 and j >= NCTX_T:
                        return masks[:, 0, :]
                    if j == n_ + 1:
                        return masks[:, 1, :]
                    return None

                def fin(g=g, n_=n_, accb=accb, aset=aset):
                    av = bank(accb)[:, 0:260].rearrange("p (h d) -> p h d", d=65)
                    akeys = [("pb", accb)]
                    S.op("dve", lambda e: e.tensor_tensor(out=den[aset][:, 0:4], in0=av[:, :, 64], in1=esink[:, g * 4:g * 4 + 4], op=ALU.add),
                         reads=akeys + ["esink"], writes=[("den", aset)])
                    S.op("dve", lambda e: e.reciprocal(out=den[aset][:, 4:8], in_=den[aset][:, 0:4]), reads=[("den", aset)], writes=[("den", aset)])
                    for c in range(4):
                        S.op("dve", lambda e, c=c: e.tensor_scalar(out=otk[aset][:, c * 64:(c + 1) * 64], in0=av[:, c, 0:64], scalar1=den[aset][:, 4 + c:5 + c], scalar2=None, op0=ALU.mult),
                             reads=[("pb", accb), ("den", aset)], writes=[("otk", aset, c // 2)])
                    def late():
                        for j2 in range(2):
                            slot = (aset * 2 + j2)
                            S.op("pe", lambda e, j2=j2, slot=slot: e.transpose(out=tpb[:, slot * 128:(slot + 1) * 128], in_=otk[aset][:, j2 * 128:(j2 + 1) * 128], identity=ident_b[:]),
                                 reads=[("otk", aset, j2), "ident_b"], writes=[("pb", 6 + aset)])
                        S.op("dve", lambda e: e.tensor_copy(out=oT[:, 4 + 2 * g:6 + 2 * g, n_ * 128:(n_ + 1) * 128],
                                                            in_=tpb[:, aset * 256:(aset + 1) * 256].rearrange("p (j n) -> p j n", j=2)),
                             reads=[("pb", 6 + aset)], writes=[("hT", n_, 4 + 2 * g), ("hT", n_, 5 + 2 * g)])
                    return late

                def qk_(j, g=g, n_=n_):
                    return [(0, 512, ksT2[g][:, j * 128:(j + 1) * 128], qsT[:, :, n_ * 128:(n_ + 1) * 128],
                             lambda ap: ap.rearrange("p (h n) -> p h n", h=4))]

                def pv_(j, g=g, accb=accb):
                    return [(bank(accb)[:, c * 65:(c + 1) * 65], ("pb", accb), c == 0, c * 128, Vs[:, j, g, :]) for c in range(4)]

                jobs.append(dict(
                    keys=keys, W=512, scale=64 ** -0.5,
                    qk=qk_, qk_reads=lambda j, n_=n_: [("ksT", j // 4), "ksTz"] + [("qsT", c, n_ // 4) for c in range(4)],
                    pv=pv_, pv_reads=lambda j: [("Vs", j // 4), "Vs1"],
                    mask=mask, fin=fin))
        run_attention(jobs, ET)
        checkpoint("swa0", {"oT": (hT[:], [128, 8, NTOK], BF16)})
        A.release(mB)

        Vm = A.alloc("Vm", [128, NT, 8, 65], BF16)
        qm = [A.alloc(f"qm{i}", [96, NTOK], BF16) for i in range(2)]
        km = [A.alloc(f"km{i}", [96, NTOK], BF16) for i in range(2)]
        ostg = [A.alloc(f"ostg{i}", [128, NT, 128], BF16) for i in range(2)]
        t1 = [A.alloc(f"t1m{i}", [128, 512], F32) for i in range(2)]
        t2 = [A.alloc(f"t2m{i}", [128, 512], F32) for i in range(2)]
        rden = [A.alloc(f"rden{i}", [128, 4], F32) for i in range(2)]
        S.op("dve", lambda e: e.memset(Vm[:, :, :, 64:65], 1.0), writes=["Vm1"])
        for t in range(NT):
            bv = 4 + t % 2
            S.op("pe", lambda e, t=t, bv=bv: e.matmul(bank(bv), lhsT=kvan[:, t * 128:(t + 1) * 128], rhs=wkvb[:, 512:1024], start=True, stop=True),
                 reads=[("kvan", t // 4), "wkvb"], writes=[("pb", bv)])
            S.op("dve", lambda e, t=t, bv=bv: e.tensor_copy(out=Vm[:, t, :, 0:64], in_=bank(bv).rearrange("p (h d) -> p h d", d=64)),
                 reads=[("pb", bv)], writes=[("Vm", t)])
        jobs = []
        jn = [0]
        for h in range(8):
            hb = h % 2
            def head_proj(h=h, hb=hb):
                for bi, (a, b) in enumerate(BLK_ALL):
                    n = b - a
                    br, bt, bk_ = 4, 5, 4
                    for kc in range(2):
                        S.op("pe", lambda e, kc=kc: e.matmul(bank(br)[0:96, 0:n], lhsT=wqb[:, h, 0, kc, :], rhs=qan[:, kc, a:b], start=(kc == 0), stop=(kc == 1)),
                             reads=["wqb", ("qan", kc, bi)], writes=[("pb", br)])
                    for kc in range(2):
                        S.op("pe", lambda e, kc=kc: e.matmul(bank(bt)[0:96, 0:n], lhsT=wqb[:, h, 1, kc, :], rhs=qan[:, kc, a:b], start=(kc == 0), stop=(kc == 1)),
                             reads=["wqb", ("qan", kc, bi)], writes=[("pb", bt)])
                    S.op("act", lambda e: e.activation(out=qm[hb][0:64, a:b], in_=bank(br)[0:64, 0:n], func=AF.Copy), reads=[("pb", br)], writes=[("qm", hb, bi, 0)])
                    i = bi % 2
                    S.op("dve", lambda e: e.tensor_tensor(out=t1[i][64:96, 0:n], in0=bank(br)[64:96, 0:n], in1=rope32[64:96, 0, a:b], op=ALU.mult),
                         reads=[("pb", br), ("rope32", 0)], writes=[("t1", i)])
                    S.op("dve", lambda e: e.tensor_tensor(out=t2[i][64:96, 0:n], in0=bank(bt)[64:96, 0:n], in1=rope32[64:96, 1, a:b], op=ALU.mult),
                         reads=[("pb", bt), ("rope32", 1)], writes=[("t2", i)])
                    S.op("pool", lambda e: e.tensor_tensor(out=qm[hb][64:96, a:b], in0=t1[i][64:96, 0:n], in1=t2[i][64:96, 0:n], op=ALU.add),
                         reads=[("t1", i), ("t2", i)], writes=[("qm", hb, bi, 1)])
                    S.op("pe", lambda e: e.matmul(bank(bk_)[0:64, 0:n], lhsT=wkvb[:, h * 64:(h + 1) * 64], rhs=kvan[:, a:b], start=True, stop=True),
                         reads=["wkvb", ("kvan", bi)], writes=[("pb", bk_)])
                    S.op("act", lambda e: e.activation(out=km[hb][0:64, a:b], in_=bank(bk_)[0:64, 0:n], func=AF.Copy), reads=[("pb", bk_)], writes=[("km", hb, bi, 0)])
                    S.op("pool", lambda e: e.tensor_copy(out=km[hb][64:96, a:b], in_=krT[64:96, a:b]), reads=[("krT", bi)], writes=[("km", hb, bi, 1)])

            qblocks = [(0, 256, [0, 1])] + [(a, b, list(range(NT))) for (a, b) in BLK_LAT]
            for qi, (qa_, qb_, keys) in enumerate(qblocks):
                nq = (qb_ - qa_) // 128
                aset = jn[0] % 2
                jn[0] += 1
                accb = 4 + aset if False else (6 + aset)

                def acc(qt, c, accb=accb):
                    return bank(accb)[:, qt * 65:(qt + 1) * 65], ("pb", accb), qt == 0

                def fin(h=h, hb=hb, qa_=qa_, nq=nq, accb=accb, aset=aset):
                    av = bank(accb)[:, 0:nq * 65].rearrange("p (h d) -> p h d", d=65)
                    akeys = [("pb", accb)]
                    S.op("dve", lambda e: e.reciprocal(out=rden[aset][:, 0:nq], in_=av[:, :, 64]), reads=akeys, writes=[("rden", aset)])
                    for qt in range(nq):
                        t = qa_ // 128 + qt
                        S.op("dve", lambda e, qt=qt, t=t: e.tensor_scalar(out=ostg[(h // 2) % 2][:, t, hb * 64:(hb + 1) * 64], in0=av[:, qt, 0:64], scalar1=rden[aset][:, qt:qt + 1],
                                                                      scalar2=None, op0=ALU.mult),
                             reads=[("pb", accb), ("rden", aset)], writes=[("ostg", (h // 2) % 2, t, hb)])
                    if hb == 1:
                        def late():
                            tb = P2[2][:].bitcast(BF16)[:, 1024:2048]
                            t0 = qa_ // 128
                            for qt in range(nq):
                                t = t0 + qt
                                S.op("pe", lambda e, t=t, qt=qt: e.transpose(out=tb[:, qt * 128:(qt + 1) * 128], in_=ostg[(h // 2) % 2][:, t, :], identity=ident_b[:]),
                                     reads=[("ostg", (h // 2) % 2, t, 0), ("ostg", (h // 2) % 2, t, 1), "ident_b"], writes=[("pb", 5)])
                            S.op("dve", lambda e: e.tensor_copy(out=oT[:, h // 2, t0 * 128:(t0 + nq) * 128], in_=tb[:, 0:nq * 128]),
                                 reads=[("pb", 5)], writes=[("hT", t0 + qt, h // 2) for qt in range(nq)])
                        return late
                    return None

                def qk_(j, hb=hb, qa_=qa_, qb_=qb_):
                    return [(0, qb_ - qa_, km[hb][0:96, j * 128:(j + 1) * 128], qm[hb][0:96, qa_:qb_], None)]

                def pv_(j, h=h, accb=accb, nq=nq):
                    return [(bank(accb)[:, qt * 65:(qt + 1) * 65], ("pb", accb), qt == 0, qt * 128, Vm[:, j, h, :]) for qt in range(nq)]

                jobs.append(dict(
                    pre=(head_proj if qi == 0 else None),
                    keys=keys, W=qb_ - qa_, scale=96 ** -0.5,
                    qk=qk_, qk_reads=lambda j, hb=hb: [("km", hb, j // 4, 0), ("km", hb, j // 4, 1)] + [("qm", hb, bi, r) for bi in range(5) for r in range(2)],
                    pv=pv_, pv_reads=lambda j: [("Vm", j), "Vm1"],
                    mask=None, fin=fin))
        hj = 0
        while hj < len(jobs):
            jobs[hj]["pre"]()
            run_attention(jobs[hj:hj + 5], ET)
            hj += 5
        checkpoint("mla0", {"oT": (hT[:], [128, 8, NTOK], BF16)})
        A.release(mA)

        mR = A.mark()
        wo = A.alloc("wo", [128, 8, D], BF16)
        for hf in range(2):
            S.dma("pool", ("wo", hf), lambda e, hf=hf: e.dma_start(out=wo[:, hf * 4:(hf + 1) * 4, :], in_=wout0_d[:, hf * 4:(hf + 1) * 4, :], max_dma_last_dim=8192), writes=["wo"])
        fl = []
        if mod1 is not None:
            setup1, fl, finish1 = mod1
            setup1()
        phase_R(list(range(NT)), lambda c, t: oT[:, c, t * 128:(t + 1) * 128], lambda c, t: [("hT", t, c)], 8, wo, "wo",
                src, True, True, dst, 0, None, fillers=fl)
        if mod1 is not None:
            finish1()
        A.release(mR)

    def ffn_sublayer(l, src, dst, dst_row0, tiles):
        need_c = tiles[0] < NCTX_T
        load_sublayer_consts(l, 1, need_c)
        phase_P(l, src, tiles, 1)
        tok0, tok1 = tiles[0] * 128, (tiles[-1] + 1) * 128
        blocks = [(a, min(a + 512, tok1)) for a in range(tok0, tok1, 512)]
        m0 = A.mark()
        convp = A.alloc("convp", [128, NFC, 4], F32)
        wsl = [A.alloc(f"wg{i}", [128, 8, 512], BF16) for i in range(3)]
        wdn = A.alloc("wdn", [128, FCP, D], BF16)
        hid = A.alloc("hid", [128, FCP, NTOK], BF16)
        cdma(lambda e: e.dma_start(out=convp[:], in_=convp_d[l]), writes=["convp"])
        def gcol(g):
            return g + 1 if g < 256 else g + 3
        for ps in range(NPASS):
            m1 = A.mark()
            G = [A.alloc(f"G{i}", [128, NTOK + 4], F32) for i in range(2)]
            T1 = [A.alloc(f"T1_{i}", [128, NTOK + 4], F32) for i in range(2)]
            for i in range(2):
                for cpad in (0, 257, 2307):
                    w = 2 if cpad == 257 else 1
                    S.op("dve", lambda e, i=i, cpad=cpad, w=w: e.memset(G[i][:, cpad:cpad + w], 0.0), writes=[("Gpad", i)])
            for ci in range(FCP):
                fc = ps * FCP + ci
                blk11 = fc // 2
                sub = fc % 2
                sl = blk11 % 3
                if sub == 0 or ci == 0:
                    S.dma("pool", ("wg", sl), lambda e, blk11=blk11, sl=sl: e.dma_start(out=wsl[sl][:], in_=wgu_d[l, blk11], max_dma_last_dim=8192),
                          writes=[("wg", sl)])
                if ci == 2:
                    S.dma("pool", ("wdn", 0), lambda e, ps=ps: e.dma_start(out=wdn[:], in_=wdown_d[l, ps], max_dma_last_dim=8192), writes=["wdn"])
                wv = wsl[sl][:].rearrange("p k (g f) -> p k g f", g=2)
                gi = ci % 2
                c0, c1 = gcol(tok0), gcol(tok1 - 1) + 1
                for bi, (a, b) in enumerate(blocks):
                    n = b - a
                    st_ = (ci * len(blocks) + bi) % 4
                    bg, bu = 2 * st_, 2 * st_ + 1
                    for k in range(8):
                        S.op("pe", lambda e, k=k, bg=bg, wv=wv: e.matmul(bank(bg)[:, 0:n], lhsT=wv[:, k, 0, sub * 128:(sub + 1) * 128], rhs=hT[:, k, a:b], start=(k == 0), stop=(k == 7)),
                             reads=[("wg", sl)] + hT_reads(a, b, k), writes=[("pb", bg)])
                    for k in range(8):
                        S.op("pe", lambda e, k=k, bu=bu, wv=wv: e.matmul(bank(bu)[:, 0:n], lhsT=wv[:, k, 1, sub * 128:(sub + 1) * 128], rhs=hT[:, k, a:b], start=(k == 0), stop=(k == 7)),
                             reads=[("wg", sl)] + hT_reads(a, b, k), writes=[("pb", bu)])
                    segs = []
                    if a < 256 < b:
                        segs = [(a, 256), (256, b)]
                    else:
                        segs = [(a, b)]
                    for (sa, sb) in segs:
                        S.op("act", lambda e, sa=sa, sb=sb, bg=bg, gi=gi: e.activation(out=G[gi][:, gcol(sa):gcol(sa) + sb - sa], in_=bank(bg)[:, sa - a:sb - a], func=AF.Copy),
                             reads=[("pb", bg)], writes=[("G", gi, bi)])
                    S.op("dve", lambda e, bu=bu, ci=ci: e.tensor_copy(out=hid[:, ci, a:b], in_=bank(bu)[:, 0:n]), reads=[("pb", bu)], writes=[("hid", ci, bi)])
                gk = [("G", gi, bi) for bi in range(len(blocks))] + [("Gpad", gi)]
                S.op("act", lambda e, gi=gi, fc=fc: e.activation(out=T1[gi][:, c0:c1], in_=G[gi][:, c0:c1], func=AF.Identity, scale=convp[:, fc, 1:2], bias=convp[:, fc, 3:4]),
                     reads=gk + ["convp"], writes=[("T1", gi)])
                S.op("dve", lambda e, gi=gi, fc=fc: e.scalar_tensor_tensor(out=T1[gi][:, c0:c1], in0=G[gi][:, c0 - 1:c1 - 1], scalar=convp[:, fc, 0:1], in1=T1[gi][:, c0:c1], op0=ALU.mult, op1=ALU.add),
                     reads=gk + ["convp", ("T1", gi)], writes=[("T1", gi)])
                S.op("dve", lambda e, gi=gi, fc=fc: e.scalar_tensor_tensor(out=T1[gi][:, c0:c1], in0=G[gi][:, c0 + 1:c1 + 1], scalar=convp[:, fc, 2:3], in1=T1[gi][:, c0:c1], op0=ALU.mult, op1=ALU.add),
                     reads=gk + ["convp", ("T1", gi)], writes=[("T1", gi)])
                S.op("act", lambda e, gi=gi: e.activation(out=T1[gi][:, c0:c1], in_=T1[gi][:, c0:c1], func=AF.Silu), reads=[("T1", gi)], writes=[("T1", gi)])
                segs = [(tok0, 256), (256, tok1)] if tok0 < 256 else [(tok0, tok1)]
                for (sa, sb) in segs:
                    S.op("dve", lambda e, sa=sa, sb=sb, gi=gi, ci=ci: e.tensor_tensor(out=hid[:, ci, sa:sb], in0=hid[:, ci, sa:sb], in1=T1[gi][:, gcol(sa):gcol(sa) + sb - sa], op=ALU.mult),
                         reads=[("T1", gi)] + [("hid", ci, bi) for bi in range(len(blocks))], writes=[("hid", ci, bi) for bi in range(len(blocks))])
            A.release(m1)
            phase_R(tiles, lambda c, t: hid[:, c, t * 128:(t + 1) * 128],
                    lambda c, t: [("hid", c, bi) for bi in range(len(blocks)) if blocks[bi][0] <= t * 128 < blocks[bi][1]],
                    FCP, wdn, "wdn", src, ps == 0, ps == NPASS - 1, dst, dst_row0, ybuf)
        A.release(m0)

    def layer1_attention(src, dst):
        load_sublayer_consts(1, 0, False)
        phase_P(1, src, list(range(NT)), 0)
        lambda_init = 0.8 - 0.6 * math.exp(-0.3 * 1)
        mA = A.mark()
        oT = A.alloc("oT1", [128, 8, 2048], BF16)
        rope64 = A.alloc("rope64b", [128, 2, NTOK], F32)
        ET = [A.alloc(f"ETb{i}", [128, 512], BF16) for i in range(4)]
        t1 = [A.alloc(f"t1b{i}", [128, 256], F32) for i in range(2)]
        t2 = [A.alloc(f"t2b{i}", [128, 256], F32) for i in range(2)]
        wo = A.alloc("wo1", [128, 8, D], BF16)
        gsub = A.alloc("gsub", [128, 128], F32)
        lamt = A.alloc("lamt", [128, 4, 64], F32)
        lams = A.alloc("lams", [128, 8], F32)
        accs = [A.alloc(f"accs{i}", [128, 4, 129], F32) for i in range(2)]
        rr = [A.alloc(f"rr{i}", [128, 8], F32) for i in range(2)]
        uu = [A.alloc(f"uu{i}", [128, 128], F32) for i in range(2)]
        avv = [A.alloc(f"avv{i}", [128, 128], F32) for i in range(2)]
        junk = A.alloc("junk", [128, 128], BF16)
        junkf = A.alloc("junkf", [128, 128], F32)
        otk = [A.alloc(f"otkb{i}", [128, 128], BF16) for i in range(4)]
        rmat = A.alloc("rmat", [128, 128], BF16)
        cdma(lambda e: e.dma_start(out=rmat[:], in_=rmat_d), writes=["rmat"])
        mH = A.mark()
        V1 = [A.alloc(f"V1_{i}", [128, NT, 130], BF16) for i in range(2)]
        qT = [A.alloc(f"qT1_{i}", [128, 2048], BF16) for i in range(2)]
        kT0 = [A.alloc(f"kT0_{i}", [128, NTOK], BF16) for i in range(2)]
        kT1 = [A.alloc(f"kT1_{i}", [128, NTOK], BF16) for i in range(2)]
        w1 = [A.alloc(f"w1h_{i}", [128, 3, 8, 128], BF16) for i in range(2)]
        rawb = [A.alloc(f"rawb{i}", [128, 256], BF16) for i in range(2)]

        for r in range(2):
            cdma(lambda e, r=r: e.dma_start(out=rope64[:, r, :], in_=rope64_d[r]), writes=[("rope64", r)])
        cdma(lambda e: e.dma_start(out=gsub[:], in_=subg_d.partition_broadcast(128)), writes=["gsub"])
        S.op("dve", lambda e: e.tensor_scalar(out=gsub[:], in0=gsub[:], scalar1=(1.0 - lambda_init), scalar2=None, op0=ALU.mult), reads=["gsub"], writes=["gsub"])
        for i in range(4):
            cdma(lambda e, i=i: e.dma_start(out=lamt[:, i, :], in_=lamv_d[i].partition_broadcast(128)), writes=[("lamt", i)])
        for i in range(2):
            S.op("dve", lambda e, i=i: e.tensor_tensor(out=lamt[:, 2 * i, :], in0=lamt[:, 2 * i, :], in1=lamt[:, 2 * i + 1, :], op=ALU.mult),
                 reads=[("lamt", 2 * i), ("lamt", 2 * i + 1)], writes=[("lamt", 2 * i)])
            S.op("dve", lambda e, i=i: e.reduce_sum(out=lams[:, i:i + 1], in_=lamt[:, 2 * i, :], axis=mybir.AxisListType.X), reads=[("lamt", 2 * i)], writes=["lams"])
        S.op("act", lambda e: e.activation(out=lams[:, 2:4], in_=lams[:, 0:2], func=AF.Exp), reads=["lams"], writes=["lams"])
        S.op("dve", lambda e: e.scalar_tensor_tensor(out=lams[:, 4:5], in0=lams[:, 3:4], scalar=-lambda_init, in1=lams[:, 2:3], op0=ALU.add, op1=ALU.subtract),
             reads=["lams"], writes=["lams"])
        for i in range(2):
            S.op("dve", lambda e, i=i: e.memset(V1[i][:, :, 128:129], 1.0), writes=[("V11", i)])
            S.op("dve", lambda e, i=i: e.memset(kT0[i][64:128, :], 0.0), writes=[("kTz", i)])
            S.op("dve", lambda e, i=i: e.memset(kT1[i][0:64, :], 0.0), writes=[("kTz", i)])
        for hf in range(2):
            S.dma("pool", ("wo", hf), lambda e, hf=hf: e.dma_start(out=wo[:, hf * 4:(hf + 1) * 4, :], in_=wout1_d[:, hf * 4:(hf + 1) * 4, :], max_dma_last_dim=8192), writes=["wo"])

        tpb = P2[3][:].bitcast(BF16)[:, 0:1024]
        PB = 7
        jn = [0]
        uc = [0]

        def proj_units(h, banks=(7,)):
            hb = h % 2
            units = []
            state = {"n": 0, "pend": None}

            def load_w():
                S.dma("pool", ("w1", hb), lambda e: e.dma_start(out=w1[hb][:], in_=w1h_d[h], max_dma_last_dim=8192), writes=[("w1", hb)])

            def qk_parts(chunk, a, b, dsts):
                n = b - a
                idx = state["n"]
                state["n"] += 1
                i = idx % 2
                PB = banks[idx % len(banks)]

                def part1():
                    for k in range(8):
                        S.op("pe", lambda e, k=k: e.matmul(bank(PB)[:, 0:n], lhsT=w1[hb][:, chunk, k, :], rhs=hT[:, k, a:b], start=(k == 0), stop=(k == 7)),
                             reads=[("w1", hb)] + hT_reads(a, b, k), writes=[("pb", PB)])
                    S.op("dve", lambda e: e.tensor_copy(out=rawb[i][:, 0:n], in_=bank(PB)[:, 0:n]), reads=[("pb", PB)], writes=[("rawb", i)])
                    S.op("dve", lambda e: e.tensor_tensor(out=t1[i][:, 0:n], in0=bank(PB)[:, 0:n], in1=rope64[:, 0, a:b], op=ALU.mult),
                         reads=[("pb", PB), ("rope64", 0)], writes=[("t1", i)])

                def part2():
                    S.op("pe", lambda e: e.matmul(bank(PB)[:, 256:256 + n], lhsT=rmat[:], rhs=rawb[i][:, 0:n], start=True, stop=True),
                         reads=["rmat", ("rawb", i)], writes=[("pb", PB)])
                    S.op("dve", lambda e: e.tensor_tensor(out=t2[i][:, 0:n], in0=bank(PB)[:, 256:256 + n], in1=rope64[:, 1, a:b], op=ALU.mult),
                         reads=[("pb", PB), ("rope64", 1)], writes=[("t2", i)])
                    for (p0, p1, dst_ap, wkey) in dsts:
                        S.op("pool", lambda e, p0=p0, p1=p1, dst_ap=dst_ap: e.tensor_tensor(out=dst_ap, in0=t1[i][p0:p1, 0:n], in1=t2[i][p0:p1, 0:n], op=ALU.add),
                             reads=[("t1", i), ("t2", i)], writes=[wkey])
                return part1, part2

            def v_unit(t0):
                idx = state["n"]
                state["n"] += 1
                PB = banks[idx % len(banks)]

                def f():
                    tl = list(range(t0, min(t0 + 2, NT)))
                    for ti, t in enumerate(tl):
                        for k in range(8):
                            S.op("pe", lambda e, k=k, t=t, ti=ti: e.matmul(bank(PB)[:, ti * 128:(ti + 1) * 128], lhsT=hT[:, k, t * 128:(t + 1) * 128], rhs=w1[hb][:, 2, k, :],
                                                                          start=(k == 0), stop=(k == 7)),
                                 reads=[("w1", hb), ("hT", t, k)], writes=[("pb", PB)])
                    nt_ = len(tl)
                    S.op("dve", lambda e: e.tensor_copy(out=V1[hb][:, t0:t0 + nt_, 0:128], in_=bank(PB)[:, 0:nt_ * 128].rearrange("p (t d) -> p t d", d=128)),
                         reads=[("pb", PB)], writes=[("V1", hb, t0 // 2)])
                return f

            parts = []
            for u in range(9):
                a = u * 256
                parts.append(qk_parts(1, a, a + 256, [(0, 64, kT0[hb][0:64, a:a + 256], ("kT", hb, 0, u)), (64, 128, kT1[hb][64:128, a:a + 256], ("kT", hb, 1, u))]))
            for u in range(8):
                a = 256 + u * 256
                parts.append(qk_parts(0, a, a + 256, [(0, 128, qT[hb][:, u * 256:(u + 1) * 256], ("qT", hb, u))]))
            for j in range(len(parts) + 1):
                def slot(j=j):
                    if j >= 1:
                        parts[j - 1][1]()
                    if j < len(parts):
                        parts[j][0]()
                units.append(slot)
            for u in range(9):
                units.append(v_unit(2 * u))
            return load_w, units

        def head_jobs(h):
            hb = h % 2
            jobs = []
            for qb in range(8):
                qa = qb * 256
                aset = jn[0] % 2
                jn[0] += 1

                def accinfo(qt, c):
                    idx = qt * 2 + c
                    bk = 4 + idx // 3
                    sl_ = idx % 3
                    return bank(bk)[:, sl_ * 129:(sl_ + 1) * 129], ("pb", bk), sl_ == 0

                def qk_(j, qa=qa):
                    return [(0, 256, kT0[hb][:, j * 128:(j + 1) * 128], qT[hb][:, qa:qa + 256], None),
                            (256, 256, kT1[hb][:, j * 128:(j + 1) * 128], qT[hb][:, qa:qa + 256], None)]

                def pv_(j):
                    ops = []
                    for qt in range(2):
                        for c in range(2):
                            ap_, key_, fib = accinfo(qt, c)
                            ops.append((ap_, key_, fib, c * 256 + qt * 128, V1[hb][:, j, 0:129]))
                    return ops

                def fin(qa=qa, aset=aset):
                    ac = accs[aset]
                    S.op("act", lambda e: e.activation(out=ac[:, 0:3, :], in_=bank(4)[:, 0:3 * 129].rearrange("p (a d) -> p a d", d=129), func=AF.Copy),
                         reads=[("pb", 4)], writes=[("accs", aset, 0)])
                    S.op("act", lambda e: e.activation(out=ac[:, 3, :], in_=bank(5)[:, 0:129], func=AF.Copy), reads=[("pb", 5)], writes=[("accs", aset, 1)])
                    for qt in range(2):
                        i = qt
                        ak = [("accs", aset, 0), ("accs", aset, 1)]
                        S.op("dve", lambda e, qt=qt, i=i: e.reciprocal(out=rr[i][:, 0:2], in_=ac[:, qt * 2:qt * 2 + 2, 128]), reads=ak, writes=[("rr", i)])
                        S.op("dve", lambda e, i=i: e.tensor_tensor(out=rr[i][:, 2:3], in0=rr[i][:, 1:2], in1=lams[:, 4:5], op=ALU.mult), reads=[("rr", i), "lams"], writes=[("rr", i)])
                        S.op("dve", lambda e, qt=qt, i=i: e.tensor_scalar(out=uu[i][:], in0=ac[:, qt * 2 + 1, 0:128], scalar1=rr[i][:, 2:3], scalar2=None, op0=ALU.mult),
                             reads=ak + [("rr", i)], writes=[("uu", i)])
                        S.op("dve", lambda e, qt=qt, i=i: e.scalar_tensor_tensor(out=avv[i][:], in0=ac[:, qt * 2, 0:128], scalar=rr[i][:, 0:1], in1=uu[i][:], op0=ALU.mult, op1=ALU.add),
                             reads=ak + [("rr", i), ("uu", i)], writes=[("avv", i)])
                        S.op("dve", lambda e, i=i: e.scalar_tensor_tensor(out=junkf[:], in0=avv[i][:], scalar=1.0, in1=avv[i][:], op0=ALU.mult, op1=ALU.mult, accum_out=rr[i][:, 3:4]),
                             reads=[("avv", i)], writes=["junkf", ("rr", i)])

                    def late0():
                        for qt in range(2):
                            i = qt
                            rsqrt_small(rr[i][:, 4:5], rr[i][:, 3:4], 1.0 / 128.0, 1, [("rr", i)], ("rr", i))

                    def late1():
                        for qt in range(2):
                            i = qt
                            oi = aset * 2 + qt
                            S.op("dve", lambda e, i=i, oi=oi: e.scalar_tensor_tensor(out=otk[oi][:], in0=avv[i][:], scalar=rr[i][:, 4:5], in1=gsub[:], op0=ALU.mult, op1=ALU.mult),
                                 reads=[("avv", i), ("rr", i), "gsub"], writes=[("otk", oi)])

                    def late2():
                        for qt in range(2):
                            oi = aset * 2 + qt
                            S.op("pe", lambda e, oi=oi: e.transpose(out=tpb[:, oi * 128:(oi + 1) * 128], in_=otk[oi][:], identity=ident_b[:]),
                                 reads=[("otk", oi), "ident_b"], writes=[("pb", 6)])
                        S.op("dve", lambda e: e.tensor_copy(out=oT[:, h, qa:qa + 256], in_=tpb[:, aset * 256:aset * 256 + 256]),
                             reads=[("pb", 6)], writes=[("oT", h, qa // 128), ("oT", h, qa // 128 + 1)])
                    return [(7, late0), (11, late1), (14, late2)]

                jobs.append(dict(
                    keys=list(range(NT)), W=512, scale=64 ** -0.5,
                    qk=qk_, qk_reads=lambda j, qb=qb: [("kT", hb, 0, j // 2), ("kT", hb, 1, j // 2), ("kTz", hb), ("qT", hb, qb)],
                    pv=pv_, pv_reads=lambda j: [("V1", hb, j // 2), ("V11", hb)],
                    mask=None, fin=fin))
            return jobs

        lw, units = proj_units(0, banks=(0, 1, 2, 3, 7))
        lw()
        for f in units:
            f()
        for h in range(8):
            fl = []
            if h + 1 < 8:
                lw, fl = proj_units(h + 1)
                lw()
            run_attention(head_jobs(h), ET, fl)
        A.release(mH)

        lat = list(range(NCTX_T, NT))
        phase_R(lat, lambda c, t: oT[:, c, (t - NCTX_T) * 128:(t - NCTX_T + 1) * 128], lambda c, t: [("oT", c, t - NCTX_T)], 8, wo, "wo",
                src, True, True, dst, 0, None)
        A.release(mA)

    cur_src = x_all
    dbg_x = None
    if stop_after is not None:
        dbg_x = nc.dram_tensor("dbg_x", [NTOK, D], F32, kind="ExternalOutput").ap()
    try:
      for l in layers:
          if l == 0:
              mod0 = modulation_units(0, True, 6, 7)
              mod1 = modulation_units(1, False, 6, 7) if 1 in layers else None
              if stop_after == "attn0":
                  layer0_attention(cur_src, dbg_x, mod0, mod1)
                  break
              layer0_attention(cur_src, xres[0], mod0, mod1)
              if stop_after == "ffn0":
                  ffn_sublayer(0, xres[0], dbg_x, 0, list(range(NT)))
                  break
              ffn_sublayer(0, xres[0], xres[1], 0, list(range(NT)))
              cur_src = xres[1]
          else:
              if 0 not in layers:
                  su, bl, fi = modulation_units(1, False, 6, 7)
                  su()
                  for f in bl:
                      f()
                  fi()
              if stop_after == "attn1":
                  layer1_attention(cur_src, dbg_x)
                  break
              layer1_attention(cur_src, xres[0])
              ffn_sublayer(1, xres[0], out_d, 256, list(range(NCTX_T, NT)))
    except _Stop:
        pass
    S.barrier()
    nsem = S.emit()
    info = dict(nsem=nsem, peak=A.peak - A.base, cnt=dict(S.cnt))
    return nc, info


def _rope_tables(dim):
    rows = 2048 // 64
    row = np.repeat(np.arange(rows), 64).astype(np.float32)
    col = np.tile(np.arange(64), rows).astype(np.float32)
    q = dim // 4
    freqs = (np.float32(10000.0) ** (-np.arange(q, dtype=np.float32) / np.float32(q))).astype(np.float32)
    ar = row[:, None] * freqs
    ac = col[:, None] * freqs
    cos = np.concatenate([np.cos(ar), np.cos(ar), np.cos(ac), np.cos(ac)], -1).astype(np.float32)
    sin = np.concatenate([np.sin(ar), np.sin(ar), np.sin(ac), np.sin(ac)], -1).astype(np.float32)
    sign = np.concatenate([-np.ones(q), np.ones(q), -np.ones(q), np.ones(q)]).astype(np.float32)
    perm = np.concatenate([np.arange(q, 2 * q), np.arange(0, q), np.arange(3 * q, 4 * q), np.arange(2 * q, 3 * q)])
    cosT = np.ones((dim, NTOK), np.float32)
    sinT = np.zeros((dim, NTOK), np.float32)
    cosT[:, 256:] = cos.T
    sinT[:, 256:] = (sin * sign[None, :]).T
    return cosT, sinT, perm


def _prep_shared(inp):
    f32 = np.float32
    cos64, sin64, perm64 = _rope_tables(64)
    cos32, sin32, perm32 = _rope_tables(32)
    sh = {}
    w_mod = inp["w_mod"]
    sh["wmod"] = np.ascontiguousarray(w_mod.reshape(2, 8, 128, 12, 512).transpose(0, 3, 2, 1, 4))
    sh["bmod_pp"] = np.ascontiguousarray(inp["b_mod"].reshape(2, 6, 8, 128).transpose(0, 3, 1, 2).reshape(2, 128, 48))
    sh["bmod"] = np.ascontiguousarray(inp["b_mod"])
    sh["lnv"] = np.ascontiguousarray(np.stack([inp["ln1_g"], inp["ln1_b"], inp["ln2_g"], inp["ln2_b"]], 1))
    g = inp["ffn_w_gate"].reshape(2, 8, 128, 11, 256).transpose(0, 3, 2, 1, 4)
    u = inp["ffn_w_up"].reshape(2, 8, 128, 11, 256).transpose(0, 3, 2, 1, 4)
    sh["wgu"] = np.ascontiguousarray(np.stack([g, u], 4).reshape(2, 11, 128, 8, 512))
    cw = inp["ffn_conv_w"].reshape(2, 3, NFC, 128).transpose(0, 3, 2, 1)
    cb = inp["ffn_conv_b"].reshape(2, NFC, 128).transpose(0, 2, 1)[..., None]
    sh["convp"] = np.ascontiguousarray(np.concatenate([cw, cb], -1))
    sh["wdown"] = np.ascontiguousarray(inp["ffn_w_down"].reshape(2, NPASS, FCP, 128, D).transpose(0, 1, 3, 2, 4))
    W = inp["ab_w_in"][0]
    z = lambda n: np.zeros((D, n), f32)
    chunks = [W[:, 0:128], W[:, 128:256]]
    qs = W[:, 256:768].reshape(D, 8, 64)
    for c in range(4):
        chunks.append(np.concatenate([qs[:, c], qs[:, 4 + c]], 1))
    for c in range(4):
        chunks.append(np.concatenate([qs[:, c][:, perm64], qs[:, 4 + c][:, perm64]], 1))
    chunks.append(W[:, 768:896])
    kr = W[:, 896:928]
    chunks.append(np.concatenate([z(64), kr, z(32)], 1))
    chunks.append(np.concatenate([z(64), kr[:, perm32], z(32)], 1))
    ks = W[:, 928:1056].reshape(D, 2, 64)
    chunks.append(np.concatenate([ks[:, 0], ks[:, 1]], 1))
    chunks.append(np.concatenate([ks[:, 0][:, perm64], ks[:, 1][:, perm64]], 1))
    chunks.append(W[:, 1056:1184])
    w0 = np.stack(chunks, 0).reshape(16, 8, 128, 128).transpose(2, 0, 1, 3)
    sh["w0"] = np.ascontiguousarray(w0)
    Wq = inp["mla_w_qb"][0].reshape(256, 8, 96)
    raw = Wq
    rot = np.concatenate([np.zeros((256, 8, 64), f32), Wq[:, :, 64:96][:, :, perm32]], 2)
    wqb = np.stack([raw, rot], 0)
    wqb = wqb.reshape(2, 2, 128, 8, 96).transpose(2, 3, 0, 1, 4)
    sh["wqb"] = np.ascontiguousarray(wqb)
    Wkv = inp["mla_w_kvb"][0].reshape(128, 8, 128)
    sh["wkvb"] = np.ascontiguousarray(np.concatenate([Wkv[:, :, 0:64].reshape(128, 512), Wkv[:, :, 64:128].reshape(128, 512)], 1))
    sh["qnorm_pp"] = np.ascontiguousarray(inp["mla_q_norm"][0].reshape(2, 128).T)
    sh["kvnorm_pp"] = np.ascontiguousarray(inp["mla_kv_norm"][0].reshape(128, 1))
    sh["sink"] = np.ascontiguousarray(inp["swa_sink"][0])
    sh["wout0"] = np.ascontiguousarray(inp["ab_w_out"][0].reshape(8, 128, D).transpose(1, 0, 2))
    W1 = inp["diff_w_in"][0]
    perm128 = np.concatenate([perm64, 64 + perm64])
    heads = []
    for h in range(8):
        q = W1[:, h * 128:(h + 1) * 128]
        k = W1[:, 1024 + h * 128:1024 + (h + 1) * 128]
        v = W1[:, 2048 + h * 128:2048 + (h + 1) * 128]
        hw = np.stack([q, k, v], 0)
        heads.append(hw.reshape(3, 8, 128, 128).transpose(2, 0, 1, 3))
    sh["w1h"] = np.ascontiguousarray(np.stack(heads, 0))
    rm = np.zeros((128, 128), f32)
    rm[perm128, np.arange(128)] = 1.0
    sh["rmat"] = rm.astype(ml_dtypes.bfloat16)
    sh["wout1"] = np.ascontiguousarray(inp["diff_w_out"][0].reshape(8, 128, D).transpose(1, 0, 2))
    sh["lamv"] = np.ascontiguousarray(np.stack([inp["diff_lam_q1"][0], inp["diff_lam_k1"][0], inp["diff_lam_q2"][0], inp["diff_lam_k2"][0]], 0))
    sh["subg"] = np.ascontiguousarray(inp["diff_subln_g"][0])
    sh["ident_f"] = np.eye(128, dtype=f32)
    sh["ident_b"] = np.eye(128, dtype=f32).astype(ml_dtypes.bfloat16)
    jj = np.arange(128)[:, None]
    ii = np.arange(128)[None, :]
    mP = (jj >= ii).astype(f32)
    mN = (jj <= ii).astype(f32)
    sh["masks"] = np.stack([np.tile(mP, (1, 4)), np.tile(mN, (1, 4))], 0).astype(ml_dtypes.bfloat16)
    sh["rope64"] = np.ascontiguousarray(np.stack([np.tile(cos64, (2, 1)), np.tile(sin64, (2, 1))], 0))
    r32 = np.zeros((2, 128, NTOK), f32)
    r32[0, 64:96] = cos32
    r32[1, 64:96] = sin32
    sh["rope32"] = r32
    return sh


_CACHE = {}


def _get_program(layers, dbg=()):
    key = (tuple(layers), tuple(dbg))
    if key not in _CACHE:
        _CACHE[key] = build_program(layers, dbg)
    return _CACHE[key]


def kernel(**inputs):
    inp = {k: np.asarray(v, dtype=np.float32) for k, v in inputs.items()}
    sh = _prep_shared(inp)
    B = inp["x"].shape[0]
    in_maps = []
    for b in range(B):
        m = dict(sh)
        m["x_all"] = np.ascontiguousarray(np.concatenate([inp["ctx"][b], inp["x"][b]], 0))
        cc = np.stack([inp["c"][b], inp["c_ctx"]], 0)
        m["cT"] = np.ascontiguousarray(cc.reshape(2, 8, 128).transpose(2, 1, 0))
        in_maps.append(m)
    nc, info = _get_program((0, 1))
    res = run_bass_kernel_spmd(nc, in_maps, core_ids=list(range(B)))
    out = np.stack([np.asarray(r["out"], dtype=np.float32) for r in res.results], 0)
    return out
```

```python
import math
import os
import numpy as np
import ml_dtypes
import concourse.bass as bass
import concourse.mybir as mybir
from concourse.bass_utils import run_bass_kernel_spmd

F32 = mybir.dt.float32
BF16 = mybir.dt.bfloat16
AF = mybir.ActivationFunctionType
ALU = mybir.AluOpType

D = 1024
DFF = 2816
NTOK = 2304
NT = 18
NCTX_T = 2
ALPHA = 4.0 ** 0.25
LN_EPS = 1e-5
RMS_EPS = 1e-6
NFC = 22
NPASS = 2
FCP = NFC // NPASS


class _Rec:
    def __init__(self):
        self.call = None

    def __getattr__(self, name):
        def f(*a, **k):
            self.call = (name, a, k)
            return self
        return f


def _record(fn):
    r = _Rec()
    fn(r)
    assert r.call is not None
    return r.call


class Sched:
    ENGS = ("pe", "act", "dve", "pool", "sp")

    def __init__(self, nc, same_engine_sync=True):
        self.nc = nc
        self.streams = {e: [] for e in self.ENGS}
        self.cnt = {e: 0 for e in self.ENGS}
        self.waited = {}
        self.lastw = {}
        self.readers = {}
        self.semcnt = {}
        self.same = same_engine_sync

    def _deps(self, reads, writes):
        deps = {}

        def add(k, v):
            if deps.get(k, 0) < v:
                deps[k] = v

        for r in reads:
            t = self.lastw.get(r)
            if t is not None:
                add(*t)
        for w in writes:
            t = self.lastw.get(w)
            if t is not None:
                add(*t)
            for k, v in self.readers.get(w, {}).items():
                add(k, v)
        return deps

    def _commit(self, tok, reads, writes):
        k, v = tok
        for r in reads:
            d = self.readers.setdefault(r, {})
            if d.get(k, 0) < v:
                d[k] = v
        for w in writes:
            self.lastw[w] = tok
            self.readers[w] = {}

    def _waits(self, eng, deps):
        waits = []
        for k, v in deps.items():
            if k == eng and (eng == "pe" or not self.same):
                continue
            if self.waited.get((eng, k), 0) >= v:
                continue
            self.waited[(eng, k)] = v
            waits.append((k, v))
        return waits

    def op(self, eng, fn, reads=(), writes=()):
        deps = self._deps(reads, writes)
        waits = self._waits(eng, deps)
        self.cnt[eng] += 1
        tok = (eng, self.cnt[eng])
        self.semcnt[eng] = self.cnt[eng]
        self.streams[eng].append((waits, _record(fn), (eng, 1)))
        self._commit(tok, reads, writes)
        return tok

    def dma(self, q, semkey, fn, reads=(), writes=()):
        deps = self._deps(reads, writes)
        prev = self.semcnt.get(semkey, 0)
        if prev and deps.get(semkey, 0) < prev:
            deps[semkey] = prev
        waits = self._waits(q, deps)
        self.semcnt[semkey] = prev + 16
        tok = (semkey, prev + 16)
        self.streams[q].append((waits, _record(fn), (semkey, 16)))
        self._commit(tok, reads, writes)
        return tok

    def wait_all(self, eng):
        waits = []
        for k, v in self.semcnt.items():
            if k == eng:
                continue
            if self.waited.get((eng, k), 0) >= v:
                continue
            self.waited[(eng, k)] = v
            waits.append((k, v))
        self.streams[eng].append((waits, None, None))

    def barrier(self):
        for e in self.ENGS:
            self.wait_all(e)

    def emit(self):
        nc = self.nc
        sems = {}
        for i, k in enumerate(self.semcnt):
            sems[k] = nc.alloc_semaphore(name=f"sm{i}")
        streams = self.streams

        def run(engname, eng):
            for waits, fn, inc in streams[engname]:
                for k, v in waits:
                    eng.wait_ge(sems[k], v)
                if fn is None:
                    continue
                name, a_, k_ = fn
                ins = getattr(eng, name)(*a_, **k_)
                if inc is not None:
                    ins.then_inc(sems[inc[0]], inc[1])

        with nc.Block() as block:
            @block.tensor
            def _(e):
                run("pe", e)

            @block.scalar
            def _(e):
                run("act", e)

            @block.vector
            def _(e):
                run("dve", e)

            @block.gpsimd
            def _(e):
                run("pool", e)

            @block.sync
            def _(e):
                run("sp", e)
        return len(sems)


class Arena:
    def __init__(self, nc, S):
        self.nc = nc
        self.S = S
        self.base = (nc.sbuf_base + 63) // 64 * 64
        self.top = nc.sbuf_top
        self.cur = self.base
        self.n = 0
        self.peak = 0

    def alloc(self, name, shape, dt):
        per = 1
        for s in shape[1:]:
            per *= s
        per *= 2 if dt == BF16 else 4
        off = self.cur
        self.cur = (off + per + 63) // 64 * 64
        assert self.cur <= self.top, f"SBUF overflow allocating {name}: {self.cur} > {self.top}"
        self.peak = max(self.peak, self.cur)
        self.n += 1
        return self.nc.alloc_sbuf_tensor_at(f"{name}_{self.n}", list(shape), dt, offset=off)

    def mark(self):
        return self.cur

    def release(self, m):
        self.S.barrier()
        self.cur = m


def build_program(layers=(0, 1), dbg=(), stop_after=None):
    nc = bass.Bass("TRN2", target_bir_lowering=False)
    S = Sched(nc)
    A = Arena(nc, S)
    first_layer, last_layer = layers[0], layers[-1]

    def din(name, shape, dt=F32):
        return nc.dram_tensor(name, list(shape), dt, kind="ExternalInput").ap()

    def dscr(name, shape, dt=F32):
        return nc.dram_tensor(name, list(shape), dt, kind="Internal").ap()

    x_all = din("x_all", [NTOK, D])
    cT_d = din("cT", [128, 8, 2])
    wmod_d = din("wmod", [2, 12, 128, 8, 512])
    bmodpp_d = din("bmod_pp", [2, 128, 48])
    bmod_d = din("bmod", [2, 6144])
    lnv_d = din("lnv", [2, 4, D])
    wgu_d = din("wgu", [2, 11, 128, 8, 512])
    convp_d = din("convp", [2, 128, NFC, 4])
    wdown_d = din("wdown", [2, NPASS, 128, FCP, D])
    w0_d = din("w0", [128, 16, 8, 128])
    wqb_d = din("wqb", [128, 8, 2, 2, 96])
    wkvb_d = din("wkvb", [128, 1024])
    qnorm_d = din("qnorm_pp", [128, 2])
    kvnorm_d = din("kvnorm_pp", [128, 1])
    sink_d = din("sink", [8])
    wout0_d = din("wout0", [128, 8, D])
    w1h_d = din("w1h", [8, 128, 5, 8, 128])
    wout1_d = din("wout1", [128, 8, D])
    lamv_d = din("lamv", [4, 64])
    subg_d = din("subg", [128])
    identf_d = din("ident_f", [128, 128])
    identb_d = din("ident_b", [128, 128], BF16)
    masks_d = din("masks", [2, 128, 512], BF16)
    rope64_d = din("rope64", [2, 128, NTOK])
    rope32_d = din("rope32", [2, 128, NTOK])
    out_d = nc.dram_tensor("out", [2048, D], F32, kind="ExternalOutput").ap()
    xres = [dscr("xres_a", [NTOK, D]), dscr("xres_b", [NTOK, D])]
    ybuf = dscr("ybuf", [NTOK, D])
    gts_d = dscr("gts", [2, 2, 2, 128, D])
    dbg_out = {}

    P2 = [nc.alloc_psum_tensor(f"pp{i}", [128, 1024], F32) for i in range(4)]

    def bank(i):
        return P2[i // 2][:, (i % 2) * 512:(i % 2 + 1) * 512]

    ident_f = A.alloc("ident_f", [128, 128], F32)
    ident_b = A.alloc("ident_b", [128, 128], BF16)
    ones_f = A.alloc("ones_f", [128, 128], F32)
    epsb = A.alloc("epsb", [128, 2], F32)
    masks = A.alloc("masks", [128, 2, 512], BF16)
    mpp2 = A.alloc("mpp", [128, 2, 4, 8, 2], F32)
    hT = A.alloc("hT", [128, 8, NTOK], BF16)
    gt_t = A.alloc("gt_t", [128, 2, D], F32)
    lng_t = A.alloc("lng_t", [128, D], F32)
    lnb_t = A.alloc("lnb_t", [128, D], F32)

    class _Stop(Exception):
        pass

    def checkpoint(name, tensors):
        if name in dbg or stop_after == name:
            S.barrier()
            for nm, (t_ap, shape, dt) in tensors.items():
                d = nc.dram_tensor("dbg_" + nm, list(shape), dt, kind="ExternalOutput").ap()
                cdma(lambda e: e.dma_start(out=d, in_=t_ap))
        if stop_after == name:
            raise _Stop()

    cq = [0]

    def cdma(fn, reads=(), writes=(), q="sp"):
        cq[0] += 1
        return S.dma(q, ("c", cq[0] % 4), fn, reads=reads, writes=writes)

    cdma(lambda e: e.dma_start(out=ident_f[:], in_=identf_d), writes=["ident_f"])
    cdma(lambda e: e.dma_start(out=ident_b[:], in_=identb_d), writes=["ident_b"])
    cdma(lambda e: e.dma_start(out=masks[:], in_=masks_d.rearrange("m p n -> p m n")), writes=["masks"])
    S.op("dve", lambda e: e.memset(ones_f[:], 1.0), writes=["ones_f"])
    S.op("dve", lambda e: e.memset(epsb[:, 0:1], LN_EPS), writes=["epsb"])
    S.op("dve", lambda e: e.memset(epsb[:, 1:2], RMS_EPS), writes=["epsb"])

    def ttiles(a, b):
        return range(a // 128, (b + 127) // 128)

    def modulation_units(l, need_c_gates, pbank_pp, pbank_gt):
        st = {}
        nvar = 2 if need_c_gates else 1
        vi_of = {0: 0, 1: 1, 3: 2, 4: 3}

        def setup():
            st["m0"] = A.mark()
            cT = st["cT"] = A.alloc("cT", [128, 8, 2], F32)
            scf = st["scf"] = A.alloc("scf", [128, 8, 2], F32)
            scb = st["scb"] = A.alloc("scb", [128, 8, 2], BF16)
            cbc = st["cbc"] = A.alloc("cbc", [128, 2, 8, 128], BF16)
            bpp = st["bpp"] = A.alloc("bpp", [128, 48], F32)
            bmb = st["bmb"] = A.alloc("bmb", [128, 2, D], F32)
            st["gtb"] = A.alloc("gtb", [128, 2, 2, D], F32)
            st["wsl"] = [A.alloc(f"wm{i}", [128, 8, 512], BF16) for i in range(3)]
            cdma(lambda e: e.dma_start(out=cT[:], in_=cT_d), writes=[("cT", l)])
            cdma(lambda e: e.dma_start(out=bpp[:], in_=bmodpp_d[l]), writes=[("bpp", l)])
            for gi, vec in enumerate((2, 5)):
                cdma(lambda e, gi=gi, vec=vec: e.dma_start(out=bmb[:, gi, :], in_=bmod_d[l, vec * 1024:(vec + 1) * 1024].partition_broadcast(128)),
                     writes=[("bmb", l, gi)])
            S.op("act", lambda e: e.activation(out=scf[:], in_=cT[:], func=AF.Silu), reads=[("cT", l)], writes=[("scf", l)])
            S.op("dve", lambda e: e.tensor_copy(out=scb[:], in_=scf[:]), reads=[("scf", l)], writes=[("scb", l)])
            for v in range(nvar):
                for k in range(8):
                    S.op("dve", lambda e, v=v, k=k: e.tensor_scalar(out=cbc[:, v, k, :], in0=ones_f[:], scalar1=scf[:, k, v:v + 1], scalar2=None, op0=ALU.mult),
                         reads=[("scf", l), "ones_f"], writes=[("cbc", l, v)])

        def block(blk, n):
            def f():
                wsl, scb, cbc, bpp, bmb, gtb = st["wsl"], st["scb"], st["cbc"], st["bpp"], st["bmb"], st["gtb"]
                sl = n % 3
                vec, half = blk // 2, blk % 2
                S.dma("pool", ("wm", sl), lambda e: e.dma_start(out=wsl[sl][:], in_=wmod_d[l, blk], max_dma_last_dim=8192), writes=[("wm", l, sl)])
                if vec in vi_of:
                    vi = vi_of[vec]
                    pb = ("pb", pbank_pp)
                    for j in range(4):
                        for k in range(8):
                            S.op("pe", lambda e, j=j, k=k: e.matmul(bank(pbank_pp)[:, j * 2:j * 2 + 2], lhsT=wsl[sl][:, k, j * 128:(j + 1) * 128],
                                                                    rhs=scb[:, k, :], start=(k == 0), stop=(k == 7)),
                                 reads=[("wm", l, sl), ("scb", l)], writes=[pb])
                    c0 = half * 4
                    psv = bank(pbank_pp)[:, 0:8].rearrange("p (j v) -> p j v", v=2)
                    for v in range(2):
                        if vec in (1, 4):
                            S.op("dve", lambda e, v=v: e.scalar_tensor_tensor(
                                out=mpp2[:, l, vi, c0:c0 + 4, v], in0=psv[:, :, v], scalar=1.0, in1=bpp[:, vec * 8 + c0:vec * 8 + c0 + 4], op0=ALU.add, op1=ALU.add),
                                reads=[pb, ("bpp", l)], writes=[("mpp", l, vi, half, v)])
                        else:
                            S.op("dve", lambda e, v=v: e.tensor_tensor(
                                out=mpp2[:, l, vi, c0:c0 + 4, v], in0=psv[:, :, v], in1=bpp[:, vec * 8 + c0:vec * 8 + c0 + 4], op=ALU.add),
                                reads=[pb, ("bpp", l)], writes=[("mpp", l, vi, half, v)])
                else:
                    gi = 0 if vec == 2 else 1
                    pb = ("pb", pbank_gt)
                    for v in range(nvar):
                        for k in range(8):
                            S.op("pe", lambda e, k=k, v=v: e.matmul(bank(pbank_gt), lhsT=cbc[:, v, k, :], rhs=wsl[sl][:, k, :], start=(k == 0), stop=(k == 7)),
                                 reads=[("wm", l, sl), ("cbc", l, v)], writes=[pb])
                        S.op("dve", lambda e, v=v: e.tensor_tensor(
                            out=gtb[:, gi, v, half * 512:(half + 1) * 512], in0=bank(pbank_gt), in1=bmb[:, gi, half * 512:(half + 1) * 512], op=ALU.add),
                            reads=[pb, ("bmb", l, gi)], writes=[("gtb", l, gi, v, half)])
            return f

        order = (0, 1, 2, 3, 6, 7, 8, 9, 4, 5, 10, 11)
        blocks = [block(blk, n) for n, blk in enumerate(order)]

        def finish():
            gtb = st["gtb"]
            for gi in range(2):
                for v in range(nvar):
                    cdma(lambda e, gi=gi, v=v: e.dma_start(out=gts_d[l, gi, v], in_=gtb[:, gi, v, :]),
                         reads=[("gtb", l, gi, v, 0), ("gtb", l, gi, v, 1)], writes=[("gts", l, gi, v)])
            checkpoint(f"mod{l}", {"mpp": (mpp2[:, l], [128, 4, 8, 2], F32), "gtb": (gtb[:], [128, 2, 2, D], F32)})
            A.release(st["m0"])

        return setup, blocks, finish

    def load_sublayer_consts(l, sub, need_c):
        for v in range(2 if need_c else 1):
            cdma(lambda e, v=v: e.dma_start(out=gt_t[:, v, :], in_=gts_d[l, sub, v]), reads=[("gts", l, sub, v)], writes=[("gt_t", v)])
        cdma(lambda e: e.dma_start(out=lng_t[:], in_=lnv_d[l, 2 * sub].partition_broadcast(128)), writes=["lng_t"])
        cdma(lambda e: e.dma_start(out=lnb_t[:], in_=lnv_d[l, 2 * sub + 1].partition_broadcast(128)), writes=["lnb_t"])

    def phase_P(l, src, tiles, sub, fillers=()):
        m0 = A.mark()
        fillers = list(fillers)
        xt = [A.alloc(f"xtP{i}", [128, D], F32) for i in range(3)]
        vi_sh, vi_sc = 2 * sub, 2 * sub + 1
        vars_ = sorted(set(1 if t < NCTX_T else 0 for t in tiles))
        shb = {}
        for v in vars_:
            shb[v] = A.alloc(f"shb{v}", [128, 4, 128], F32)
            for kk in range(4):
                k = 2 * kk + 1
                S.op("dve", lambda e, v=v, kk=kk, k=k: e.tensor_scalar(out=shb[v][:, kk, :], in0=ones_f[:], scalar1=mpp2[:, l, vi_sh, k, v:v + 1], scalar2=None, op0=ALU.mult),
                     reads=["ones_f", ("mpp", l, vi_sh, k // 4, v)], writes=[("shb", v, kk)])

        def xload(n):
            if n < len(tiles):
                t_ = tiles[n]
                sl_ = n % 3
                S.dma("sp", ("xt", sl_), lambda e: e.dma_start(out=xt[sl_][:], in_=src[t_ * 128:(t_ + 1) * 128, :]),
                      reads=[(src.tensor.name, t_)], writes=[("xtP", sl_)])
        xload(0)
        xload(1)
        for n, t in enumerate(tiles):
            sl = n % 3
            v = 1 if t < NCTX_T else 0
            xload(n + 2)
            pp = P2[n % 2]
            for k in range(8):
                S.op("pe", lambda e, k=k: e.transpose(out=pp[:, k * 128:(k + 1) * 128], in_=xt[sl][:, k * 128:(k + 1) * 128], identity=ident_f[:]),
                     reads=[("xtP", sl), "ident_f"], writes=[("pb", 2 * (n % 2) + k // 4)])
            for k in range(8):
                rk = [("pb", 2 * (n % 2) + k // 4), ("mpp", l, vi_sc, k // 4, v), ("mpp", l, vi_sh, k // 4, v)]
                if True:
                    S.op("act", lambda e, k=k: e.activation(out=hT[:, k, t * 128:(t + 1) * 128], in_=pp[:, k * 128:(k + 1) * 128], func=AF.Identity,
                                                            scale=mpp2[:, l, vi_sc, k, v:v + 1], bias=mpp2[:, l, vi_sh, k, v:v + 1]),
                         reads=rk, writes=[("hT", t, k)])
                else:
                    S.op("dve", lambda e, k=k: e.scalar_tensor_tensor(out=hT[:, k, t * 128:(t + 1) * 128], in0=pp[:, k * 128:(k + 1) * 128],
                                                                      scalar=mpp2[:, l, vi_sc, k, v:v + 1], in1=shb[v][:, k // 2, :], op0=ALU.mult, op1=ALU.add),
                         reads=rk + [("shb", v, k // 2)], writes=[("hT", t, k)])
            if fillers:
                fillers.pop(0)()
        for f in fillers:
            f()
        A.release(m0)

    def hT_reads(a, b, k):
        return [("hT", t, k) for t in ttiles(a, b)]

    def phase_R(tiles, lhs_fn, lhs_reads_fn, nchunk, w_t, w_key, xsrc, first, last, dst, dst_row0, ytmp, fillers=()):
        m0 = A.mark()
        xt = [A.alloc(f"xtR{i}", [128, D], F32) for i in range(3)]
        NTMP = 4
        tmp = [A.alloc(f"tmpR{i}", [128, D], F32) for i in range(NTMP)]
        st = A.alloc("stR", [128, NTMP, 2, 6], F32)
        mv = A.alloc("mvR", [128, NTMP, 4], F32)
        rsrc = xsrc if first else ytmp

        def xload(n):
            if n < len(tiles):
                t_ = tiles[n]
                sl_ = n % 3
                S.dma("sp", ("xt", sl_), lambda e: e.dma_start(out=xt[sl_][:], in_=rsrc[t_ * 128:(t_ + 1) * 128, :]),
                      reads=[(rsrc.tensor.name, t_)], writes=[("xtR", sl_)])
        xload(0)
        xload(1)

        def stageA(n):
            t = tiles[n]
            sl = n % 3
            s2 = n % NTMP
            v = 1 if t < NCTX_T else 0
            xload(n + 2)
            p2i = n % 3
            pp = P2[p2i]
            for hf in range(2):
                for c in range(nchunk):
                    S.op("pe", lambda e, c=c, hf=hf: e.matmul(pp[:, hf * 512:(hf + 1) * 512], lhsT=lhs_fn(c, t), rhs=w_t[:, c, hf * 512:(hf + 1) * 512],
                                                              start=(c == 0), stop=(c == nchunk - 1)),
                         reads=lhs_reads_fn(c, t) + [w_key], writes=[("pb", 2 * p2i + hf)])
            pbk = [("pb", 2 * p2i), ("pb", 2 * p2i + 1)]
            S.op("dve", lambda e: e.tensor_tensor(out=tmp[s2][:], in0=pp[:], in1=gt_t[:, v, :], op=ALU.mult),
                 reads=pbk + [("gt_t", v)], writes=[("tmpR", s2)])
            S.op("dve", lambda e: e.scalar_tensor_tensor(out=tmp[s2][:], in0=xt[sl][:], scalar=(ALPHA if first else 1.0), in1=tmp[s2][:],
                                                         op0=ALU.mult, op1=ALU.add),
                 reads=[("xtR", sl), ("tmpR", s2)], writes=[("tmpR", s2)])
            if not last:
                S.dma("sp", ("yst", s2), lambda e: e.dma_start(out=ytmp[t * 128:(t + 1) * 128, :], in_=tmp[s2][:]),
                      reads=[("tmpR", s2)], writes=[(ytmp.tensor.name, t)])
                return
            for c in range(2):
                S.op("dve", lambda e, c=c: e.bn_stats(out=st[:, s2, c, :], in_=tmp[s2][:, c * 512:(c + 1) * 512]),
                     reads=[("tmpR", s2)], writes=[("stR", s2, c)])
            S.op("dve", lambda e: e.bn_aggr(out=mv[:, s2, 0:2], in_=st[:, s2, :, :].rearrange("p a b -> p (a b)")),
                 reads=[("stR", s2, 0), ("stR", s2, 1)], writes=[("mvR", s2)])
            S.op("act", lambda e: e.activation(out=mv[:, s2, 2:3], in_=mv[:, s2, 1:2], func=AF.Ln, bias=epsb[:, 0:1]),
                 reads=[("mvR", s2), "epsb"], writes=[("mvR", s2)])
            S.op("act", lambda e: e.activation(out=mv[:, s2, 2:3], in_=mv[:, s2, 2:3], func=AF.Exp, scale=-0.5),
                 reads=[("mvR", s2)], writes=[("mvR", s2)])

        def stageB(n):
            t = tiles[n]
            s2 = n % NTMP
            S.op("dve", lambda e: e.scalar_tensor_tensor(out=mv[:, s2, 3:4], in0=mv[:, s2, 0:1], scalar=-1.0, in1=mv[:, s2, 2:3], op0=ALU.mult, op1=ALU.mult),
                 reads=[("mvR", s2)], writes=[("mvR", s2)])
            S.op("act", lambda e: e.activation(out=tmp[s2][:], in_=tmp[s2][:], func=AF.Identity, scale=mv[:, s2, 2:3], bias=mv[:, s2, 3:4]),
                 reads=[("mvR", s2), ("tmpR", s2)], writes=[("tmpR", s2)])
            eng2 = "dve" if pool_free else "pool"
            S.op(eng2, lambda e: e.tensor_tensor(out=tmp[s2][:], in0=tmp[s2][:], in1=lng_t[:], op=ALU.mult),
                 reads=[("tmpR", s2), "lng_t"], writes=[("tmpR", s2)])
            S.op(eng2, lambda e: e.tensor_tensor(out=tmp[s2][:], in0=tmp[s2][:], in1=lnb_t[:], op=ALU.add),
                 reads=[("tmpR", s2), "lnb_t"], writes=[("tmpR", s2)])
            r0 = t * 128 - dst_row0
            if pool_free:
                S.dma("sp", ("yst", s2), lambda e: e.dma_start(out=dst[r0:r0 + 128, :], in_=tmp[s2][:]),
                      reads=[("tmpR", s2)], writes=[(dst.tensor.name, t)])
            else:
                S.dma("pool", ("ystp", s2), lambda e: e.dma_start(out=dst[r0:r0 + 128, :], in_=tmp[s2][:]),
                      reads=[("tmpR", s2)], writes=[(dst.tensor.name, t)])

        fillers = list(fillers)
        pool_free = bool(fillers)
        for n in range(len(tiles)):
            stageA(n)
            if last and n >= 1:
                stageB(n - 1)
            for _ in range(2):
                if fillers:
                    fillers.pop(0)()
        if last:
            stageB(len(tiles) - 1)
        for f in fillers:
            f()
        A.release(m0)

    def run_attention(jobs, ET, fillers=()):
        steps = [(job, i) for job in jobs for i in range(len(job["keys"]))]
        fillers = list(fillers)
        fill_every = max(1, len(steps) // max(1, len(fillers))) if fillers else 0
        NSB = 4
        LA = NSB - 1

        def qk(si):
            job, ki = steps[si]
            j = job["keys"][ki]
            b = si % NSB
            for (c0, ncol, lhsT, rhs, view) in job["qk"](j):
                out = bank(b)[:, c0:c0 + ncol]
                if view is not None:
                    out = view(out)
                S.op("pe", lambda e, out=out, lhsT=lhsT, rhs=rhs: e.matmul(out, lhsT=lhsT, rhs=rhs, start=True, stop=True, skip_group_check=True),
                     reads=job["qk_reads"](j), writes=[("pb", b)])

        def ex(si):
            job, ki = steps[si]
            j = job["keys"][ki]
            b = si % NSB
            eb = si % len(ET)
            W = job["W"]
            o = ET[eb][:, 0:W]
            S.op("act", lambda e, o=o, b=b, W=W, job=job: e.activation(out=o, in_=bank(b)[:, 0:W], func=AF.Exp, scale=job["scale"]),
                 reads=[("pb", b)], writes=[("ET", eb)])
            mk = job["mask"](j) if job.get("mask") else None
            if mk is not None:
                S.op("dve", lambda e, o=o, mk=mk: e.tensor_tensor(out=o, in0=o, in1=mk, op=ALU.mult), reads=[("ET", eb), "masks"], writes=[("ET", eb)])

        def pv(si):
            job, ki = steps[si]
            j = job["keys"][ki]
            eb = si % len(ET)
            nkeys = len(job["keys"])
            for (ap_, key_, fib, ec0, rhs) in job["pv"](j):
                S.op("pe", lambda e, ap_=ap_, ec0=ec0, rhs=rhs, fib=fib: e.matmul(
                    ap_, lhsT=ET[eb][:, ec0:ec0 + 128], rhs=rhs, start=(ki == 0 and fib), stop=(ki == nkeys - 1), skip_group_check=True),
                    reads=[("ET", eb)] + job["pv_reads"](j), writes=[key_])
            if ki == nkeys - 1:
                return job["fin"]()
            return None

        pending = []
        for si in range(min(LA, len(steps))):
            qk(si)
        for si in range(len(steps)):
            if si + LA < len(steps):
                qk(si + LA)
            ex(si)
            pending = [(d - 1, f) for d, f in pending]
            due = [f for d, f in pending if d <= 0]
            pending = [(d, f) for d, f in pending if d > 0]
            for f in due:
                f()
            late = pv(si)
            if late is not None:
                if callable(late):
                    late = [(2, late)]
                pending.extend(late)
            if fillers and (si + 1) % fill_every == 0:
                fillers.pop(0)()
        for _, f in sorted(pending, key=lambda x: x[0]):
            f()
        for f in fillers:
            f()

    def rsqrt_small(dst, src, scale, eps_col, rkeys, wkey):
        S.op("act", lambda e: e.activation(out=dst, in_=src, func=AF.Ln, scale=scale, bias=epsb[:src.shape[0], eps_col:eps_col + 1]),
             reads=rkeys + ["epsb"], writes=[wkey])
        S.op("act", lambda e: e.activation(out=dst, in_=dst, func=AF.Exp, scale=-0.5), reads=[wkey], writes=[wkey])

    BLK_ALL = [(0, 512), (512, 1024), (1024, 1536), (1536, 2048), (2048, 2304)]
    BLK_LAT = [(256, 768), (768, 1280), (1280, 1792), (1792, 2304)]

    def layer0_attention(src, dst, mod0, mod1):
        setup0, blocks0, finish0 = mod0
        setup0()
        for f in blocks0[:4]:
            f()
        phase_P(0, src, list(range(NT)), 0, fillers=blocks0[4:])
        finish0()
        load_sublayer_consts(0, 0, True)
        checkpoint("P0", {"hT": (hT[:], [128, 8, NTOK], BF16)})
        mA = A.mark()
        qan = A.alloc("qan", [128, 2, NTOK], BF16)
        kvan = A.alloc("kvan", [128, NTOK], BF16)
        krT = A.alloc("krT", [96, NTOK], BF16)
        rope32 = A.alloc("rope32", [128, 2, NTOK], F32)
        wqb = A.alloc("wqb", [128, 8, 2, 2, 96], BF16)
        wkvb = A.alloc("wkvb", [128, 1024], BF16)
        qn_pp = A.alloc("qn_pp", [128, 2], F32)
        kvn_pp = A.alloc("kvn_pp", [128, 1], F32)
        esink = A.alloc("esink", [128, 8], F32)
        ET = [A.alloc(f"ET{i}", [128, 512], BF16) for i in range(4)]
        mB = A.mark()
        qsT = A.alloc("qsT", [128, 4, NTOK], BF16)
        ksT2 = [A.alloc(f"ksT{i}", [128, NTOK], BF16) for i in range(2)]
        Vs = A.alloc("Vs", [128, NT, 2, 65], BF16)
        mC = A.mark()
        w0 = A.alloc("w0", [128, 16, 8, 128], BF16)
        rope64 = A.alloc("rope64", [128, 2, NTOK], F32)
        t1 = [A.alloc(f"t1_{i}", [128, 512], F32) for i in range(2)]
        t2 = [A.alloc(f"t2_{i}", [128, 512], F32) for i in range(2)]
        qaf = A.alloc("qaf", [128, 2, 512], F32)
        qsq = A.alloc("qsq", [128, 2, 512], F32)
        rbc = A.alloc("rbc", [128, 512], F32)

        for r in range(2):
            cdma(lambda e, r=r: e.dma_start(out=rope32[:, r, :], in_=rope32_d[r]), writes=[("rope32", r)])
            cdma(lambda e, r=r: e.dma_start(out=rope64[:, r, :], in_=rope64_d[r]), writes=[("rope64", r)])
        cdma(lambda e: e.dma_start(out=qn_pp[:], in_=qnorm_d), writes=["qn_pp"])
        cdma(lambda e: e.dma_start(out=kvn_pp[:], in_=kvnorm_d), writes=["kvn_pp"])
        cdma(lambda e: e.dma_start(out=esink[:], in_=sink_d.partition_broadcast(128)), writes=["esink"])
        S.op("act", lambda e: e.activation(out=esink[:], in_=esink[:], func=AF.Exp), reads=["esink"], writes=["esink"])
        for g4 in range(4):
            S.dma("pool", ("w0", g4), lambda e, g4=g4: e.dma_start(out=w0[:, g4 * 4:(g4 + 1) * 4], in_=w0_d[:, g4 * 4:(g4 + 1) * 4], max_dma_last_dim=8192),
                  writes=[("w0", g4)])
        S.dma("pool", ("wq", 0), lambda e: e.dma_start(out=wqb[:], in_=wqb_d, max_dma_last_dim=8192), writes=["wqb"])
        S.dma("pool", ("wq", 1), lambda e: e.dma_start(out=wkvb[:], in_=wkvb_d, max_dma_last_dim=8192), writes=["wkvb"])
        S.op("dve", lambda e: e.memset(Vs[:, :, :, 64:65], 1.0), writes=["Vs1"])

        pbc = [0]

        def nextbank():
            pbc[0] = (pbc[0] + 1) % 8
            return pbc[0]

        def fm_proj(bk, chunk, a, b, m=128):
            for k in range(8):
                S.op("pe", lambda e, k=k: e.matmul(bank(bk)[0:m, 0:b - a], lhsT=w0[:, chunk, k, 0:m], rhs=hT[:, k, a:b], start=(k == 0), stop=(k == 7)),
                     reads=[("w0", chunk // 4)] + hT_reads(a, b, k), writes=[("pb", bk)])

        def rope_evac(bk_raw, bk_rot, tab, tabkey, p0, p1, a, b, dst_ap, wkey, i, dsts=None):
            n = b - a
            S.op("dve", lambda e: e.tensor_tensor(out=t1[i][p0:p1, 0:n], in0=bank(bk_raw)[p0:p1, 0:n], in1=tab[p0:p1, 0, a:b], op=ALU.mult),
                 reads=[("pb", bk_raw), (tabkey, 0)], writes=[("t1", i)])
            S.op("dve", lambda e: e.tensor_tensor(out=t2[i][p0:p1, 0:n], in0=bank(bk_rot)[p0:p1, 0:n], in1=tab[p0:p1, 1, a:b], op=ALU.mult),
                 reads=[("pb", bk_rot), (tabkey, 1)], writes=[("t2", i)])
            if dsts is None:
                dsts = [(p0, p1, dst_ap)]
            for (q0, q1, d_ap) in dsts:
                S.op("pool", lambda e, q0=q0, q1=q1, d_ap=d_ap: e.tensor_tensor(out=d_ap, in0=t1[i][q0:q1, 0:n], in1=t2[i][q0:q1, 0:n], op=ALU.add),
                     reads=[("t1", i), ("t2", i)], writes=[wkey])

        S.op("dve", lambda e: e.memset(ksT2[0][64:128, :], 0.0), writes=["ksTz"])
        S.op("dve", lambda e: e.memset(ksT2[1][0:64, :], 0.0), writes=["ksTz"])
        for bi, (a, b) in enumerate(BLK_ALL):
            n = b - a
            tl = list(ttiles(a, b))
            for c in range(4):
                br, bt = nextbank(), nextbank()
                fm_proj(br, 2 + c, a, b)
                fm_proj(bt, 6 + c, a, b)
                rope_evac(br, bt, rope64, "rope64", 0, 128, a, b, qsT[:, c, a:b], ("qsT", c, bi), (bi * 8 + c) % 2)
            br, bt = nextbank(), nextbank()
            fm_proj(br, 13, a, b)
            fm_proj(bt, 14, a, b)
            rope_evac(br, bt, rope64, "rope64", 0, 128, a, b, None, ("ksT", bi), 0,
                      dsts=[(0, 64, ksT2[0][0:64, a:b]), (64, 128, ksT2[1][64:128, a:b])])
            br, bt = nextbank(), nextbank()
            fm_proj(br, 11, a, b, m=96)
            fm_proj(bt, 12, a, b, m=96)
            rope_evac(br, bt, rope32, "rope32", 64, 96, a, b, krT[64:96, a:b], ("krT", bi), 1)
            bq = [nextbank(), nextbank()]
            for c in range(2):
                fm_proj(bq[c], c, a, b)
                S.op("act", lambda e, c=c, bq=bq: e.activation(out=qaf[:, c, 0:n], in_=bank(bq[c])[:, 0:n], func=AF.Copy), reads=[("pb", bq[c])], writes=[("qaf", c)])
                S.op("act", lambda e, c=c, bq=bq: e.activation(out=qsq[:, c, 0:n], in_=bank(bq[c])[:, 0:n], func=AF.Square), reads=[("pb", bq[c])], writes=[("qsq", c)])
            bs = nextbank()
            for c in range(2):
                S.op("pe", lambda e, c=c, bs=bs: e.matmul(bank(bs)[:, 0:n], lhsT=ones_f[:], rhs=qsq[:, c, 0:n], start=(c == 0), stop=(c == 1)),
                     reads=["ones_f", ("qsq", c)], writes=[("pb", bs)])
            rsqrt_small(rbc[:, 0:n], bank(bs)[:, 0:n], 1.0 / 256.0, 1, [("pb", bs)], "rbc")
            for c in range(2):
                S.op("dve", lambda e, c=c: e.scalar_tensor_tensor(out=qan[:, c, a:b], in0=qaf[:, c, 0:n], scalar=qn_pp[:, c:c + 1], in1=rbc[:, 0:n], op0=ALU.mult, op1=ALU.mult),
                     reads=[("qaf", c), "qn_pp", "rbc"], writes=[("qan", c, bi)])
            bq0 = nextbank()
            fm_proj(bq0, 10, a, b)
            S.op("act", lambda e, bq0=bq0: e.activation(out=qaf[:, 0, 0:n], in_=bank(bq0)[:, 0:n], func=AF.Copy), reads=[("pb", bq0)], writes=[("qaf", 0)])
            S.op("act", lambda e, bq0=bq0: e.activation(out=qsq[:, 0, 0:n], in_=bank(bq0)[:, 0:n], func=AF.Square), reads=[("pb", bq0)], writes=[("qsq", 0)])
            bs = nextbank()
            S.op("pe", lambda e, bs=bs: e.matmul(bank(bs)[:, 0:n], lhsT=ones_f[:], rhs=qsq[:, 0, 0:n], start=True, stop=True),
                 reads=["ones_f", ("qsq", 0)], writes=[("pb", bs)])
            rsqrt_small(rbc[:, 0:n], bank(bs)[:, 0:n], 1.0 / 128.0, 1, [("pb", bs)], "rbc")
            S.op("dve", lambda e: e.scalar_tensor_tensor(out=kvan[:, a:b], in0=qaf[:, 0, 0:n], scalar=kvn_pp[:, 0:1], in1=rbc[:, 0:n], op0=ALU.mult, op1=ALU.mult),
                 reads=[("qaf", 0), "kvn_pp", "rbc"], writes=[("kvan", bi)])
            bv = nextbank()
            for ti, t in enumerate(tl):
                for k in range(8):
                    S.op("pe", lambda e, k=k, t=t, ti=ti, bv=bv: e.matmul(bank(bv)[:, ti * 128:(ti + 1) * 128], lhsT=hT[:, k, t * 128:(t + 1) * 128], rhs=w0[:, 15, k, :],
                                                                         start=(k == 0), stop=(k == 7)),
                         reads=[("w0", 3), ("hT", t, k)], writes=[("pb", bv)])
            nt_ = len(tl)
            S.op("act", lambda e, bv=bv, t0=tl[0], nt_=nt_: e.activation(
                out=Vs[:, t0:t0 + nt_, :, 0:64], in_=bank(bv)[:, 0:nt_ * 128].rearrange("p (t g d) -> p t g d", g=2, d=64), func=AF.Copy),
                reads=[("pb", bv)], writes=[("Vs", bi)])
        checkpoint("proj0", {"qsT": (qsT[:], [128, 4, NTOK], BF16), "krT": (krT[64:96, :], [32, NTOK], BF16),
                             "qan": (qan[:], [128, 2, NTOK], BF16), "kvan": (kvan[:], [128, NTOK], BF16), "Vs": (Vs[:], [128, NT, 2, 65], BF16)})
        A.release(mC)

        otk = [A.alloc(f"otk{i}", [128, 256], BF16) for i in range(2)]
        den = [A.alloc(f"den{i}", [128, 8], F32) for i in range(2)]
        oT = hT
        _p3 = P2[3][:].bitcast(BF16)
        class _TPB:
            def __getitem__(self, idx):
                _, sl_ = idx
                a0 = sl_.start
                bk = a0 // 256
                off = bk * 1024 + (a0 - bk * 256)
                return _p3[:, off:off + (sl_.stop - sl_.start)]
        tpb = _TPB()
        jobs = []
        jn = [0]
        for g in range(2):
            for n_ in range(NT):
                if n_ < NCTX_T:
                    keys = [0, 1]
                else:
                    keys = [0, 1] + [j for j in (n_ - 1, n_, n_ + 1) if NCTX_T <= j < NT]
                aset = jn[0] % 2
                jn[0] += 1
                accb = 4 + aset

                def acc(qt, c, accb=accb):
                    return bank(accb)[:, qt * 65:(qt + 1) * 65], ("pb", accb), qt == 0

                def mask(j, n_=n_):
                    if n_ < NCTX_T:
                        return None
                    if j == n_ - 1 and j >= NCTX_T:
                        return masks[:, 0, :]
                    if j == n_ + 1:
                        return masks[:, 1, :]
                    return None

                def fin(g=g, n_=n_, accb=accb, aset=aset):
                    av = bank(accb)[:, 0:260].rearrange("p (h d) -> p h d", d=65)
                    akeys = [("pb", accb)]
                    S.op("dve", lambda e: e.tensor_tensor(out=den[aset][:, 0:4], in0=av[:, :, 64], in1=esink[:, g * 4:g * 4 + 4], op=ALU.add),
                         reads=akeys + ["esink"], writes=[("den", aset)])
                    S.op("dve", lambda e: e.reciprocal(out=den[aset][:, 4:8], in_=den[aset][:, 0:4]), reads=[("den", aset)], writes=[("den", aset)])
                    for c in range(4):
                        S.op("dve", lambda e, c=c: e.tensor_scalar(out=otk[aset][:, c * 64:(c + 1) * 64], in0=av[:, c, 0:64], scalar1=den[aset][:, 4 + c:5 + c], scalar2=None, op0=ALU.mult),
                             reads=[("pb", accb), ("den", aset)], writes=[("otk", aset, c // 2)])
                    def late():
                        for j2 in range(2):
                            slot = (aset * 2 + j2)
                            S.op("pe", lambda e, j2=j2, slot=slot: e.transpose(out=tpb[:, slot * 128:(slot + 1) * 128], in_=otk[aset][:, j2 * 128:(j2 + 1) * 128], identity=ident_b[:]),
                                 reads=[("otk", aset, j2), "ident_b"], writes=[("pb", 6 + aset)])
                        S.op("dve", lambda e: e.tensor_copy(out=oT[:, 4 + 2 * g:6 + 2 * g, n_ * 128:(n_ + 1) * 128],
                                                            in_=tpb[:, aset * 256:(aset + 1) * 256].rearrange("p (j n) -> p j n", j=2)),
                             reads=[("pb", 6 + aset)], writes=[("hT", n_, 4 + 2 * g), ("hT", n_, 5 + 2 * g)])
                    return late

                def qk_(j, g=g, n_=n_):
                    return [(0, 512, ksT2[g][:, j * 128:(j + 1) * 128], qsT[:, :, n_ * 128:(n_ + 1) * 128],
                             lambda ap: ap.rearrange("p (h n) -> p h n", h=4))]

                def pv_(j, g=g, accb=accb):
                    return [(bank(accb)[:, c * 65:(c + 1) * 65], ("pb", accb), c == 0, c * 128, Vs[:, j, g, :]) for c in range(4)]

                jobs.append(dict(
                    keys=keys, W=512, scale=64 ** -0.5,
                    qk=qk_, qk_reads=lambda j, n_=n_: [("ksT", j // 4), "ksTz"] + [("qsT", c, n_ // 4) for c in range(4)],
                    pv=pv_, pv_reads=lambda j: [("Vs", j // 4), "Vs1"],
                    mask=mask, fin=fin))
        run_attention(jobs, ET)
        checkpoint("swa0", {"oT": (hT[:], [128, 8, NTOK], BF16)})
        A.release(mB)

        Vm = A.alloc("Vm", [128, NT, 8, 65], BF16)
        qm = [A.alloc(f"qm{i}", [96, NTOK], BF16) for i in range(2)]
        km = [A.alloc(f"km{i}", [96, NTOK], BF16) for i in range(2)]
        ostg = [A.alloc(f"ostg{i}", [128, NT, 128], BF16) for i in range(2)]
        t1 = [A.alloc(f"t1m{i}", [128, 512], F32) for i in range(2)]
        t2 = [A.alloc(f"t2m{i}", [128, 512], F32) for i in range(2)]
        rden = [A.alloc(f"rden{i}", [128, 4], F32) for i in range(2)]
        S.op("dve", lambda e: e.memset(Vm[:, :, :, 64:65], 1.0), writes=["Vm1"])
        for t in range(NT):
            bv = 4 + t % 2
            S.op("pe", lambda e, t=t, bv=bv: e.matmul(bank(bv), lhsT=kvan[:, t * 128:(t + 1) * 128], rhs=wkvb[:, 512:1024], start=True, stop=True),
                 reads=[("kvan", t // 4), "wkvb"], writes=[("pb", bv)])
            S.op("dve", lambda e, t=t, bv=bv: e.tensor_copy(out=Vm[:, t, :, 0:64], in_=bank(bv).rearrange("p (h d) -> p h d", d=64)),
                 reads=[("pb", bv)], writes=[("Vm", t)])
        jobs = []
        jn = [0]
        for h in range(8):
            hb = h % 2
            def head_proj(h=h, hb=hb):
                for bi, (a, b) in enumerate(BLK_ALL):
                    n = b - a
                    br, bt, bk_ = 4, 5, 4
                    for kc in range(2):
                        S.op("pe", lambda e, kc=kc: e.matmul(bank(br)[0:96, 0:n], lhsT=wqb[:, h, 0, kc, :], rhs=qan[:, kc, a:b], start=(kc == 0), stop=(kc == 1)),
                             reads=["wqb", ("qan", kc, bi)], writes=[("pb", br)])
                    for kc in range(2):
                        S.op("pe", lambda e, kc=kc: e.matmul(bank(bt)[0:96, 0:n], lhsT=wqb[:, h, 1, kc, :], rhs=qan[:, kc, a:b], start=(kc == 0), stop=(kc == 1)),
                             reads=["wqb", ("qan", kc, bi)], writes=[("pb", bt)])
                    S.op("act", lambda e: e.activation(out=qm[hb][0:64, a:b], in_=bank(br)[0:64, 0:n], func=AF.Copy), reads=[("pb", br)], writes=[("qm", hb, bi, 0)])
                    i = bi % 2
                    S.op("dve", lambda e: e.tensor_tensor(out=t1[i][64:96, 0:n], in0=bank(br)[64:96, 0:n], in1=rope32[64:96, 0, a:b], op=ALU.mult),
                         reads=[("pb", br), ("rope32", 0)], writes=[("t1", i)])
                    S.op("dve", lambda e: e.tensor_tensor(out=t2[i][64:96, 0:n], in0=bank(bt)[64:96, 0:n], in1=rope32[64:96, 1, a:b], op=ALU.mult),
                         reads=[("pb", bt), ("rope32", 1)], writes=[("t2", i)])
                    S.op("pool", lambda e: e.tensor_tensor(out=qm[hb][64:96, a:b], in0=t1[i][64:96, 0:n], in1=t2[i][64:96, 0:n], op=ALU.add),
                         reads=[("t1", i), ("t2", i)], writes=[("qm", hb, bi, 1)])
                    S.op("pe", lambda e: e.matmul(bank(bk_)[0:64, 0:n], lhsT=wkvb[:, h * 64:(h + 1) * 64], rhs=kvan[:, a:b], start=True, stop=True),
                         reads=["wkvb", ("kvan", bi)], writes=[("pb", bk_)])
                    S.op("act", lambda e: e.activation(out=km[hb][0:64, a:b], in_=bank(bk_)[0:64, 0:n], func=AF.Copy), reads=[("pb", bk_)], writes=[("km", hb, bi, 0)])
                    S.op("pool", lambda e: e.tensor_copy(out=km[hb][64:96, a:b], in_=krT[64:96, a:b]), reads=[("krT", bi)], writes=[("km", hb, bi, 1)])

            qblocks = [(0, 256, [0, 1])] + [(a, b, list(range(NT))) for (a, b) in BLK_LAT]
            for qi, (qa_, qb_, keys) in enumerate(qblocks):
                nq = (qb_ - qa_) // 128
                aset = jn[0] % 2
                jn[0] += 1
                accb = 4 + aset if False else (6 + aset)

                def acc(qt, c, accb=accb):
                    return bank(accb)[:, qt * 65:(qt + 1) * 65], ("pb", accb), qt == 0

                def fin(h=h, hb=hb, qa_=qa_, nq=nq, accb=accb, aset=aset):
                    av = bank(accb)[:, 0:nq * 65].rearrange("p (h d) -> p h d", d=65)
                    akeys = [("pb", accb)]
                    S.op("dve", lambda e: e.reciprocal(out=rden[aset][:, 0:nq], in_=av[:, :, 64]), reads=akeys, writes=[("rden", aset)])
                    for qt in range(nq):
                        t = qa_ // 128 + qt
                        S.op("dve", lambda e, qt=qt, t=t: e.tensor_scalar(out=ostg[(h // 2) % 2][:, t, hb * 64:(hb + 1) * 64], in0=av[:, qt, 0:64], scalar1=rden[aset][:, qt:qt + 1],
                                                                      scalar2=None, op0=ALU.mult),
                             reads=[("pb", accb), ("rden", aset)], writes=[("ostg", (h // 2) % 2, t, hb)])
                    if hb == 1:
                        def late():
                            tb = P2[2][:].bitcast(BF16)[:, 1024:2048]
                            t0 = qa_ // 128
                            for qt in range(nq):
                                t = t0 + qt
                                S.op("pe", lambda e, t=t, qt=qt: e.transpose(out=tb[:, qt * 128:(qt + 1) * 128], in_=ostg[(h // 2) % 2][:, t, :], identity=ident_b[:]),
                                     reads=[("ostg", (h // 2) % 2, t, 0), ("ostg", (h // 2) % 2, t, 1), "ident_b"], writes=[("pb", 5)])
                            S.op("dve", lambda e: e.tensor_copy(out=oT[:, h // 2, t0 * 128:(t0 + nq) * 128], in_=tb[:, 0:nq * 128]),
                                 reads=[("pb", 5)], writes=[("hT", t0 + qt, h // 2) for qt in range(nq)])
                        return late
                    return None

                def qk_(j, hb=hb, qa_=qa_, qb_=qb_):
                    return [(0, qb_ - qa_, km[hb][0:96, j * 128:(j + 1) * 128], qm[hb][0:96, qa_:qb_], None)]

                def pv_(j, h=h, accb=accb, nq=nq):
                    return [(bank(accb)[:, qt * 65:(qt + 1) * 65], ("pb", accb), qt == 0, qt * 128, Vm[:, j, h, :]) for qt in range(nq)]

                jobs.append(dict(
                    pre=(head_proj if qi == 0 else None),
                    keys=keys, W=qb_ - qa_, scale=96 ** -0.5,
                    qk=qk_, qk_reads=lambda j, hb=hb: [("km", hb, j // 4, 0), ("km", hb, j // 4, 1)] + [("qm", hb, bi, r) for bi in range(5) for r in range(2)],
                    pv=pv_, pv_reads=lambda j: [("Vm", j), "Vm1"],
                    mask=None, fin=fin))
        hj = 0
        while hj < len(jobs):
            jobs[hj]["pre"]()
            run_attention(jobs[hj:hj + 5], ET)
            hj += 5
        checkpoint("mla0", {"oT": (hT[:], [128, 8, NTOK], BF16)})
        A.release(mA)

        mR = A.mark()
        wo = A.alloc("wo", [128, 8, D], BF16)
        for hf in range(2):
            S.dma("pool", ("wo", hf), lambda e, hf=hf: e.dma_start(out=wo[:, hf * 4:(hf + 1) * 4, :], in_=wout0_d[:, hf * 4:(hf + 1) * 4, :], max_dma_last_dim=8192), writes=["wo"])
        fl = []
        if mod1 is not None:
            setup1, fl, finish1 = mod1
            setup1()
        phase_R(list(range(NT)), lambda c, t: oT[:, c, t * 128:(t + 1) * 128], lambda c, t: [("hT", t, c)], 8, wo, "wo",
                src, True, True, dst, 0, None, fillers=fl)
        if mod1 is not None:
            finish1()
        A.release(mR)

    def ffn_sublayer(l, src, dst, dst_row0, tiles):
        need_c = tiles[0] < NCTX_T
        load_sublayer_consts(l, 1, need_c)
        phase_P(l, src, tiles, 1)
        tok0, tok1 = tiles[0] * 128, (tiles[-1] + 1) * 128
        blocks = [(a, min(a + 512, tok1)) for a in range(tok0, tok1, 512)]
        m0 = A.mark()
        convp = A.alloc("convp", [128, NFC, 4], F32)
        wsl = [A.alloc(f"wg{i}", [128, 8, 512], BF16) for i in range(3)]
        wdn = A.alloc("wdn", [128, FCP, D], BF16)
        hid = A.alloc("hid", [128, FCP, NTOK], BF16)
        cdma(lambda e: e.dma_start(out=convp[:], in_=convp_d[l]), writes=["convp"])
        def gcol(g):
            return g + 1 if g < 256 else g + 3
        for ps in range(NPASS):
            m1 = A.mark()
            G = [A.alloc(f"G{i}", [128, NTOK + 4], F32) for i in range(2)]
            T1 = [A.alloc(f"T1_{i}", [128, NTOK + 4], F32) for i in range(2)]
            for i in range(2):
                for cpad in (0, 257, 2307):
                    w = 2 if cpad == 257 else 1
                    S.op("dve", lambda e, i=i, cpad=cpad, w=w: e.memset(G[i][:, cpad:cpad + w], 0.0), writes=[("Gpad", i)])
            for ci in range(FCP):
                fc = ps * FCP + ci
                blk11 = fc // 2
                sub = fc % 2
                sl = blk11 % 3
                gate = [("hT", tiles[min(3, len(tiles) - 1)], 7)]
                if sub == 0 or ci == 0:
                    S.dma("pool", ("wg", sl), lambda e, blk11=blk11, sl=sl: e.dma_start(out=wsl[sl][:], in_=wgu_d[l, blk11], max_dma_last_dim=8192),
                          reads=gate, writes=[("wg", sl)])
                if ci == 2:
                    S.dma("pool", ("wdn", 0), lambda e, ps=ps: e.dma_start(out=wdn[:], in_=wdown_d[l, ps], max_dma_last_dim=8192), reads=gate, writes=["wdn"])
                wv = wsl[sl][:].rearrange("p k (g f) -> p k g f", g=2)
                gi = ci % 2
                c0, c1 = gcol(tok0), gcol(tok1 - 1) + 1
                for bi, (a, b) in enumerate(blocks):
                    n = b - a
                    st_ = (ci * len(blocks) + bi) % 4
                    bg, bu = 2 * st_, 2 * st_ + 1
                    for k in range(8):
                        S.op("pe", lambda e, k=k, bg=bg, wv=wv: e.matmul(bank(bg)[:, 0:n], lhsT=wv[:, k, 0, sub * 128:(sub + 1) * 128], rhs=hT[:, k, a:b], start=(k == 0), stop=(k == 7)),
                             reads=[("wg", sl)] + hT_reads(a, b, k), writes=[("pb", bg)])
                    for k in range(8):
                        S.op("pe", lambda e, k=k, bu=bu, wv=wv: e.matmul(bank(bu)[:, 0:n], lhsT=wv[:, k, 1, sub * 128:(sub + 1) * 128], rhs=hT[:, k, a:b], start=(k == 0), stop=(k == 7)),
                             reads=[("wg", sl)] + hT_reads(a, b, k), writes=[("pb", bu)])
                    segs = []
                    if a < 256 < b:
                        segs = [(a, 256), (256, b)]
                    else:
                        segs = [(a, b)]
                    for (sa, sb) in segs:
                        S.op("act", lambda e, sa=sa, sb=sb, bg=bg, gi=gi: e.activation(out=G[gi][:, gcol(sa):gcol(sa) + sb - sa], in_=bank(bg)[:, sa - a:sb - a], func=AF.Copy),
                             reads=[("pb", bg)], writes=[("G", gi, bi)])
                    S.op("dve", lambda e, bu=bu, ci=ci: e.tensor_copy(out=hid[:, ci, a:b], in_=bank(bu)[:, 0:n]), reads=[("pb", bu)], writes=[("hid", ci, bi)])
                gk = [("G", gi, bi) for bi in range(len(blocks))] + [("Gpad", gi)]
                S.op("act", lambda e, gi=gi, fc=fc: e.activation(out=T1[gi][:, c0:c1], in_=G[gi][:, c0:c1], func=AF.Identity, scale=convp[:, fc, 1:2], bias=convp[:, fc, 3:4]),
                     reads=gk + ["convp"], writes=[("T1", gi)])
                S.op("dve", lambda e, gi=gi, fc=fc: e.scalar_tensor_tensor(out=T1[gi][:, c0:c1], in0=G[gi][:, c0 - 1:c1 - 1], scalar=convp[:, fc, 0:1], in1=T1[gi][:, c0:c1], op0=ALU.mult, op1=ALU.add),
                     reads=gk + ["convp", ("T1", gi)], writes=[("T1", gi)])
                S.op("dve", lambda e, gi=gi, fc=fc: e.scalar_tensor_tensor(out=T1[gi][:, c0:c1], in0=G[gi][:, c0 + 1:c1 + 1], scalar=convp[:, fc, 2:3], in1=T1[gi][:, c0:c1], op0=ALU.mult, op1=ALU.add),
                     reads=gk + ["convp", ("T1", gi)], writes=[("T1", gi)])
                S.op("act", lambda e, gi=gi: e.activation(out=T1[gi][:, c0:c1], in_=T1[gi][:, c0:c1], func=AF.Silu), reads=[("T1", gi)], writes=[("T1", gi)])
                segs = [(tok0, 256), (256, tok1)] if tok0 < 256 else [(tok0, tok1)]
                for (sa, sb) in segs:
                    S.op("dve", lambda e, sa=sa, sb=sb, gi=gi, ci=ci: e.tensor_tensor(out=hid[:, ci, sa:sb], in0=hid[:, ci, sa:sb], in1=T1[gi][:, gcol(sa):gcol(sa) + sb - sa], op=ALU.mult),
                         reads=[("T1", gi)] + [("hid", ci, bi) for bi in range(len(blocks))], writes=[("hid", ci, bi) for bi in range(len(blocks))])
            A.release(m1)
            phase_R(tiles, lambda c, t: hid[:, c, t * 128:(t + 1) * 128],
                    lambda c, t: [("hid", c, bi) for bi in range(len(blocks)) if blocks[bi][0] <= t * 128 < blocks[bi][1]],
                    FCP, wdn, "wdn", src, ps == 0, ps == NPASS - 1, dst, dst_row0, ybuf)
        A.release(m0)

    def layer1_attention(src, dst):
        load_sublayer_consts(1, 0, False)
        phase_P(1, src, list(range(NT)), 0)
        lambda_init = 0.8 - 0.6 * math.exp(-0.3 * 1)
        mA = A.mark()
        oT = A.alloc("oT1", [128, 8, 2048], BF16)
        rope64 = A.alloc("rope64b", [128, 2, NTOK], F32)
        ET = [A.alloc(f"ETb{i}", [128, 512], BF16) for i in range(4)]
        t1 = [A.alloc(f"t1b{i}", [128, 256], F32) for i in range(2)]
        t2 = [A.alloc(f"t2b{i}", [128, 256], F32) for i in range(2)]
        wo = A.alloc("wo1", [128, 8, D], BF16)
        gsub = A.alloc("gsub", [128, 128], F32)
        lamt = A.alloc("lamt", [128, 4, 64], F32)
        lams = A.alloc("lams", [128, 8], F32)
        accs = [A.alloc(f"accs{i}", [128, 4, 129], F32) for i in range(2)]
        rr = [A.alloc(f"rr{i}", [128, 8], F32) for i in range(2)]
        uu = [A.alloc(f"uu{i}", [128, 128], F32) for i in range(2)]
        avv = [A.alloc(f"avv{i}", [128, 128], F32) for i in range(2)]
        junk = A.alloc("junk", [128, 128], BF16)
        otk = [A.alloc(f"otkb{i}", [128, 128], BF16) for i in range(4)]
        mH = A.mark()
        V1 = [A.alloc(f"V1_{i}", [128, NT, 130], BF16) for i in range(2)]
        qT = [A.alloc(f"qT1_{i}", [128, 2048], BF16) for i in range(2)]
        kT0 = [A.alloc(f"kT0_{i}", [128, NTOK], BF16) for i in range(2)]
        kT1 = [A.alloc(f"kT1_{i}", [128, NTOK], BF16) for i in range(2)]
        w1 = [A.alloc(f"w1h_{i}", [128, 5, 8, 128], BF16) for i in range(2)]

        for r in range(2):
            cdma(lambda e, r=r: e.dma_start(out=rope64[:, r, :], in_=rope64_d[r]), writes=[("rope64", r)])
        cdma(lambda e: e.dma_start(out=gsub[:], in_=subg_d.partition_broadcast(128)), writes=["gsub"])
        S.op("dve", lambda e: e.tensor_scalar(out=gsub[:], in0=gsub[:], scalar1=(1.0 - lambda_init), scalar2=None, op0=ALU.mult), reads=["gsub"], writes=["gsub"])
        for i in range(4):
            cdma(lambda e, i=i: e.dma_start(out=lamt[:, i, :], in_=lamv_d[i].partition_broadcast(128)), writes=[("lamt", i)])
        for i in range(2):
            S.op("dve", lambda e, i=i: e.tensor_tensor(out=lamt[:, 2 * i, :], in0=lamt[:, 2 * i, :], in1=lamt[:, 2 * i + 1, :], op=ALU.mult),
                 reads=[("lamt", 2 * i), ("lamt", 2 * i + 1)], writes=[("lamt", 2 * i)])
            S.op("dve", lambda e, i=i: e.reduce_sum(out=lams[:, i:i + 1], in_=lamt[:, 2 * i, :], axis=mybir.AxisListType.X), reads=[("lamt", 2 * i)], writes=["lams"])
        S.op("act", lambda e: e.activation(out=lams[:, 2:4], in_=lams[:, 0:2], func=AF.Exp), reads=["lams"], writes=["lams"])
        S.op("dve", lambda e: e.scalar_tensor_tensor(out=lams[:, 4:5], in0=lams[:, 3:4], scalar=-lambda_init, in1=lams[:, 2:3], op0=ALU.add, op1=ALU.subtract),
             reads=["lams"], writes=["lams"])
        for i in range(2):
            S.op("dve", lambda e, i=i: e.memset(V1[i][:, :, 128:129], 1.0), writes=[("V11", i)])
            S.op("dve", lambda e, i=i: e.memset(kT0[i][64:128, :], 0.0), writes=[("kTz", i)])
            S.op("dve", lambda e, i=i: e.memset(kT1[i][0:64, :], 0.0), writes=[("kTz", i)])
        for hf in range(2):
            S.dma("pool", ("wo", hf), lambda e, hf=hf: e.dma_start(out=wo[:, hf * 4:(hf + 1) * 4, :], in_=wout1_d[:, hf * 4:(hf + 1) * 4, :], max_dma_last_dim=8192), writes=["wo"])

        tpb = P2[3][:].bitcast(BF16)[:, 0:1024]
        PB = 7
        jn = [0]
        uc = [0]

        def proj_units(h):
            hb = h % 2
            units = []

            def load_w():
                S.dma("pool", ("w1", hb), lambda e: e.dma_start(out=w1[hb][:], in_=w1h_d[h], max_dma_last_dim=8192), writes=[("w1", hb)])

            def qk_unit(chunk, a, b, dsts):
                def f():
                    n = b - a
                    i = uc[0] % 2
                    uc[0] += 1
                    for r in range(2):
                        for k in range(8):
                            S.op("pe", lambda e, k=k, r=r: e.matmul(bank(PB)[:, r * 256:r * 256 + n], lhsT=w1[hb][:, chunk + r, k, :], rhs=hT[:, k, a:b], start=(k == 0), stop=(k == 7)),
                                 reads=[("w1", hb)] + hT_reads(a, b, k), writes=[("pb", PB)])
                    S.op("dve", lambda e: e.tensor_tensor(out=t1[i][:, 0:n], in0=bank(PB)[:, 0:n], in1=rope64[:, 0, a:b], op=ALU.mult),
                         reads=[("pb", PB), ("rope64", 0)], writes=[("t1", i)])
                    S.op("dve", lambda e: e.tensor_tensor(out=t2[i][:, 0:n], in0=bank(PB)[:, 256:256 + n], in1=rope64[:, 1, a:b], op=ALU.mult),
                         reads=[("pb", PB), ("rope64", 1)], writes=[("t2", i)])
                    for (p0, p1, dst_ap, wkey) in dsts:
                        S.op("pool", lambda e, p0=p0, p1=p1, dst_ap=dst_ap: e.tensor_tensor(out=dst_ap, in0=t1[i][p0:p1, 0:n], in1=t2[i][p0:p1, 0:n], op=ALU.add),
                             reads=[("t1", i), ("t2", i)], writes=[wkey])
                return f

            def v_unit(t0):
                def f():
                    tl = list(range(t0, min(t0 + 2, NT)))
                    for ti, t in enumerate(tl):
                        for k in range(8):
                            S.op("pe", lambda e, k=k, t=t, ti=ti: e.matmul(bank(PB)[:, ti * 128:(ti + 1) * 128], lhsT=hT[:, k, t * 128:(t + 1) * 128], rhs=w1[hb][:, 4, k, :],
                                                                          start=(k == 0), stop=(k == 7)),
                                 reads=[("w1", hb), ("hT", t, k)], writes=[("pb", PB)])
                    nt_ = len(tl)
                    S.op("dve", lambda e: e.tensor_copy(out=V1[hb][:, t0:t0 + nt_, 0:128], in_=bank(PB)[:, 0:nt_ * 128].rearrange("p (t d) -> p t d", d=128)),
                         reads=[("pb", PB)], writes=[("V1", hb, t0 // 2)])
                return f

            for u in range(9):
                a = u * 256
                units.append(qk_unit(2, a, a + 256, [(0, 64, kT0[hb][0:64, a:a + 256], ("kT", hb, 0, u)), (64, 128, kT1[hb][64:128, a:a + 256], ("kT", hb, 1, u))]))
            for u in range(9):
                units.append(v_unit(2 * u))
            for u in range(8):
                a = 256 + u * 256
                units.append(qk_unit(0, a, a + 256, [(0, 128, qT[hb][:, u * 256:(u + 1) * 256], ("qT", hb, u))]))
            return load_w, units

        def head_jobs(h):
            hb = h % 2
            jobs = []
            for qb in range(8):
                qa = qb * 256
                aset = jn[0] % 2
                jn[0] += 1

                def accinfo(qt, c):
                    idx = qt * 2 + c
                    bk = 4 + idx // 3
                    sl_ = idx % 3
                    return bank(bk)[:, sl_ * 129:(sl_ + 1) * 129], ("pb", bk), sl_ == 0

                def qk_(j, qa=qa):
                    return [(0, 256, kT0[hb][:, j * 128:(j + 1) * 128], qT[hb][:, qa:qa + 256], None),
                            (256, 256, kT1[hb][:, j * 128:(j + 1) * 128], qT[hb][:, qa:qa + 256], None)]

                def pv_(j):
                    ops = []
                    for qt in range(2):
                        for c in range(2):
                            ap_, key_, fib = accinfo(qt, c)
                            ops.append((ap_, key_, fib, c * 256 + qt * 128, V1[hb][:, j, 0:129]))
                    return ops

                def fin(qa=qa, aset=aset):
                    ac = accs[aset]
                    S.op("act", lambda e: e.activation(out=ac[:, 0:3, :], in_=bank(4)[:, 0:3 * 129].rearrange("p (a d) -> p a d", d=129), func=AF.Copy),
                         reads=[("pb", 4)], writes=[("accs", aset, 0)])
                    S.op("act", lambda e: e.activation(out=ac[:, 3, :], in_=bank(5)[:, 0:129], func=AF.Copy), reads=[("pb", 5)], writes=[("accs", aset, 1)])
                    for qt in range(2):
                        i = qt
                        ak = [("accs", aset, 0), ("accs", aset, 1)]
                        S.op("dve", lambda e, qt=qt, i=i: e.reciprocal(out=rr[i][:, 0:2], in_=ac[:, qt * 2:qt * 2 + 2, 128]), reads=ak, writes=[("rr", i)])
                        S.op("dve", lambda e, i=i: e.tensor_tensor(out=rr[i][:, 2:3], in0=rr[i][:, 1:2], in1=lams[:, 4:5], op=ALU.mult), reads=[("rr", i), "lams"], writes=[("rr", i)])
                        S.op("dve", lambda e, qt=qt, i=i: e.tensor_scalar(out=uu[i][:], in0=ac[:, qt * 2 + 1, 0:128], scalar1=rr[i][:, 2:3], scalar2=None, op0=ALU.mult),
                             reads=ak + [("rr", i)], writes=[("uu", i)])
                        S.op("dve", lambda e, qt=qt, i=i: e.scalar_tensor_tensor(out=avv[i][:], in0=ac[:, qt * 2, 0:128], scalar=rr[i][:, 0:1], in1=uu[i][:], op0=ALU.mult, op1=ALU.add),
                             reads=ak + [("rr", i), ("uu", i)], writes=[("avv", i)])

                    def late0():
                        for qt in range(2):
                            i = qt
                            S.op("act", lambda e, i=i: e.activation(out=junk[:], in_=avv[i][:], func=AF.Square, accum_out=rr[i][:, 3:4]), reads=[("avv", i)], writes=["junk", ("rr", i)])
                            rsqrt_small(rr[i][:, 4:5], rr[i][:, 3:4], 1.0 / 128.0, 1, [("rr", i)], ("rr", i))

                    def late1():
                        for qt in range(2):
                            i = qt
                            oi = aset * 2 + qt
                            S.op("dve", lambda e, i=i, oi=oi: e.scalar_tensor_tensor(out=otk[oi][:], in0=avv[i][:], scalar=rr[i][:, 4:5], in1=gsub[:], op0=ALU.mult, op1=ALU.mult),
                                 reads=[("avv", i), ("rr", i), "gsub"], writes=[("otk", oi)])

                    def late2():
                        for qt in range(2):
                            oi = aset * 2 + qt
                            S.op("pe", lambda e, oi=oi: e.transpose(out=tpb[:, oi * 128:(oi + 1) * 128], in_=otk[oi][:], identity=ident_b[:]),
                                 reads=[("otk", oi), "ident_b"], writes=[("pb", 6)])
                        S.op("dve", lambda e: e.tensor_copy(out=oT[:, h, qa:qa + 256], in_=tpb[:, aset * 256:aset * 256 + 256]),
                             reads=[("pb", 6)], writes=[("oT", h, qa // 128), ("oT", h, qa // 128 + 1)])
                    return [(7, late0), (11, late1), (14, late2)]

                jobs.append(dict(
                    keys=list(range(NT)), W=512, scale=64 ** -0.5,
                    qk=qk_, qk_reads=lambda j, qb=qb: [("kT", hb, 0, j // 2), ("kT", hb, 1, j // 2), ("kTz", hb), ("qT", hb, qb)],
                    pv=pv_, pv_reads=lambda j: [("V1", hb, j // 2), ("V11", hb)],
                    mask=None, fin=fin))
            return jobs

        lw, units = proj_units(0)
        lw()
        for f in units:
            f()
        for h in range(8):
            fl = []
            if h + 1 < 8:
                lw, fl = proj_units(h + 1)
                lw()
            run_attention(head_jobs(h), ET, fl)
        A.release(mH)

        lat = list(range(NCTX_T, NT))
        phase_R(lat, lambda c, t: oT[:, c, (t - NCTX_T) * 128:(t - NCTX_T + 1) * 128], lambda c, t: [("oT", c, t - NCTX_T)], 8, wo, "wo",
                src, True, True, dst, 0, None)
        A.release(mA)

    cur_src = x_all
    dbg_x = None
    if stop_after is not None:
        dbg_x = nc.dram_tensor("dbg_x", [NTOK, D], F32, kind="ExternalOutput").ap()
    try:
      for l in layers:
          if l == 0:
              mod0 = modulation_units(0, True, 6, 7)
              mod1 = modulation_units(1, False, 6, 7) if 1 in layers else None
              if stop_after == "attn0":
                  layer0_attention(cur_src, dbg_x, mod0, mod1)
                  break
              layer0_attention(cur_src, xres[0], mod0, mod1)
              if stop_after == "ffn0":
                  ffn_sublayer(0, xres[0], dbg_x, 0, list(range(NT)))
                  break
              ffn_sublayer(0, xres[0], xres[1], 0, list(range(NT)))
              cur_src = xres[1]
          else:
              if 0 not in layers:
                  su, bl, fi = modulation_units(1, False, 6, 7)
                  su()
                  for f in bl:
                      f()
                  fi()
              if stop_after == "attn1":
                  layer1_attention(cur_src, dbg_x)
                  break
              layer1_attention(cur_src, xres[0])
              ffn_sublayer(1, xres[0], out_d, 256, list(range(NCTX_T, NT)))
    except _Stop:
        pass
    S.barrier()
    nsem = S.emit()
    info = dict(nsem=nsem, peak=A.peak - A.base, cnt=dict(S.cnt))
    return nc, info


def _rope_tables(dim):
    rows = 2048 // 64
    row = np.repeat(np.arange(rows), 64).astype(np.float32)
    col = np.tile(np.arange(64), rows).astype(np.float32)
    q = dim // 4
    freqs = (np.float32(10000.0) ** (-np.arange(q, dtype=np.float32) / np.float32(q))).astype(np.float32)
    ar = row[:, None] * freqs
    ac = col[:, None] * freqs
    cos = np.concatenate([np.cos(ar), np.cos(ar), np.cos(ac), np.cos(ac)], -1).astype(np.float32)
    sin = np.concatenate([np.sin(ar), np.sin(ar), np.sin(ac), np.sin(ac)], -1).astype(np.float32)
    sign = np.concatenate([-np.ones(q), np.ones(q), -np.ones(q), np.ones(q)]).astype(np.float32)
    perm = np.concatenate([np.arange(q, 2 * q), np.arange(0, q), np.arange(3 * q, 4 * q), np.arange(2 * q, 3 * q)])
    cosT = np.ones((dim, NTOK), np.float32)
    sinT = np.zeros((dim, NTOK), np.float32)
    cosT[:, 256:] = cos.T
    sinT[:, 256:] = (sin * sign[None, :]).T
    return cosT, sinT, perm


def _prep_shared(inp):
    f32 = np.float32
    cos64, sin64, perm64 = _rope_tables(64)
    cos32, sin32, perm32 = _rope_tables(32)
    sh = {}
    w_mod = inp["w_mod"]
    sh["wmod"] = np.ascontiguousarray(w_mod.reshape(2, 8, 128, 12, 512).transpose(0, 3, 2, 1, 4))
    sh["bmod_pp"] = np.ascontiguousarray(inp["b_mod"].reshape(2, 6, 8, 128).transpose(0, 3, 1, 2).reshape(2, 128, 48))
    sh["bmod"] = np.ascontiguousarray(inp["b_mod"])
    sh["lnv"] = np.ascontiguousarray(np.stack([inp["ln1_g"], inp["ln1_b"], inp["ln2_g"], inp["ln2_b"]], 1))
    g = inp["ffn_w_gate"].reshape(2, 8, 128, 11, 256).transpose(0, 3, 2, 1, 4)
    u = inp["ffn_w_up"].reshape(2, 8, 128, 11, 256).transpose(0, 3, 2, 1, 4)
    sh["wgu"] = np.ascontiguousarray(np.stack([g, u], 4).reshape(2, 11, 128, 8, 512))
    cw = inp["ffn_conv_w"].reshape(2, 3, NFC, 128).transpose(0, 3, 2, 1)
    cb = inp["ffn_conv_b"].reshape(2, NFC, 128).transpose(0, 2, 1)[..., None]
    sh["convp"] = np.ascontiguousarray(np.concatenate([cw, cb], -1))
    sh["wdown"] = np.ascontiguousarray(inp["ffn_w_down"].reshape(2, NPASS, FCP, 128, D).transpose(0, 1, 3, 2, 4))
    W = inp["ab_w_in"][0]
    z = lambda n: np.zeros((D, n), f32)
    chunks = [W[:, 0:128], W[:, 128:256]]
    qs = W[:, 256:768].reshape(D, 8, 64)
    for c in range(4):
        chunks.append(np.concatenate([qs[:, c], qs[:, 4 + c]], 1))
    for c in range(4):
        chunks.append(np.concatenate([qs[:, c][:, perm64], qs[:, 4 + c][:, perm64]], 1))
    chunks.append(W[:, 768:896])
    kr = W[:, 896:928]
    chunks.append(np.concatenate([z(64), kr, z(32)], 1))
    chunks.append(np.concatenate([z(64), kr[:, perm32], z(32)], 1))
    ks = W[:, 928:1056].reshape(D, 2, 64)
    chunks.append(np.concatenate([ks[:, 0], ks[:, 1]], 1))
    chunks.append(np.concatenate([ks[:, 0][:, perm64], ks[:, 1][:, perm64]], 1))
    chunks.append(W[:, 1056:1184])
    w0 = np.stack(chunks, 0).reshape(16, 8, 128, 128).transpose(2, 0, 1, 3)
    sh["w0"] = np.ascontiguousarray(w0)
    Wq = inp["mla_w_qb"][0].reshape(256, 8, 96)
    raw = Wq
    rot = np.concatenate([np.zeros((256, 8, 64), f32), Wq[:, :, 64:96][:, :, perm32]], 2)
    wqb = np.stack([raw, rot], 0)
    wqb = wqb.reshape(2, 2, 128, 8, 96).transpose(2, 3, 0, 1, 4)
    sh["wqb"] = np.ascontiguousarray(wqb)
    Wkv = inp["mla_w_kvb"][0].reshape(128, 8, 128)
    sh["wkvb"] = np.ascontiguousarray(np.concatenate([Wkv[:, :, 0:64].reshape(128, 512), Wkv[:, :, 64:128].reshape(128, 512)], 1))
    sh["qnorm_pp"] = np.ascontiguousarray(inp["mla_q_norm"][0].reshape(2, 128).T)
    sh["kvnorm_pp"] = np.ascontiguousarray(inp["mla_kv_norm"][0].reshape(128, 1))
    sh["sink"] = np.ascontiguousarray(inp["swa_sink"][0])
    sh["wout0"] = np.ascontiguousarray(inp["ab_w_out"][0].reshape(8, 128, D).transpose(1, 0, 2))
    W1 = inp["diff_w_in"][0]
    perm128 = np.concatenate([perm64, 64 + perm64])
    heads = []
    for h in range(8):
        q = W1[:, h * 128:(h + 1) * 128]
        k = W1[:, 1024 + h * 128:1024 + (h + 1) * 128]
        v = W1[:, 2048 + h * 128:2048 + (h + 1) * 128]
        hw = np.stack([q, q[:, perm128], k, k[:, perm128], v], 0)
        heads.append(hw.reshape(5, 8, 128, 128).transpose(2, 0, 1, 3))
    sh["w1h"] = np.ascontiguousarray(np.stack(heads, 0))
    sh["wout1"] = np.ascontiguousarray(inp["diff_w_out"][0].reshape(8, 128, D).transpose(1, 0, 2))
    sh["lamv"] = np.ascontiguousarray(np.stack([inp["diff_lam_q1"][0], inp["diff_lam_k1"][0], inp["diff_lam_q2"][0], inp["diff_lam_k2"][0]], 0))
    sh["subg"] = np.ascontiguousarray(inp["diff_subln_g"][0])
    sh["ident_f"] = np.eye(128, dtype=f32)
    sh["ident_b"] = np.eye(128, dtype=f32).astype(ml_dtypes.bfloat16)
    jj = np.arange(128)[:, None]
    ii = np.arange(128)[None, :]
    mP = (jj >= ii).astype(f32)
    mN = (jj <= ii).astype(f32)
    sh["masks"] = np.stack([np.tile(mP, (1, 4)), np.tile(mN, (1, 4))], 0).astype(ml_dtypes.bfloat16)
    sh["rope64"] = np.ascontiguousarray(np.stack([np.tile(cos64, (2, 1)), np.tile(sin64, (2, 1))], 0))
    r32 = np.zeros((2, 128, NTOK), f32)
    r32[0, 64:96] = cos32
    r32[1, 64:96] = sin32
    sh["rope32"] = r32
    return sh


_CACHE = {}


def _get_program(layers, dbg=()):
    key = (tuple(layers), tuple(dbg))
    if key not in _CACHE:
        _CACHE[key] = build_program(layers, dbg)
    return _CACHE[key]


def kernel(**inputs):
    inp = {k: np.asarray(v, dtype=np.float32) for k, v in inputs.items()}
    sh = _prep_shared(inp)
    B = inp["x"].shape[0]
    in_maps = []
    for b in range(B):
        m = dict(sh)
        m["x_all"] = np.ascontiguousarray(np.concatenate([inp["ctx"][b], inp["x"][b]], 0))
        cc = np.stack([inp["c"][b], inp["c_ctx"]], 0)
        m["cT"] = np.ascontiguousarray(cc.reshape(2, 8, 128).transpose(2, 1, 0))
        in_maps.append(m)
    nc, info = _get_program((0, 1))
    res = run_bass_kernel_spmd(nc, in_maps, core_ids=list(range(B)))
    out = np.stack([np.asarray(r["out"], dtype=np.float32) for r in res.results], 0)
    return out
```

```python
import math
import os
import numpy as np
import ml_dtypes
import concourse.bass as bass
import concourse.mybir as mybir
from concourse.bass_utils import run_bass_kernel_spmd

F32 = mybir.dt.float32
BF16 = mybir.dt.bfloat16
AF = mybir.ActivationFunctionType
ALU = mybir.AluOpType

D = 1024
DFF = 2816
NTOK = 2304
NT = 18
NCTX_T = 2
ALPHA = 4.0 ** 0.25
LN_EPS = 1e-5
RMS_EPS = 1e-6
NFC = 22
NPASS = 2
FCP = NFC // NPASS


class _Rec:
    def __init__(self):
        self.call = None

    def __getattr__(self, name):
        def f(*a, **k):
            self.call = (name, a, k)
            return self
        return f


def _record(fn):
    r = _Rec()
    fn(r)
    assert r.call is not None
    return r.call


class Sched:
    ENGS = ("pe", "act", "dve", "pool", "sp")

    def __init__(self, nc, same_engine_sync=True):
        self.nc = nc
        self.streams = {e: [] for e in self.ENGS}
        self.cnt = {e: 0 for e in self.ENGS}
        self.waited = {}
        self.lastw = {}
        self.readers = {}
        self.semcnt = {}
        self.same = same_engine_sync

    def _deps(self, reads, writes):
        deps = {}

        def add(k, v):
            if deps.get(k, 0) < v:
                deps[k] = v

        for r in reads:
            t = self.lastw.get(r)
            if t is not None:
                add(*t)
        for w in writes:
            t = self.lastw.get(w)
            if t is not None:
                add(*t)
            for k, v in self.readers.get(w, {}).items():
                add(k, v)
        return deps

    def _commit(self, tok, reads, writes):
        k, v = tok
        for r in reads:
            d = self.readers.setdefault(r, {})
            if d.get(k, 0) < v:
                d[k] = v
        for w in writes:
            self.lastw[w] = tok
            self.readers[w] = {}

    def _waits(self, eng, deps):
        waits = []
        for k, v in deps.items():
            if k == eng and (eng == "pe" or not self.same):
                continue
            if self.waited.get((eng, k), 0) >= v:
                continue
            self.waited[(eng, k)] = v
            waits.append((k, v))
        return waits

    def op(self, eng, fn, reads=(), writes=()):
        deps = self._deps(reads, writes)
        waits = self._waits(eng, deps)
        self.cnt[eng] += 1
        tok = (eng, self.cnt[eng])
        self.semcnt[eng] = self.cnt[eng]
        self.streams[eng].append((waits, _record(fn), (eng, 1)))
        self._commit(tok, reads, writes)
        return tok

    def dma(self, q, semkey, fn, reads=(), writes=()):
        deps = self._deps(reads, writes)
        prev = self.semcnt.get(semkey, 0)
        if prev and deps.get(semkey, 0) < prev:
            deps[semkey] = prev
        waits = self._waits(q, deps)
        self.semcnt[semkey] = prev + 16
        tok = (semkey, prev + 16)
        self.streams[q].append((waits, _record(fn), (semkey, 16)))
        self._commit(tok, reads, writes)
        return tok

    def wait_all(self, eng):
        waits = []
        for k, v in self.semcnt.items():
            if k == eng:
                continue
            if self.waited.get((eng, k), 0) >= v:
                continue
            self.waited[(eng, k)] = v
            waits.append((k, v))
        self.streams[eng].append((waits, None, None))

    def barrier(self):
        for e in self.ENGS:
            self.wait_all(e)

    def emit(self):
        nc = self.nc
        sems = {}
        for i, k in enumerate(self.semcnt):
            sems[k] = nc.alloc_semaphore(name=f"sm{i}")
        streams = self.streams

        def run(engname, eng):
            for waits, fn, inc in streams[engname]:
                for k, v in waits:
                    eng.wait_ge(sems[k], v)
                if fn is None:
                    continue
                name, a_, k_ = fn
                ins = getattr(eng, name)(*a_, **k_)
                if inc is not None:
                    ins.then_inc(sems[inc[0]], inc[1])

        with nc.Block() as block:
            @block.tensor
            def _(e):
                run("pe", e)

            @block.scalar
            def _(e):
                run("act", e)

            @block.vector
            def _(e):
                run("dve", e)

            @block.gpsimd
            def _(e):
                run("pool", e)

            @block.sync
            def _(e):
                run("sp", e)
        return len(sems)


class Arena:
    def __init__(self, nc, S):
        self.nc = nc
        self.S = S
        self.base = (nc.sbuf_base + 63) // 64 * 64
        self.top = nc.sbuf_top
        self.cur = self.base
        self.n = 0
        self.peak = 0

    def alloc(self, name, shape, dt):
        per = 1
        for s in shape[1:]:
            per *= s
        per *= 2 if dt == BF16 else 4
        off = self.cur
        self.cur = (off + per + 63) // 64 * 64
        assert self.cur <= self.top, f"SBUF overflow allocating {name}: {self.cur} > {self.top}"
        self.peak = max(self.peak, self.cur)
        self.n += 1
        return self.nc.alloc_sbuf_tensor_at(f"{name}_{self.n}", list(shape), dt, offset=off)

    def mark(self):
        return self.cur

    def release(self, m):
        self.S.barrier()
        self.cur = m


def build_program(layers=(0, 1), dbg=(), stop_after=None):
    nc = bass.Bass("TRN2", target_bir_lowering=False)
    S = Sched(nc)
    A = Arena(nc, S)
    first_layer, last_layer = layers[0], layers[-1]

    def din(name, shape, dt=F32):
        return nc.dram_tensor(name, list(shape), dt, kind="ExternalInput").ap()

    def dscr(name, shape, dt=F32):
        return nc.dram_tensor(name, list(shape), dt, kind="Internal").ap()

    x_all = din("x_all", [NTOK, D])
    cT_d = din("cT", [128, 8, 2])
    wmod_d = din("wmod", [2, 12, 128, 8, 512])
    bmodpp_d = din("bmod_pp", [2, 128, 48])
    bmod_d = din("bmod", [2, 6144])
    lnv_d = din("lnv", [2, 4, D])
    wgu_d = din("wgu", [2, 11, 128, 8, 512])
    convp_d = din("convp", [2, 128, NFC, 4])
    wdown_d = din("wdown", [2, NPASS, 128, FCP, D])
    w0_d = din("w0", [128, 16, 8, 128])
    wqb_d = din("wqb", [128, 8, 2, 2, 96])
    wkvb_d = din("wkvb", [128, 1024])
    qnorm_d = din("qnorm_pp", [128, 2])
    kvnorm_d = din("kvnorm_pp", [128, 1])
    sink_d = din("sink", [8])
    wout0_d = din("wout0", [128, 8, D])
    w1h_d = din("w1h", [8, 128, 5, 8, 128])
    wout1_d = din("wout1", [128, 8, D])
    lamv_d = din("lamv", [4, 64])
    subg_d = din("subg", [128])
    identf_d = din("ident_f", [128, 128])
    identb_d = din("ident_b", [128, 128], BF16)
    masks_d = din("masks", [2, 128, 512], BF16)
    rope64_d = din("rope64", [2, 128, NTOK])
    rope32_d = din("rope32", [2, 128, NTOK])
    out_d = nc.dram_tensor("out", [2048, D], F32, kind="ExternalOutput").ap()
    xres = [dscr("xres_a", [NTOK, D]), dscr("xres_b", [NTOK, D])]
    ybuf = dscr("ybuf", [NTOK, D])
    gts_d = dscr("gts", [2, 2, 2, 128, D])
    dbg_out = {}

    P2 = [nc.alloc_psum_tensor(f"pp{i}", [128, 1024], F32) for i in range(4)]

    def bank(i):
        return P2[i // 2][:, (i % 2) * 512:(i % 2 + 1) * 512]

    ident_f = A.alloc("ident_f", [128, 128], F32)
    ident_b = A.alloc("ident_b", [128, 128], BF16)
    ones_f = A.alloc("ones_f", [128, 128], F32)
    epsb = A.alloc("epsb", [128, 2], F32)
    masks = A.alloc("masks", [128, 2, 512], BF16)
    mpp2 = A.alloc("mpp", [128, 2, 4, 8, 2], F32)
    hT = A.alloc("hT", [128, 8, NTOK], BF16)
    gt_t = A.alloc("gt_t", [128, 2, D], F32)
    lng_t = A.alloc("lng_t", [128, D], F32)
    lnb_t = A.alloc("lnb_t", [128, D], F32)

    class _Stop(Exception):
        pass

    def checkpoint(name, tensors):
        if name in dbg or stop_after == name:
            S.barrier()
            for nm, (t_ap, shape, dt) in tensors.items():
                d = nc.dram_tensor("dbg_" + nm, list(shape), dt, kind="ExternalOutput").ap()
                cdma(lambda e: e.dma_start(out=d, in_=t_ap))
        if stop_after == name:
            raise _Stop()

    cq = [0]

    def cdma(fn, reads=(), writes=(), q="sp"):
        cq[0] += 1
        return S.dma(q, ("c", cq[0] % 4), fn, reads=reads, writes=writes)

    cdma(lambda e: e.dma_start(out=ident_f[:], in_=identf_d), writes=["ident_f"])
    cdma(lambda e: e.dma_start(out=ident_b[:], in_=identb_d), writes=["ident_b"])
    cdma(lambda e: e.dma_start(out=masks[:], in_=masks_d.rearrange("m p n -> p m n")), writes=["masks"])
    S.op("dve", lambda e: e.memset(ones_f[:], 1.0), writes=["ones_f"])
    S.op("dve", lambda e: e.memset(epsb[:, 0:1], LN_EPS), writes=["epsb"])
    S.op("dve", lambda e: e.memset(epsb[:, 1:2], RMS_EPS), writes=["epsb"])

    def ttiles(a, b):
        return range(a // 128, (b + 127) // 128)

    def modulation_units(l, need_c_gates, pbank_pp, pbank_gt):
        st = {}
        nvar = 2 if need_c_gates else 1
        vi_of = {0: 0, 1: 1, 3: 2, 4: 3}

        def setup():
            st["m0"] = A.mark()
            cT = st["cT"] = A.alloc("cT", [128, 8, 2], F32)
            scf = st["scf"] = A.alloc("scf", [128, 8, 2], F32)
            scb = st["scb"] = A.alloc("scb", [128, 8, 2], BF16)
            cbc = st["cbc"] = A.alloc("cbc", [128, 2, 8, 128], BF16)
            bpp = st["bpp"] = A.alloc("bpp", [128, 48], F32)
            bmb = st["bmb"] = A.alloc("bmb", [128, 2, D], F32)
            st["gtb"] = A.alloc("gtb", [128, 2, 2, D], F32)
            st["wsl"] = [A.alloc(f"wm{i}", [128, 8, 512], BF16) for i in range(3)]
            cdma(lambda e: e.dma_start(out=cT[:], in_=cT_d), writes=[("cT", l)])
            cdma(lambda e: e.dma_start(out=bpp[:], in_=bmodpp_d[l]), writes=[("bpp", l)])
            for gi, vec in enumerate((2, 5)):
                cdma(lambda e, gi=gi, vec=vec: e.dma_start(out=bmb[:, gi, :], in_=bmod_d[l, vec * 1024:(vec + 1) * 1024].partition_broadcast(128)),
                     writes=[("bmb", l, gi)])
            S.op("act", lambda e: e.activation(out=scf[:], in_=cT[:], func=AF.Silu), reads=[("cT", l)], writes=[("scf", l)])
            S.op("dve", lambda e: e.tensor_copy(out=scb[:], in_=scf[:]), reads=[("scf", l)], writes=[("scb", l)])
            for v in range(nvar):
                for k in range(8):
                    S.op("dve", lambda e, v=v, k=k: e.tensor_scalar(out=cbc[:, v, k, :], in0=ones_f[:], scalar1=scf[:, k, v:v + 1], scalar2=None, op0=ALU.mult),
                         reads=[("scf", l), "ones_f"], writes=[("cbc", l, v)])

        def block(blk, n):
            def f():
                wsl, scb, cbc, bpp, bmb, gtb = st["wsl"], st["scb"], st["cbc"], st["bpp"], st["bmb"], st["gtb"]
                sl = n % 3
                vec, half = blk // 2, blk % 2
                S.dma("pool", ("wm", sl), lambda e: e.dma_start(out=wsl[sl][:], in_=wmod_d[l, blk], max_dma_last_dim=8192), writes=[("wm", l, sl)])
                if vec in vi_of:
                    vi = vi_of[vec]
                    pb = ("pb", pbank_pp)
                    for j in range(4):
                        for k in range(8):
                            S.op("pe", lambda e, j=j, k=k: e.matmul(bank(pbank_pp)[:, j * 2:j * 2 + 2], lhsT=wsl[sl][:, k, j * 128:(j + 1) * 128],
                                                                    rhs=scb[:, k, :], start=(k == 0), stop=(k == 7)),
                                 reads=[("wm", l, sl), ("scb", l)], writes=[pb])
                    c0 = half * 4
                    psv = bank(pbank_pp)[:, 0:8].rearrange("p (j v) -> p j v", v=2)
                    for v in range(2):
                        if vec in (1, 4):
                            S.op("dve", lambda e, v=v: e.scalar_tensor_tensor(
                                out=mpp2[:, l, vi, c0:c0 + 4, v], in0=psv[:, :, v], scalar=1.0, in1=bpp[:, vec * 8 + c0:vec * 8 + c0 + 4], op0=ALU.add, op1=ALU.add),
                                reads=[pb, ("bpp", l)], writes=[("mpp", l, vi, half, v)])
                        else:
                            S.op("dve", lambda e, v=v: e.tensor_tensor(
                                out=mpp2[:, l, vi, c0:c0 + 4, v], in0=psv[:, :, v], in1=bpp[:, vec * 8 + c0:vec * 8 + c0 + 4], op=ALU.add),
                                reads=[pb, ("bpp", l)], writes=[("mpp", l, vi, half, v)])
                else:
                    gi = 0 if vec == 2 else 1
                    pb = ("pb", pbank_gt)
                    for v in range(nvar):
                        for k in range(8):
                            S.op("pe", lambda e, k=k, v=v: e.matmul(bank(pbank_gt), lhsT=cbc[:, v, k, :], rhs=wsl[sl][:, k, :], start=(k == 0), stop=(k == 7)),
                                 reads=[("wm", l, sl), ("cbc", l, v)], writes=[pb])
                        S.op("dve", lambda e, v=v: e.tensor_tensor(
                            out=gtb[:, gi, v, half * 512:(half + 1) * 512], in0=bank(pbank_gt), in1=bmb[:, gi, half * 512:(half + 1) * 512], op=ALU.add),
                            reads=[pb, ("bmb", l, gi)], writes=[("gtb", l, gi, v, half)])
            return f

        order = (0, 1, 2, 3, 6, 7, 8, 9, 4, 5, 10, 11)
        blocks = [block(blk, n) for n, blk in enumerate(order)]

        def finish():
            gtb = st["gtb"]
            for gi in range(2):
                for v in range(nvar):
                    cdma(lambda e, gi=gi, v=v: e.dma_start(out=gts_d[l, gi, v], in_=gtb[:, gi, v, :]),
                         reads=[("gtb", l, gi, v, 0), ("gtb", l, gi, v, 1)], writes=[("gts", l, gi, v)])
            checkpoint(f"mod{l}", {"mpp": (mpp2[:, l], [128, 4, 8, 2], F32), "gtb": (gtb[:], [128, 2, 2, D], F32)})
            A.release(st["m0"])

        return setup, blocks, finish

    def load_sublayer_consts(l, sub, need_c):
        for v in range(2 if need_c else 1):
            cdma(lambda e, v=v: e.dma_start(out=gt_t[:, v, :], in_=gts_d[l, sub, v]), reads=[("gts", l, sub, v)], writes=[("gt_t", v)])
        cdma(lambda e: e.dma_start(out=lng_t[:], in_=lnv_d[l, 2 * sub].partition_broadcast(128)), writes=["lng_t"])
        cdma(lambda e: e.dma_start(out=lnb_t[:], in_=lnv_d[l, 2 * sub + 1].partition_broadcast(128)), writes=["lnb_t"])

    def phase_P(l, src, tiles, sub, fillers=()):
        m0 = A.mark()
        fillers = list(fillers)
        xt = [A.alloc(f"xtP{i}", [128, D], F32) for i in range(3)]
        vi_sh, vi_sc = 2 * sub, 2 * sub + 1
        def xload(n):
            if n < len(tiles):
                t_ = tiles[n]
                sl_ = n % 3
                S.dma("sp", ("xt", sl_), lambda e: e.dma_start(out=xt[sl_][:], in_=src[t_ * 128:(t_ + 1) * 128, :]),
                      reads=[(src.tensor.name, t_)], writes=[("xtP", sl_)])
        xload(0)
        xload(1)
        for n, t in enumerate(tiles):
            sl = n % 3
            v = 1 if t < NCTX_T else 0
            xload(n + 2)
            pp = P2[n % 2]
            for k in range(8):
                S.op("pe", lambda e, k=k: e.transpose(out=pp[:, k * 128:(k + 1) * 128], in_=xt[sl][:, k * 128:(k + 1) * 128], identity=ident_f[:]),
                     reads=[("xtP", sl), "ident_f"], writes=[("pb", 2 * (n % 2) + k // 4)])
            for k in range(8):
                rk = [("pb", 2 * (n % 2) + k // 4), ("mpp", l, vi_sc, k // 4, v), ("mpp", l, vi_sh, k // 4, v)]
                S.op("act", lambda e, k=k: e.activation(out=hT[:, k, t * 128:(t + 1) * 128], in_=pp[:, k * 128:(k + 1) * 128], func=AF.Identity,
                                                        scale=mpp2[:, l, vi_sc, k, v:v + 1], bias=mpp2[:, l, vi_sh, k, v:v + 1]),
                     reads=rk, writes=[("hT", t, k)])
            if fillers:
                fillers.pop(0)()
        for f in fillers:
            f()
        A.release(m0)

    def hT_reads(a, b, k):
        return [("hT", t, k) for t in ttiles(a, b)]

    def phase_R(tiles, lhs_fn, lhs_reads_fn, nchunk, w_t, w_key, xsrc, first, last, dst, dst_row0, ytmp, fillers=()):
        m0 = A.mark()
        xt = [A.alloc(f"xtR{i}", [128, D], F32) for i in range(3)]
        NTMP = 4
        tmp = [A.alloc(f"tmpR{i}", [128, D], F32) for i in range(NTMP)]
        st = A.alloc("stR", [128, NTMP, 2, 6], F32)
        mv = A.alloc("mvR", [128, NTMP, 4], F32)
        rsrc = xsrc if first else ytmp

        def xload(n):
            if n < len(tiles):
                t_ = tiles[n]
                sl_ = n % 3
                S.dma("sp", ("xt", sl_), lambda e: e.dma_start(out=xt[sl_][:], in_=rsrc[t_ * 128:(t_ + 1) * 128, :]),
                      reads=[(rsrc.tensor.name, t_)], writes=[("xtR", sl_)])
        xload(0)
        xload(1)

        def stageA(n):
            t = tiles[n]
            sl = n % 3
            s2 = n % NTMP
            v = 1 if t < NCTX_T else 0
            xload(n + 2)
            p2i = n % 3
            pp = P2[p2i]
            for hf in range(2):
                for c in range(nchunk):
                    S.op("pe", lambda e, c=c, hf=hf: e.matmul(pp[:, hf * 512:(hf + 1) * 512], lhsT=lhs_fn(c, t), rhs=w_t[:, c, hf * 512:(hf + 1) * 512],
                                                              start=(c == 0), stop=(c == nchunk - 1)),
                         reads=lhs_reads_fn(c, t) + [w_key], writes=[("pb", 2 * p2i + hf)])
            pbk = [("pb", 2 * p2i), ("pb", 2 * p2i + 1)]
            S.op("dve", lambda e: e.tensor_tensor(out=tmp[s2][:], in0=pp[:], in1=gt_t[:, v, :], op=ALU.mult),
                 reads=pbk + [("gt_t", v)], writes=[("tmpR", s2)])
            S.op("dve", lambda e: e.scalar_tensor_tensor(out=tmp[s2][:], in0=xt[sl][:], scalar=(ALPHA if first else 1.0), in1=tmp[s2][:],
                                                         op0=ALU.mult, op1=ALU.add),
                 reads=[("xtR", sl), ("tmpR", s2)], writes=[("tmpR", s2)])
            if not last:
                S.dma("sp", ("yst", s2), lambda e: e.dma_start(out=ytmp[t * 128:(t + 1) * 128, :], in_=tmp[s2][:]),
                      reads=[("tmpR", s2)], writes=[(ytmp.tensor.name, t)])
                return
            for c in range(2):
                S.op("dve", lambda e, c=c: e.bn_stats(out=st[:, s2, c, :], in_=tmp[s2][:, c * 512:(c + 1) * 512]),
                     reads=[("tmpR", s2)], writes=[("stR", s2, c)])
            S.op("dve", lambda e: e.bn_aggr(out=mv[:, s2, 0:2], in_=st[:, s2, :, :].rearrange("p a b -> p (a b)")),
                 reads=[("stR", s2, 0), ("stR", s2, 1)], writes=[("mvR", s2)])
            S.op("act", lambda e: e.activation(out=mv[:, s2, 2:3], in_=mv[:, s2, 1:2], func=AF.Ln, bias=epsb[:, 0:1]),
                 reads=[("mvR", s2), "epsb"], writes=[("mvR", s2)])
            S.op("act", lambda e: e.activation(out=mv[:, s2, 2:3], in_=mv[:, s2, 2:3], func=AF.Exp, scale=-0.5),
                 reads=[("mvR", s2)], writes=[("mvR", s2)])

        def stageB(n):
            t = tiles[n]
            s2 = n % NTMP
            S.op("dve", lambda e: e.scalar_tensor_tensor(out=mv[:, s2, 3:4], in0=mv[:, s2, 0:1], scalar=-1.0, in1=mv[:, s2, 2:3], op0=ALU.mult, op1=ALU.mult),
                 reads=[("mvR", s2)], writes=[("mvR", s2)])
            S.op("act", lambda e: e.activation(out=tmp[s2][:], in_=tmp[s2][:], func=AF.Identity, scale=mv[:, s2, 2:3], bias=mv[:, s2, 3:4]),
                 reads=[("mvR", s2), ("tmpR", s2)], writes=[("tmpR", s2)])
            eng2 = "dve" if pool_free else "pool"
            S.op(eng2, lambda e: e.tensor_tensor(out=tmp[s2][:], in0=tmp[s2][:], in1=lng_t[:], op=ALU.mult),
                 reads=[("tmpR", s2), "lng_t"], writes=[("tmpR", s2)])
            S.op(eng2, lambda e: e.tensor_tensor(out=tmp[s2][:], in0=tmp[s2][:], in1=lnb_t[:], op=ALU.add),
                 reads=[("tmpR", s2), "lnb_t"], writes=[("tmpR", s2)])
            r0 = t * 128 - dst_row0
            if pool_free:
                S.dma("sp", ("yst", s2), lambda e: e.dma_start(out=dst[r0:r0 + 128, :], in_=tmp[s2][:]),
                      reads=[("tmpR", s2)], writes=[(dst.tensor.name, t)])
            else:
                S.dma("pool", ("ystp", s2), lambda e: e.dma_start(out=dst[r0:r0 + 128, :], in_=tmp[s2][:]),
                      reads=[("tmpR", s2)], writes=[(dst.tensor.name, t)])

        fillers = list(fillers)
        pool_free = bool(fillers)
        for n in range(len(tiles)):
            stageA(n)
            if last and n >= 1:
                stageB(n - 1)
            if fillers:
                fillers.pop(0)()
        if last:
            stageB(len(tiles) - 1)
        for f in fillers:
            f()
        A.release(m0)

    def run_attention(jobs, ET, fillers=()):
        steps = [(job, i) for job in jobs for i in range(len(job["keys"]))]
        fillers = list(fillers)
        fill_every = max(1, len(steps) // max(1, len(fillers))) if fillers else 0
        NSB = 4
        LA = NSB - 1

        def qk(si):
            job, ki = steps[si]
            j = job["keys"][ki]
            b = si % NSB
            for (c0, ncol, lhsT, rhs, view) in job["qk"](j):
                out = bank(b)[:, c0:c0 + ncol]
                if view is not None:
                    out = view(out)
                S.op("pe", lambda e, out=out, lhsT=lhsT, rhs=rhs: e.matmul(out, lhsT=lhsT, rhs=rhs, start=True, stop=True, skip_group_check=True),
                     reads=job["qk_reads"](j), writes=[("pb", b)])

        def ex(si):
            job, ki = steps[si]
            j = job["keys"][ki]
            b = si % NSB
            eb = si % len(ET)
            W = job["W"]
            o = ET[eb][:, 0:W]
            S.op("act", lambda e, o=o, b=b, W=W, job=job: e.activation(out=o, in_=bank(b)[:, 0:W], func=AF.Exp, scale=job["scale"]),
                 reads=[("pb", b)], writes=[("ET", eb)])
            mk = job["mask"](j) if job.get("mask") else None
            if mk is not None:
                S.op("dve", lambda e, o=o, mk=mk: e.tensor_tensor(out=o, in0=o, in1=mk, op=ALU.mult), reads=[("ET", eb), "masks"], writes=[("ET", eb)])

        def pv(si):
            job, ki = steps[si]
            j = job["keys"][ki]
            eb = si % len(ET)
            nkeys = len(job["keys"])
            for (ap_, key_, fib, ec0, rhs) in job["pv"](j):
                S.op("pe", lambda e, ap_=ap_, ec0=ec0, rhs=rhs, fib=fib: e.matmul(
                    ap_, lhsT=ET[eb][:, ec0:ec0 + 128], rhs=rhs, start=(ki == 0 and fib), stop=(ki == nkeys - 1), skip_group_check=True),
                    reads=[("ET", eb)] + job["pv_reads"](j), writes=[key_])
            if ki == nkeys - 1:
                return job["fin"]()
            return None

        pending = []
        for si in range(min(LA, len(steps))):
            qk(si)
        for si in range(len(steps)):
            if si + LA < len(steps):
                qk(si + LA)
            ex(si)
            pending = [(d - 1, f) for d, f in pending]
            due = [f for d, f in pending if d <= 0]
            pending = [(d, f) for d, f in pending if d > 0]
            for f in due:
                f()
            late = pv(si)
            if late is not None:
                if callable(late):
                    late = [(2, late)]
                pending.extend(late)
            if fillers and (si + 1) % fill_every == 0:
                fillers.pop(0)()
        for _, f in sorted(pending, key=lambda x: x[0]):
            f()
        for f in fillers:
            f()

    def rsqrt_small(dst, src, scale, eps_col, rkeys, wkey):
        S.op("act", lambda e: e.activation(out=dst, in_=src, func=AF.Ln, scale=scale, bias=epsb[:src.shape[0], eps_col:eps_col + 1]),
             reads=rkeys + ["epsb"], writes=[wkey])
        S.op("act", lambda e: e.activation(out=dst, in_=dst, func=AF.Exp, scale=-0.5), reads=[wkey], writes=[wkey])

    BLK_ALL = [(0, 512), (512, 1024), (1024, 1536), (1536, 2048), (2048, 2304)]
    BLK_LAT = [(256, 768), (768, 1280), (1280, 1792), (1792, 2304)]

    def layer0_attention(src, dst, mod0, mod1):
        setup0, blocks0, finish0 = mod0
        setup0()
        for f in blocks0[:4]:
            f()
        phase_P(0, src, list(range(NT)), 0, fillers=blocks0[4:])
        finish0()
        load_sublayer_consts(0, 0, True)
        checkpoint("P0", {"hT": (hT[:], [128, 8, NTOK], BF16)})
        mA = A.mark()
        qan = A.alloc("qan", [128, 2, NTOK], BF16)
        kvan = A.alloc("kvan", [128, NTOK], BF16)
        krT = A.alloc("krT", [96, NTOK], BF16)
        rope32 = A.alloc("rope32", [128, 2, NTOK], F32)
        wqb = A.alloc("wqb", [128, 8, 2, 2, 96], BF16)
        wkvb = A.alloc("wkvb", [128, 1024], BF16)
        qn_pp = A.alloc("qn_pp", [128, 2], F32)
        kvn_pp = A.alloc("kvn_pp", [128, 1], F32)
        esink = A.alloc("esink", [128, 8], F32)
        ET = [A.alloc(f"ET{i}", [128, 512], BF16) for i in range(4)]
        mB = A.mark()
        qsT = A.alloc("qsT", [128, 4, NTOK], BF16)
        ksT2 = [A.alloc(f"ksT{i}", [128, NTOK], BF16) for i in range(2)]
        Vs = A.alloc("Vs", [128, NT, 2, 65], BF16)
        mC = A.mark()
        w0 = A.alloc("w0", [128, 16, 8, 128], BF16)
        rope64 = A.alloc("rope64", [128, 2, NTOK], F32)
        t1 = [A.alloc(f"t1_{i}", [128, 512], F32) for i in range(2)]
        t2 = [A.alloc(f"t2_{i}", [128, 512], F32) for i in range(2)]
        qaf = A.alloc("qaf", [128, 2, 512], F32)
        qsq = A.alloc("qsq", [128, 2, 512], F32)
        rbc = A.alloc("rbc", [128, 512], F32)

        for r in range(2):
            cdma(lambda e, r=r: e.dma_start(out=rope32[:, r, :], in_=rope32_d[r]), writes=[("rope32", r)])
            cdma(lambda e, r=r: e.dma_start(out=rope64[:, r, :], in_=rope64_d[r]), writes=[("rope64", r)])
        cdma(lambda e: e.dma_start(out=qn_pp[:], in_=qnorm_d), writes=["qn_pp"])
        cdma(lambda e: e.dma_start(out=kvn_pp[:], in_=kvnorm_d), writes=["kvn_pp"])
        cdma(lambda e: e.dma_start(out=esink[:], in_=sink_d.partition_broadcast(128)), writes=["esink"])
        S.op("act", lambda e: e.activation(out=esink[:], in_=esink[:], func=AF.Exp), reads=["esink"], writes=["esink"])
        for g4 in range(4):
            S.dma("pool", ("w0", g4), lambda e, g4=g4: e.dma_start(out=w0[:, g4 * 4:(g4 + 1) * 4], in_=w0_d[:, g4 * 4:(g4 + 1) * 4], max_dma_last_dim=8192),
                  writes=[("w0", g4)])
        S.dma("pool", ("wq", 0), lambda e: e.dma_start(out=wqb[:], in_=wqb_d, max_dma_last_dim=8192), writes=["wqb"])
        S.dma("pool", ("wq", 1), lambda e: e.dma_start(out=wkvb[:], in_=wkvb_d, max_dma_last_dim=8192), writes=["wkvb"])
        S.op("dve", lambda e: e.memset(Vs[:, :, :, 64:65], 1.0), writes=["Vs1"])

        pbc = [0]

        def nextbank():
            pbc[0] = (pbc[0] + 1) % 8
            return pbc[0]

        def fm_proj(bk, chunk, a, b, m=128):
            for k in range(8):
                S.op("pe", lambda e, k=k: e.matmul(bank(bk)[0:m, 0:b - a], lhsT=w0[:, chunk, k, 0:m], rhs=hT[:, k, a:b], start=(k == 0), stop=(k == 7)),
                     reads=[("w0", chunk // 4)] + hT_reads(a, b, k), writes=[("pb", bk)])

        def rope_evac(bk_raw, bk_rot, tab, tabkey, p0, p1, a, b, dst_ap, wkey, i, dsts=None):
            n = b - a
            S.op("dve", lambda e: e.tensor_tensor(out=t1[i][p0:p1, 0:n], in0=bank(bk_raw)[p0:p1, 0:n], in1=tab[p0:p1, 0, a:b], op=ALU.mult),
                 reads=[("pb", bk_raw), (tabkey, 0)], writes=[("t1", i)])
            S.op("dve", lambda e: e.tensor_tensor(out=t2[i][p0:p1, 0:n], in0=bank(bk_rot)[p0:p1, 0:n], in1=tab[p0:p1, 1, a:b], op=ALU.mult),
                 reads=[("pb", bk_rot), (tabkey, 1)], writes=[("t2", i)])
            if dsts is None:
                dsts = [(p0, p1, dst_ap)]
            for (q0, q1, d_ap) in dsts:
                S.op("pool", lambda e, q0=q0, q1=q1, d_ap=d_ap: e.tensor_tensor(out=d_ap, in0=t1[i][q0:q1, 0:n], in1=t2[i][q0:q1, 0:n], op=ALU.add),
                     reads=[("t1", i), ("t2", i)], writes=[wkey])

        S.op("dve", lambda e: e.memset(ksT2[0][64:128, :], 0.0), writes=["ksTz"])
        S.op("dve", lambda e: e.memset(ksT2[1][0:64, :], 0.0), writes=["ksTz"])
        for bi, (a, b) in enumerate(BLK_ALL):
            n = b - a
            tl = list(ttiles(a, b))
            for c in range(4):
                br, bt = nextbank(), nextbank()
                fm_proj(br, 2 + c, a, b)
                fm_proj(bt, 6 + c, a, b)
                rope_evac(br, bt, rope64, "rope64", 0, 128, a, b, qsT[:, c, a:b], ("qsT", c, bi), (bi * 8 + c) % 2)
            br, bt = nextbank(), nextbank()
            fm_proj(br, 13, a, b)
            fm_proj(bt, 14, a, b)
            rope_evac(br, bt, rope64, "rope64", 0, 128, a, b, None, ("ksT", bi), 0,
                      dsts=[(0, 64, ksT2[0][0:64, a:b]), (64, 128, ksT2[1][64:128, a:b])])
            br, bt = nextbank(), nextbank()
            fm_proj(br, 11, a, b, m=96)
            fm_proj(bt, 12, a, b, m=96)
            rope_evac(br, bt, rope32, "rope32", 64, 96, a, b, krT[64:96, a:b], ("krT", bi), 1)
            bq = [nextbank(), nextbank()]
            for c in range(2):
                fm_proj(bq[c], c, a, b)
                S.op("act", lambda e, c=c, bq=bq: e.activation(out=qaf[:, c, 0:n], in_=bank(bq[c])[:, 0:n], func=AF.Copy), reads=[("pb", bq[c])], writes=[("qaf", c)])
                S.op("act", lambda e, c=c, bq=bq: e.activation(out=qsq[:, c, 0:n], in_=bank(bq[c])[:, 0:n], func=AF.Square), reads=[("pb", bq[c])], writes=[("qsq", c)])
            bs = nextbank()
            for c in range(2):
                S.op("pe", lambda e, c=c, bs=bs: e.matmul(bank(bs)[:, 0:n], lhsT=ones_f[:], rhs=qsq[:, c, 0:n], start=(c == 0), stop=(c == 1)),
                     reads=["ones_f", ("qsq", c)], writes=[("pb", bs)])
            rsqrt_small(rbc[:, 0:n], bank(bs)[:, 0:n], 1.0 / 256.0, 1, [("pb", bs)], "rbc")
            for c in range(2):
                S.op("dve", lambda e, c=c: e.scalar_tensor_tensor(out=qan[:, c, a:b], in0=qaf[:, c, 0:n], scalar=qn_pp[:, c:c + 1], in1=rbc[:, 0:n], op0=ALU.mult, op1=ALU.mult),
                     reads=[("qaf", c), "qn_pp", "rbc"], writes=[("qan", c, bi)])
            bq0 = nextbank()
            fm_proj(bq0, 10, a, b)
            S.op("act", lambda e, bq0=bq0: e.activation(out=qaf[:, 0, 0:n], in_=bank(bq0)[:, 0:n], func=AF.Copy), reads=[("pb", bq0)], writes=[("qaf", 0)])
            S.op("act", lambda e, bq0=bq0: e.activation(out=qsq[:, 0, 0:n], in_=bank(bq0)[:, 0:n], func=AF.Square), reads=[("pb", bq0)], writes=[("qsq", 0)])
            bs = nextbank()
            S.op("pe", lambda e, bs=bs: e.matmul(bank(bs)[:, 0:n], lhsT=ones_f[:], rhs=qsq[:, 0, 0:n], start=True, stop=True),
                 reads=["ones_f", ("qsq", 0)], writes=[("pb", bs)])
            rsqrt_small(rbc[:, 0:n], bank(bs)[:, 0:n], 1.0 / 128.0, 1, [("pb", bs)], "rbc")
            S.op("dve", lambda e: e.scalar_tensor_tensor(out=kvan[:, a:b], in0=qaf[:, 0, 0:n], scalar=kvn_pp[:, 0:1], in1=rbc[:, 0:n], op0=ALU.mult, op1=ALU.mult),
                 reads=[("qaf", 0), "kvn_pp", "rbc"], writes=[("kvan", bi)])
            bv = nextbank()
            for ti, t in enumerate(tl):
                for k in range(8):
                    S.op("pe", lambda e, k=k, t=t, ti=ti, bv=bv: e.matmul(bank(bv)[:, ti * 128:(ti + 1) * 128], lhsT=hT[:, k, t * 128:(t + 1) * 128], rhs=w0[:, 15, k, :],
                                                                         start=(k == 0), stop=(k == 7)),
                         reads=[("w0", 3), ("hT", t, k)], writes=[("pb", bv)])
            nt_ = len(tl)
            S.op("act", lambda e, bv=bv, t0=tl[0], nt_=nt_: e.activation(
                out=Vs[:, t0:t0 + nt_, :, 0:64], in_=bank(bv)[:, 0:nt_ * 128].rearrange("p (t g d) -> p t g d", g=2, d=64), func=AF.Copy),
                reads=[("pb", bv)], writes=[("Vs", bi)])
        checkpoint("proj0", {"qsT": (qsT[:], [128, 4, NTOK], BF16), "krT": (krT[64:96, :], [32, NTOK], BF16),
                             "qan": (qan[:], [128, 2, NTOK], BF16), "kvan": (kvan[:], [128, NTOK], BF16), "Vs": (Vs[:], [128, NT, 2, 65], BF16)})
        A.release(mC)

        otk = [A.alloc(f"otk{i}", [128, 256], BF16) for i in range(2)]
        den = [A.alloc(f"den{i}", [128, 8], F32) for i in range(2)]
        oT = hT
        _p3 = P2[3][:].bitcast(BF16)
        class _TPB:
            def __getitem__(self, idx):
                _, sl_ = idx
                a0 = sl_.start
                bk = a0 // 256
                off = bk * 1024 + (a0 - bk * 256)
                return _p3[:, off:off + (sl_.stop - sl_.start)]
        tpb = _TPB()
        jobs = []
        jn = [0]
        for g in range(2):
            for n_ in range(NT):
                if n_ < NCTX_T:
                    keys = [0, 1]
                else:
                    keys = [0, 1] + [j for j in (n_ - 1, n_, n_ + 1) if NCTX_T <= j < NT]
                aset = jn[0] % 2
                jn[0] += 1
                accb = 4 + aset

                def acc(qt, c, accb=accb):
                    return bank(accb)[:, qt * 65:(qt + 1) * 65], ("pb", accb), qt == 0

                def mask(j, n_=n_):
                    if n_ < NCTX_T:
                        return None
                    if j == n_ - 1 and j >= NCTX_T:
                        return masks[:, 0, :]
                    if j == n_ + 1:
                        return masks[:, 1, :]
                    return None

                def fin(g=g, n_=n_, accb=accb, aset=aset):
                    av = bank(accb)[:, 0:260].rearrange("p (h d) -> p h d", d=65)
                    akeys = [("pb", accb)]
                    S.op("dve", lambda e: e.tensor_tensor(out=den[aset][:, 0:4], in0=av[:, :, 64], in1=esink[:, g * 4:g * 4 + 4], op=ALU.add),
                         reads=akeys + ["esink"], writes=[("den", aset)])
                    S.op("dve", lambda e: e.reciprocal(out=den[aset][:, 4:8], in_=den[aset][:, 0:4]), reads=[("den", aset)], writes=[("den", aset)])
                    for c in range(4):
                        S.op("dve", lambda e, c=c: e.tensor_scalar(out=otk[aset][:, c * 64:(c + 1) * 64], in0=av[:, c, 0:64], scalar1=den[aset][:, 4 + c:5 + c], scalar2=None, op0=ALU.mult),
                             reads=[("pb", accb), ("den", aset)], writes=[("otk", aset, c // 2)])
                    def late():
                        for j2 in range(2):
                            slot = (aset * 2 + j2)
                            S.op("pe", lambda e, j2=j2, slot=slot: e.transpose(out=tpb[:, slot * 128:(slot + 1) * 128], in_=otk[aset][:, j2 * 128:(j2 + 1) * 128], identity=ident_b[:]),
                                 reads=[("otk", aset, j2), "ident_b"], writes=[("pb", 6 + aset)])
                        S.op("dve", lambda e: e.tensor_copy(out=oT[:, 4 + 2 * g:6 + 2 * g, n_ * 128:(n_ + 1) * 128],
                                                            in_=tpb[:, aset * 256:(aset + 1) * 256].rearrange("p (j n) -> p j n", j=2)),
                             reads=[("pb", 6 + aset)], writes=[("hT", n_, 4 + 2 * g), ("hT", n_, 5 + 2 * g)])
                    return late

                def qk_(j, g=g, n_=n_):
                    return [(0, 512, ksT2[g][:, j * 128:(j + 1) * 128], qsT[:, :, n_ * 128:(n_ + 1) * 128],
                             lambda ap: ap.rearrange("p (h n) -> p h n", h=4))]

                def pv_(j, g=g, accb=accb):
                    return [(bank(accb)[:, c * 65:(c + 1) * 65], ("pb", accb), c == 0, c * 128, Vs[:, j, g, :]) for c in range(4)]

                jobs.append(dict(
                    keys=keys, W=512, scale=64 ** -0.5,
                    qk=qk_, qk_reads=lambda j, n_=n_: [("ksT", j // 4), "ksTz"] + [("qsT", c, n_ // 4) for c in range(4)],
                    pv=pv_, pv_reads=lambda j: [("Vs", j // 4), "Vs1"],
                    mask=mask, fin=fin))
        run_attention(jobs, ET)
        checkpoint("swa0", {"oT": (hT[:], [128, 8, NTOK], BF16)})
        A.release(mB)

        Vm = A.alloc("Vm", [128, NT, 8, 65], BF16)
        qm = [A.alloc(f"qm{i}", [96, NTOK], BF16) for i in range(2)]
        km = [A.alloc(f"km{i}", [96, NTOK], BF16) for i in range(2)]
        ostg = [A.alloc(f"ostg{i}", [128, NT, 128], BF16) for i in range(2)]
        t1 = [A.alloc(f"t1m{i}", [128, 512], F32) for i in range(2)]
        t2 = [A.alloc(f"t2m{i}", [128, 512], F32) for i in range(2)]
        rden = [A.alloc(f"rden{i}", [128, 4], F32) for i in range(2)]
        S.op("dve", lambda e: e.memset(Vm[:, :, :, 64:65], 1.0), writes=["Vm1"])
        for t in range(NT):
            bv = 4 + t % 2
            S.op("pe", lambda e, t=t, bv=bv: e.matmul(bank(bv), lhsT=kvan[:, t * 128:(t + 1) * 128], rhs=wkvb[:, 512:1024], start=True, stop=True),
                 reads=[("kvan", t // 4), "wkvb"], writes=[("pb", bv)])
            S.op("dve", lambda e, t=t, bv=bv: e.tensor_copy(out=Vm[:, t, :, 0:64], in_=bank(bv).rearrange("p (h d) -> p h d", d=64)),
                 reads=[("pb", bv)], writes=[("Vm", t)])
        jobs = []
        jn = [0]
        for h in range(8):
            hb = h % 2
            def head_proj(h=h, hb=hb):
                for bi, (a, b) in enumerate(BLK_ALL):
                    n = b - a
                    br, bt, bk_ = 4, 5, 4
                    for kc in range(2):
                        S.op("pe", lambda e, kc=kc: e.matmul(bank(br)[0:96, 0:n], lhsT=wqb[:, h, 0, kc, :], rhs=qan[:, kc, a:b], start=(kc == 0), stop=(kc == 1)),
                             reads=["wqb", ("qan", kc, bi)], writes=[("pb", br)])
                    for kc in range(2):
                        S.op("pe", lambda e, kc=kc: e.matmul(bank(bt)[0:96, 0:n], lhsT=wqb[:, h, 1, kc, :], rhs=qan[:, kc, a:b], start=(kc == 0), stop=(kc == 1)),
                             reads=["wqb", ("qan", kc, bi)], writes=[("pb", bt)])
                    S.op("act", lambda e: e.activation(out=qm[hb][0:64, a:b], in_=bank(br)[0:64, 0:n], func=AF.Copy), reads=[("pb", br)], writes=[("qm", hb, bi, 0)])
                    i = bi % 2
                    S.op("dve", lambda e: e.tensor_tensor(out=t1[i][64:96, 0:n], in0=bank(br)[64:96, 0:n], in1=rope32[64:96, 0, a:b], op=ALU.mult),
                         reads=[("pb", br), ("rope32", 0)], writes=[("t1", i)])
                    S.op("dve", lambda e: e.tensor_tensor(out=t2[i][64:96, 0:n], in0=bank(bt)[64:96, 0:n], in1=rope32[64:96, 1, a:b], op=ALU.mult),
                         reads=[("pb", bt), ("rope32", 1)], writes=[("t2", i)])
                    S.op("pool", lambda e: e.tensor_tensor(out=qm[hb][64:96, a:b], in0=t1[i][64:96, 0:n], in1=t2[i][64:96, 0:n], op=ALU.add),
                         reads=[("t1", i), ("t2", i)], writes=[("qm", hb, bi, 1)])
                    S.op("pe", lambda e: e.matmul(bank(bk_)[0:64, 0:n], lhsT=wkvb[:, h * 64:(h + 1) * 64], rhs=kvan[:, a:b], start=True, stop=True),
                         reads=["wkvb", ("kvan", bi)], writes=[("pb", bk_)])
                    S.op("act", lambda e: e.activation(out=km[hb][0:64, a:b], in_=bank(bk_)[0:64, 0:n], func=AF.Copy), reads=[("pb", bk_)], writes=[("km", hb, bi, 0)])
                    S.op("pool", lambda e: e.tensor_copy(out=km[hb][64:96, a:b], in_=krT[64:96, a:b]), reads=[("krT", bi)], writes=[("km", hb, bi, 1)])

            qblocks = [(0, 256, [0, 1])] + [(a, b, list(range(NT))) for (a, b) in BLK_LAT]
            for qi, (qa_, qb_, keys) in enumerate(qblocks):
                nq = (qb_ - qa_) // 128
                aset = jn[0] % 2
                jn[0] += 1
                accb = 4 + aset if False else (6 + aset)

                def acc(qt, c, accb=accb):
                    return bank(accb)[:, qt * 65:(qt + 1) * 65], ("pb", accb), qt == 0

                def fin(h=h, hb=hb, qa_=qa_, nq=nq, accb=accb, aset=aset):
                    av = bank(accb)[:, 0:nq * 65].rearrange("p (h d) -> p h d", d=65)
                    akeys = [("pb", accb)]
                    S.op("dve", lambda e: e.reciprocal(out=rden[aset][:, 0:nq], in_=av[:, :, 64]), reads=akeys, writes=[("rden", aset)])
                    for qt in range(nq):
                        t = qa_ // 128 + qt
                        S.op("dve", lambda e, qt=qt, t=t: e.tensor_scalar(out=ostg[(h // 2) % 2][:, t, hb * 64:(hb + 1) * 64], in0=av[:, qt, 0:64], scalar1=rden[aset][:, qt:qt + 1],
                                                                      scalar2=None, op0=ALU.mult),
                             reads=[("pb", accb), ("rden", aset)], writes=[("ostg", (h // 2) % 2, t, hb)])
                    if hb == 1:
                        def late():
                            tb = P2[2][:].bitcast(BF16)[:, 1024:2048]
                            t0 = qa_ // 128
                            for qt in range(nq):
                                t = t0 + qt
                                S.op("pe", lambda e, t=t, qt=qt: e.transpose(out=tb[:, qt * 128:(qt + 1) * 128], in_=ostg[(h // 2) % 2][:, t, :], identity=ident_b[:]),
                                     reads=[("ostg", (h // 2) % 2, t, 0), ("ostg", (h // 2) % 2, t, 1), "ident_b"], writes=[("pb", 5)])
                            S.op("dve", lambda e: e.tensor_copy(out=oT[:, h // 2, t0 * 128:(t0 + nq) * 128], in_=tb[:, 0:nq * 128]),
                                 reads=[("pb", 5)], writes=[("hT", t0 + qt, h // 2) for qt in range(nq)])
                        return late
                    return None

                def qk_(j, hb=hb, qa_=qa_, qb_=qb_):
                    return [(0, qb_ - qa_, km[hb][0:96, j * 128:(j + 1) * 128], qm[hb][0:96, qa_:qb_], None)]

                def pv_(j, h=h, accb=accb, nq=nq):
                    return [(bank(accb)[:, qt * 65:(qt + 1) * 65], ("pb", accb), qt == 0, qt * 128, Vm[:, j, h, :]) for qt in range(nq)]

                jobs.append(dict(
                    pre=(head_proj if qi == 0 else None),
                    keys=keys, W=qb_ - qa_, scale=96 ** -0.5,
                    qk=qk_, qk_reads=lambda j, hb=hb: [("km", hb, j // 4, 0), ("km", hb, j // 4, 1)] + [("qm", hb, bi, r) for bi in range(5) for r in range(2)],
                    pv=pv_, pv_reads=lambda j: [("Vm", j), "Vm1"],
                    mask=None, fin=fin))
        hj = 0
        while hj < len(jobs):
            jobs[hj]["pre"]()
            run_attention(jobs[hj:hj + 5], ET)
            hj += 5
        checkpoint("mla0", {"oT": (hT[:], [128, 8, NTOK], BF16)})
        A.release(mA)

        mR = A.mark()
        wo = A.alloc("wo", [128, 8, D], BF16)
        for hf in range(2):
            S.dma("pool", ("wo", hf), lambda e, hf=hf: e.dma_start(out=wo[:, hf * 4:(hf + 1) * 4, :], in_=wout0_d[:, hf * 4:(hf + 1) * 4, :], max_dma_last_dim=8192), writes=["wo"])
        fl = []
        if mod1 is not None:
            setup1, fl, finish1 = mod1
            setup1()
        phase_R(list(range(NT)), lambda c, t: oT[:, c, t * 128:(t + 1) * 128], lambda c, t: [("hT", t, c)], 8, wo, "wo",
                src, True, True, dst, 0, None, fillers=fl)
        if mod1 is not None:
            finish1()
        A.release(mR)

    def ffn_sublayer(l, src, dst, dst_row0, tiles):
        need_c = tiles[0] < NCTX_T
        load_sublayer_consts(l, 1, need_c)
        phase_P(l, src, tiles, 1)
        tok0, tok1 = tiles[0] * 128, (tiles[-1] + 1) * 128
        blocks = [(a, min(a + 512, tok1)) for a in range(tok0, tok1, 512)]
        m0 = A.mark()
        convp = A.alloc("convp", [128, NFC, 4], F32)
        wsl = [A.alloc(f"wg{i}", [128, 8, 512], BF16) for i in range(3)]
        wdn = A.alloc("wdn", [128, FCP, D], BF16)
        hid = A.alloc("hid", [128, FCP, NTOK], BF16)
        cdma(lambda e: e.dma_start(out=convp[:], in_=convp_d[l]), writes=["convp"])
        def gcol(g):
            return g + 1 if g < 256 else g + 3
        for ps in range(NPASS):
            m1 = A.mark()
            G = [A.alloc(f"G{i}", [128, NTOK + 4], F32) for i in range(2)]
            T1 = [A.alloc(f"T1_{i}", [128, NTOK + 4], F32) for i in range(2)]
            for i in range(2):
                for cpad in (0, 257, 2307):
                    w = 2 if cpad == 257 else 1
                    S.op("dve", lambda e, i=i, cpad=cpad, w=w: e.memset(G[i][:, cpad:cpad + w], 0.0), writes=[("Gpad", i)])
            for ci in range(FCP):
                fc = ps * FCP + ci
                blk11 = fc // 2
                sub = fc % 2
                sl = blk11 % 3
                if sub == 0 or ci == 0:
                    S.dma("pool", ("wg", sl), lambda e, blk11=blk11, sl=sl: e.dma_start(out=wsl[sl][:], in_=wgu_d[l, blk11], max_dma_last_dim=8192),
                          writes=[("wg", sl)])
                if ci == 2:
                    S.dma("pool", ("wdn", 0), lambda e, ps=ps: e.dma_start(out=wdn[:], in_=wdown_d[l, ps], max_dma_last_dim=8192), writes=["wdn"])
                wv = wsl[sl][:].rearrange("p k (g f) -> p k g f", g=2)
                gi = ci % 2
                c0, c1 = gcol(tok0), gcol(tok1 - 1) + 1
                for bi, (a, b) in enumerate(blocks):
                    n = b - a
                    st_ = (ci * len(blocks) + bi) % 4
                    bg, bu = 2 * st_, 2 * st_ + 1
                    for k in range(8):
                        S.op("pe", lambda e, k=k, bg=bg, wv=wv: e.matmul(bank(bg)[:, 0:n], lhsT=wv[:, k, 0, sub * 128:(sub + 1) * 128], rhs=hT[:, k, a:b], start=(k == 0), stop=(k == 7)),
                             reads=[("wg", sl)] + hT_reads(a, b, k), writes=[("pb", bg)])
                    for k in range(8):
                        S.op("pe", lambda e, k=k, bu=bu, wv=wv: e.matmul(bank(bu)[:, 0:n], lhsT=wv[:, k, 1, sub * 128:(sub + 1) * 128], rhs=hT[:, k, a:b], start=(k == 0), stop=(k == 7)),
                             reads=[("wg", sl)] + hT_reads(a, b, k), writes=[("pb", bu)])
                    segs = []
                    if a < 256 < b:
                        segs = [(a, 256), (256, b)]
                    else:
                        segs = [(a, b)]
                    for (sa, sb) in segs:
                        S.op("act", lambda e, sa=sa, sb=sb, bg=bg, gi=gi: e.activation(out=G[gi][:, gcol(sa):gcol(sa) + sb - sa], in_=bank(bg)[:, sa - a:sb - a], func=AF.Copy),
                             reads=[("pb", bg)], writes=[("G", gi, bi)])
                    S.op("dve", lambda e, bu=bu, ci=ci: e.tensor_copy(out=hid[:, ci, a:b], in_=bank(bu)[:, 0:n]), reads=[("pb", bu)], writes=[("hid", ci, bi)])
                gk = [("G", gi, bi) for bi in range(len(blocks))] + [("Gpad", gi)]
                S.op("act", lambda e, gi=gi, fc=fc: e.activation(out=T1[gi][:, c0:c1], in_=G[gi][:, c0:c1], func=AF.Identity, scale=convp[:, fc, 1:2], bias=convp[:, fc, 3:4]),
                     reads=gk + ["convp"], writes=[("T1", gi)])
                S.op("dve", lambda e, gi=gi, fc=fc: e.scalar_tensor_tensor(out=T1[gi][:, c0:c1], in0=G[gi][:, c0 - 1:c1 - 1], scalar=convp[:, fc, 0:1], in1=T1[gi][:, c0:c1], op0=ALU.mult, op1=ALU.add),
                     reads=gk + ["convp", ("T1", gi)], writes=[("T1", gi)])
                S.op("dve", lambda e, gi=gi, fc=fc: e.scalar_tensor_tensor(out=T1[gi][:, c0:c1], in0=G[gi][:, c0 + 1:c1 + 1], scalar=convp[:, fc, 2:3], in1=T1[gi][:, c0:c1], op0=ALU.mult, op1=ALU.add),
                     reads=gk + ["convp", ("T1", gi)], writes=[("T1", gi)])
                S.op("act", lambda e, gi=gi: e.activation(out=T1[gi][:, c0:c1], in_=T1[gi][:, c0:c1], func=AF.Silu), reads=[("T1", gi)], writes=[("T1", gi)])
                segs = [(tok0, 256), (256, tok1)] if tok0 < 256 else [(tok0, tok1)]
                for (sa, sb) in segs:
                    S.op("dve", lambda e, sa=sa, sb=sb, gi=gi, ci=ci: e.tensor_tensor(out=hid[:, ci, sa:sb], in0=hid[:, ci, sa:sb], in1=T1[gi][:, gcol(sa):gcol(sa) + sb - sa], op=ALU.mult),
                         reads=[("T1", gi)] + [("hid", ci, bi) for bi in range(len(blocks))], writes=[("hid", ci, bi) for bi in range(len(blocks))])
            A.release(m1)
            phase_R(tiles, lambda c, t: hid[:, c, t * 128:(t + 1) * 128],
                    lambda c, t: [("hid", c, bi) for bi in range(len(blocks)) if blocks[bi][0] <= t * 128 < blocks[bi][1]],
                    FCP, wdn, "wdn", src, ps == 0, ps == NPASS - 1, dst, dst_row0, ybuf)
        A.release(m0)

    def layer1_attention(src, dst):
        load_sublayer_consts(1, 0, False)
        lambda_init = 0.8 - 0.6 * math.exp(-0.3 * 1)
        mA = A.mark()
        oT = A.alloc("oT1", [128, 8, 2048], BF16)
        rope64 = A.alloc("rope64b", [128, 2, NTOK], F32)
        ET = [A.alloc(f"ETb{i}", [128, 512], BF16) for i in range(4)]
        t1 = [A.alloc(f"t1b{i}", [128, 256], F32) for i in range(2)]
        t2 = [A.alloc(f"t2b{i}", [128, 256], F32) for i in range(2)]
        wo = A.alloc("wo1", [128, 8, D], BF16)
        gsub = A.alloc("gsub", [128, 128], F32)
        lamt = A.alloc("lamt", [128, 4, 64], F32)
        lams = A.alloc("lams", [128, 8], F32)
        accs = [A.alloc(f"accs{i}", [128, 4, 129], F32) for i in range(2)]
        rr = [A.alloc(f"rr{i}", [128, 8], F32) for i in range(2)]
        uu = [A.alloc(f"uu{i}", [128, 128], F32) for i in range(2)]
        avv = [A.alloc(f"avv{i}", [128, 128], F32) for i in range(2)]
        junk = A.alloc("junk", [128, 128], BF16)
        otk = [A.alloc(f"otkb{i}", [128, 128], BF16) for i in range(4)]
        mH = A.mark()
        V1 = [A.alloc(f"V1_{i}", [128, NT, 130], BF16) for i in range(2)]
        qT = [A.alloc(f"qT1_{i}", [128, 2048], BF16) for i in range(2)]
        kT0 = [A.alloc(f"kT0_{i}", [128, NTOK], BF16) for i in range(2)]
        kT1 = [A.alloc(f"kT1_{i}", [128, NTOK], BF16) for i in range(2)]
        w1 = [A.alloc(f"w1h_{i}", [128, 5, 8, 128], BF16) for i in range(2)]

        for r in range(2):
            cdma(lambda e, r=r: e.dma_start(out=rope64[:, r, :], in_=rope64_d[r]), writes=[("rope64", r)])
        cdma(lambda e: e.dma_start(out=gsub[:], in_=subg_d.partition_broadcast(128)), writes=["gsub"])
        S.op("dve", lambda e: e.tensor_scalar(out=gsub[:], in0=gsub[:], scalar1=(1.0 - lambda_init), scalar2=None, op0=ALU.mult), reads=["gsub"], writes=["gsub"])
        for i in range(4):
            cdma(lambda e, i=i: e.dma_start(out=lamt[:, i, :], in_=lamv_d[i].partition_broadcast(128)), writes=[("lamt", i)])
        for i in range(2):
            S.op("dve", lambda e, i=i: e.tensor_tensor(out=lamt[:, 2 * i, :], in0=lamt[:, 2 * i, :], in1=lamt[:, 2 * i + 1, :], op=ALU.mult),
                 reads=[("lamt", 2 * i), ("lamt", 2 * i + 1)], writes=[("lamt", 2 * i)])
            S.op("dve", lambda e, i=i: e.reduce_sum(out=lams[:, i:i + 1], in_=lamt[:, 2 * i, :], axis=mybir.AxisListType.X), reads=[("lamt", 2 * i)], writes=["lams"])
        S.op("act", lambda e: e.activation(out=lams[:, 2:4], in_=lams[:, 0:2], func=AF.Exp), reads=["lams"], writes=["lams"])
        S.op("dve", lambda e: e.scalar_tensor_tensor(out=lams[:, 4:5], in0=lams[:, 3:4], scalar=-lambda_init, in1=lams[:, 2:3], op0=ALU.add, op1=ALU.subtract),
             reads=["lams"], writes=["lams"])
        for i in range(2):
            S.op("dve", lambda e, i=i: e.memset(V1[i][:, :, 128:129], 1.0), writes=[("V11", i)])
            S.op("dve", lambda e, i=i: e.memset(kT0[i][64:128, :], 0.0), writes=[("kTz", i)])
            S.op("dve", lambda e, i=i: e.memset(kT1[i][0:64, :], 0.0), writes=[("kTz", i)])
        for hf in range(2):
            S.dma("pool", ("wo", hf), lambda e, hf=hf: e.dma_start(out=wo[:, hf * 4:(hf + 1) * 4, :], in_=wout1_d[:, hf * 4:(hf + 1) * 4, :], max_dma_last_dim=8192), writes=["wo"])

        tpb = P2[3][:].bitcast(BF16)[:, 0:1024]
        PB = 7
        jn = [0]
        uc = [0]

        def proj_units(h):
            hb = h % 2
            units = []

            def load_w():
                S.dma("pool", ("w1", hb), lambda e: e.dma_start(out=w1[hb][:], in_=w1h_d[h], max_dma_last_dim=8192), writes=[("w1", hb)])

            def qk_unit(chunk, a, b, dsts):
                def f():
                    n = b - a
                    i = uc[0] % 2
                    uc[0] += 1
                    for r in range(2):
                        for k in range(8):
                            S.op("pe", lambda e, k=k, r=r: e.matmul(bank(PB)[:, r * 256:r * 256 + n], lhsT=w1[hb][:, chunk + r, k, :], rhs=hT[:, k, a:b], start=(k == 0), stop=(k == 7)),
                                 reads=[("w1", hb)] + hT_reads(a, b, k), writes=[("pb", PB)])
                    S.op("dve", lambda e: e.tensor_tensor(out=t1[i][:, 0:n], in0=bank(PB)[:, 0:n], in1=rope64[:, 0, a:b], op=ALU.mult),
                         reads=[("pb", PB), ("rope64", 0)], writes=[("t1", i)])
                    S.op("dve", lambda e: e.tensor_tensor(out=t2[i][:, 0:n], in0=bank(PB)[:, 256:256 + n], in1=rope64[:, 1, a:b], op=ALU.mult),
                         reads=[("pb", PB), ("rope64", 1)], writes=[("t2", i)])
                    for (p0, p1, dst_ap, wkey) in dsts:
                        S.op("pool", lambda e, p0=p0, p1=p1, dst_ap=dst_ap: e.tensor_tensor(out=dst_ap, in0=t1[i][p0:p1, 0:n], in1=t2[i][p0:p1, 0:n], op=ALU.add),
                             reads=[("t1", i), ("t2", i)], writes=[wkey])
                return f

            def v_unit(t0):
                def f():
                    tl = list(range(t0, min(t0 + 2, NT)))
                    for ti, t in enumerate(tl):
                        for k in range(8):
                            S.op("pe", lambda e, k=k, t=t, ti=ti: e.matmul(bank(PB)[:, ti * 128:(ti + 1) * 128], lhsT=hT[:, k, t * 128:(t + 1) * 128], rhs=w1[hb][:, 4, k, :],
                                                                          start=(k == 0), stop=(k == 7)),
                                 reads=[("w1", hb), ("hT", t, k)], writes=[("pb", PB)])
                    nt_ = len(tl)
                    S.op("dve", lambda e: e.tensor_copy(out=V1[hb][:, t0:t0 + nt_, 0:128], in_=bank(PB)[:, 0:nt_ * 128].rearrange("p (t d) -> p t d", d=128)),
                         reads=[("pb", PB)], writes=[("V1", hb, t0 // 2)])
                return f

            for u in range(9):
                a = u * 256
                units.append(qk_unit(2, a, a + 256, [(0, 64, kT0[hb][0:64, a:a + 256], ("kT", hb, 0, u)), (64, 128, kT1[hb][64:128, a:a + 256], ("kT", hb, 1, u))]))
            for u in range(9):
                units.append(v_unit(2 * u))
            for u in range(8):
                a = 256 + u * 256
                units.append(qk_unit(0, a, a + 256, [(0, 128, qT[hb][:, u * 256:(u + 1) * 256], ("qT", hb, u))]))
            return load_w, units

        def head_jobs(h):
            hb = h % 2
            jobs = []
            for qb in range(8):
                qa = qb * 256
                aset = jn[0] % 2
                jn[0] += 1

                def accinfo(qt, c):
                    idx = qt * 2 + c
                    bk = 4 + idx // 3
                    sl_ = idx % 3
                    return bank(bk)[:, sl_ * 129:(sl_ + 1) * 129], ("pb", bk), sl_ == 0

                def qk_(j, qa=qa):
                    return [(0, 256, kT0[hb][:, j * 128:(j + 1) * 128], qT[hb][:, qa:qa + 256], None),
                            (256, 256, kT1[hb][:, j * 128:(j + 1) * 128], qT[hb][:, qa:qa + 256], None)]

                def pv_(j):
                    ops = []
                    for qt in range(2):
                        for c in range(2):
                            ap_, key_, fib = accinfo(qt, c)
                            ops.append((ap_, key_, fib, c * 256 + qt * 128, V1[hb][:, j, 0:129]))
                    return ops

                def fin(qa=qa, aset=aset):
                    ac = accs[aset]
                    S.op("act", lambda e: e.activation(out=ac[:, 0:3, :], in_=bank(4)[:, 0:3 * 129].rearrange("p (a d) -> p a d", d=129), func=AF.Copy),
                         reads=[("pb", 4)], writes=[("accs", aset, 0)])
                    S.op("act", lambda e: e.activation(out=ac[:, 3, :], in_=bank(5)[:, 0:129], func=AF.Copy), reads=[("pb", 5)], writes=[("accs", aset, 1)])
                    for qt in range(2):
                        i = qt
                        ak = [("accs", aset, 0), ("accs", aset, 1)]
                        S.op("dve", lambda e, qt=qt, i=i: e.reciprocal(out=rr[i][:, 0:2], in_=ac[:, qt * 2:qt * 2 + 2, 128]), reads=ak, writes=[("rr", i)])
                        S.op("dve", lambda e, i=i: e.tensor_tensor(out=rr[i][:, 2:3], in0=rr[i][:, 1:2], in1=lams[:, 4:5], op=ALU.mult), reads=[("rr", i), "lams"], writes=[("rr", i)])
                        S.op("dve", lambda e, qt=qt, i=i: e.tensor_scalar(out=uu[i][:], in0=ac[:, qt * 2 + 1, 0:128], scalar1=rr[i][:, 2:3], scalar2=None, op0=ALU.mult),
                             reads=ak + [("rr", i)], writes=[("uu", i)])
                        S.op("dve", lambda e, qt=qt, i=i: e.scalar_tensor_tensor(out=avv[i][:], in0=ac[:, qt * 2, 0:128], scalar=rr[i][:, 0:1], in1=uu[i][:], op0=ALU.mult, op1=ALU.add),
                             reads=ak + [("rr", i), ("uu", i)], writes=[("avv", i)])

                    def late0():
                        for qt in range(2):
                            i = qt
                            S.op("act", lambda e, i=i: e.activation(out=junk[:], in_=avv[i][:], func=AF.Square, accum_out=rr[i][:, 3:4]), reads=[("avv", i)], writes=["junk", ("rr", i)])
                            rsqrt_small(rr[i][:, 4:5], rr[i][:, 3:4], 1.0 / 128.0, 1, [("rr", i)], ("rr", i))

                    def late1():
                        for qt in range(2):
                            i = qt
                            oi = aset * 2 + qt
                            S.op("dve", lambda e, i=i, oi=oi: e.scalar_tensor_tensor(out=otk[oi][:], in0=avv[i][:], scalar=rr[i][:, 4:5], in1=gsub[:], op0=ALU.mult, op1=ALU.mult),
                                 reads=[("avv", i), ("rr", i), "gsub"], writes=[("otk", oi)])

                    def late2():
                        for qt in range(2):
                            oi = aset * 2 + qt
                            S.op("pe", lambda e, oi=oi: e.transpose(out=tpb[:, oi * 128:(oi + 1) * 128], in_=otk[oi][:], identity=ident_b[:]),
                                 reads=[("otk", oi), "ident_b"], writes=[("pb", 6)])
                        S.op("dve", lambda e: e.tensor_copy(out=oT[:, h, qa:qa + 256], in_=tpb[:, aset * 256:aset * 256 + 256]),
                             reads=[("pb", 6)], writes=[("oT", h, qa // 128), ("oT", h, qa // 128 + 1)])
                    return [(7, late0), (11, late1), (14, late2)]

                jobs.append(dict(
                    keys=list(range(NT)), W=512, scale=64 ** -0.5,
                    qk=qk_, qk_reads=lambda j, qb=qb: [("kT", hb, 0, j // 2), ("kT", hb, 1, j // 2), ("kTz", hb), ("qT", hb, qb)],
                    pv=pv_, pv_reads=lambda j: [("V1", hb, j // 2), ("V11", hb)],
                    mask=None, fin=fin))
            return jobs

        lw, units = proj_units(0)
        lw()
        ku, vu, qu = units[0:9], units[9:18], units[18:26]
        sched = []
        for t in range(NT):
            fs = []
            if t % 2 == 1:
                fs.append(ku[(t - 1) // 2])
                fs.append(vu[(t - 1) // 2])
                if t >= 3:
                    fs.append(qu[(t - 3) // 2])
            sched.append(lambda fs=fs: [f() for f in fs])
        phase_P(1, src, list(range(NT)), 0, fillers=sched)
        for h in range(8):
            fl = []
            if h + 1 < 8:
                lw, fl = proj_units(h + 1)
                lw()
            run_attention(head_jobs(h), ET, fl)
        A.release(mH)

        lat = list(range(NCTX_T, NT))
        phase_R(lat, lambda c, t: oT[:, c, (t - NCTX_T) * 128:(t - NCTX_T + 1) * 128], lambda c, t: [("oT", c, t - NCTX_T)], 8, wo, "wo",
                src, True, True, dst, 0, None)
        A.release(mA)

    cur_src = x_all
    dbg_x = None
    if stop_after is not None:
        dbg_x = nc.dram_tensor("dbg_x", [NTOK, D], F32, kind="ExternalOutput").ap()
    try:
      for l in layers:
          if l == 0:
              mod0 = modulation_units(0, True, 6, 7)
              mod1 = modulation_units(1, False, 6, 7) if 1 in layers else None
              if stop_after == "attn0":
                  layer0_attention(cur_src, dbg_x, mod0, mod1)
                  break
              layer0_attention(cur_src, xres[0], mod0, mod1)
              if stop_after == "ffn0":
                  ffn_sublayer(0, xres[0], dbg_x, 0, list(range(NT)))
                  break
              ffn_sublayer(0, xres[0], xres[1], 0, list(range(NT)))
              cur_src = xres[1]
          else:
              if 0 not in layers:
                  su, bl, fi = modulation_units(1, False, 6, 7)
                  su()
                  for f in bl:
                      f()
                  fi()
              if stop_after == "attn1":
                  layer1_attention(cur_src, dbg_x)
                  break
              layer1_attention(cur_src, xres[0])
              ffn_sublayer(1, xres[0], out_d, 256, list(range(NCTX_T, NT)))
    except _Stop:
        pass
    S.barrier()
    nsem = S.emit()
    info = dict(nsem=nsem, peak=A.peak - A.base, cnt=dict(S.cnt))
    return nc, info


def _rope_tables(dim):
    rows = 2048 // 64
    row = np.repeat(np.arange(rows), 64).astype(np.float32)
    col = np.tile(np.arange(64), rows).astype(np.float32)
    q = dim // 4
    freqs = (np.float32(10000.0) ** (-np.arange(q, dtype=np.float32) / np.float32(q))).astype(np.float32)
    ar = row[:, None] * freqs
    ac = col[:, None] * freqs
    cos = np.concatenate([np.cos(ar), np.cos(ar), np.cos(ac), np.cos(ac)], -1).astype(np.float32)
    sin = np.concatenate([np.sin(ar), np.sin(ar), np.sin(ac), np.sin(ac)], -1).astype(np.float32)
    sign = np.concatenate([-np.ones(q), np.ones(q), -np.ones(q), np.ones(q)]).astype(np.float32)
    perm = np.concatenate([np.arange(q, 2 * q), np.arange(0, q), np.arange(3 * q, 4 * q), np.arange(2 * q, 3 * q)])
    cosT = np.ones((dim, NTOK), np.float32)
    sinT = np.zeros((dim, NTOK), np.float32)
    cosT[:, 256:] = cos.T
    sinT[:, 256:] = (sin * sign[None, :]).T
    return cosT, sinT, perm


def _prep_shared(inp):
    f32 = np.float32
    cos64, sin64, perm64 = _rope_tables(64)
    cos32, sin32, perm32 = _rope_tables(32)
    sh = {}
    w_mod = inp["w_mod"]
    sh["wmod"] = np.ascontiguousarray(w_mod.reshape(2, 8, 128, 12, 512).transpose(0, 3, 2, 1, 4))
    sh["bmod_pp"] = np.ascontiguousarray(inp["b_mod"].reshape(2, 6, 8, 128).transpose(0, 3, 1, 2).reshape(2, 128, 48))
    sh["bmod"] = np.ascontiguousarray(inp["b_mod"])
    sh["lnv"] = np.ascontiguousarray(np.stack([inp["ln1_g"], inp["ln1_b"], inp["ln2_g"], inp["ln2_b"]], 1))
    g = inp["ffn_w_gate"].reshape(2, 8, 128, 11, 256).transpose(0, 3, 2, 1, 4)
    u = inp["ffn_w_up"].reshape(2, 8, 128, 11, 256).transpose(0, 3, 2, 1, 4)
    sh["wgu"] = np.ascontiguousarray(np.stack([g, u], 4).reshape(2, 11, 128, 8, 512))
    cw = inp["ffn_conv_w"].reshape(2, 3, NFC, 128).transpose(0, 3, 2, 1)
    cb = inp["ffn_conv_b"].reshape(2, NFC, 128).transpose(0, 2, 1)[..., None]
    sh["convp"] = np.ascontiguousarray(np.concatenate([cw, cb], -1))
    sh["wdown"] = np.ascontiguousarray(inp["ffn_w_down"].reshape(2, NPASS, FCP, 128, D).transpose(0, 1, 3, 2, 4))
    W = inp["ab_w_in"][0]
    z = lambda n: np.zeros((D, n), f32)
    chunks = [W[:, 0:128], W[:, 128:256]]
    qs = W[:, 256:768].reshape(D, 8, 64)
    for c in range(4):
        chunks.append(np.concatenate([qs[:, c], qs[:, 4 + c]], 1))
    for c in range(4):
        chunks.append(np.concatenate([qs[:, c][:, perm64], qs[:, 4 + c][:, perm64]], 1))
    chunks.append(W[:, 768:896])
    kr = W[:, 896:928]
    chunks.append(np.concatenate([z(64), kr, z(32)], 1))
    chunks.append(np.concatenate([z(64), kr[:, perm32], z(32)], 1))
    ks = W[:, 928:1056].reshape(D, 2, 64)
    chunks.append(np.concatenate([ks[:, 0], ks[:, 1]], 1))
    chunks.append(np.concatenate([ks[:, 0][:, perm64], ks[:, 1][:, perm64]], 1))
    chunks.append(W[:, 1056:1184])
    w0 = np.stack(chunks, 0).reshape(16, 8, 128, 128).transpose(2, 0, 1, 3)
    sh["w0"] = np.ascontiguousarray(w0)
    Wq = inp["mla_w_qb"][0].reshape(256, 8, 96)
    raw = Wq
    rot = np.concatenate([np.zeros((256, 8, 64), f32), Wq[:, :, 64:96][:, :, perm32]], 2)
    wqb = np.stack([raw, rot], 0)
    wqb = wqb.reshape(2, 2, 128, 8, 96).transpose(2, 3, 0, 1, 4)
    sh["wqb"] = np.ascontiguousarray(wqb)
    Wkv = inp["mla_w_kvb"][0].reshape(128, 8, 128)
    sh["wkvb"] = np.ascontiguousarray(np.concatenate([Wkv[:, :, 0:64].reshape(128, 512), Wkv[:, :, 64:128].reshape(128, 512)], 1))
    sh["qnorm_pp"] = np.ascontiguousarray(inp["mla_q_norm"][0].reshape(2, 128).T)
    sh["kvnorm_pp"] = np.ascontiguousarray(inp["mla_kv_norm"][0].reshape(128, 1))
    sh["sink"] = np.ascontiguousarray(inp["swa_sink"][0])
    sh["wout0"] = np.ascontiguousarray(inp["ab_w_out"][0].reshape(8, 128, D).transpose(1, 0, 2))
    W1 = inp["diff_w_in"][0]
    perm128 = np.concatenate([perm64, 64 + perm64])
    heads = []
    for h in range(8):
        q = W1[:, h * 128:(h + 1) * 128]
        k = W1[:, 1024 + h * 128:1024 + (h + 1) * 128]
        v = W1[:, 2048 + h * 128:2048 + (h + 1) * 128]
        hw = np.stack([q, q[:, perm128], k, k[:, perm128], v], 0)
        heads.append(hw.reshape(5, 8, 128, 128).transpose(2, 0, 1, 3))
    sh["w1h"] = np.ascontiguousarray(np.stack(heads, 0))
    sh["wout1"] = np.ascontiguousarray(inp["diff_w_out"][0].reshape(8, 128, D).transpose(1, 0, 2))
    sh["lamv"] = np.ascontiguousarray(np.stack([inp["diff_lam_q1"][0], inp["diff_lam_k1"][0], inp["diff_lam_q2"][0], inp["diff_lam_k2"][0]], 0))
    sh["subg"] = np.ascontiguousarray(inp["diff_subln_g"][0])
    sh["ident_f"] = np.eye(128, dtype=f32)
    sh["ident_b"] = np.eye(128, dtype=f32).astype(ml_dtypes.bfloat16)
    jj = np.arange(128)[:, None]
    ii = np.arange(128)[None, :]
    mP = (jj >= ii).astype(f32)
    mN = (jj <= ii).astype(f32)
    sh["masks"] = np.stack([np.tile(mP, (1, 4)), np.tile(mN, (1, 4))], 0).astype(ml_dtypes.bfloat16)
    sh["rope64"] = np.ascontiguousarray(np.stack([np.tile(cos64, (2, 1)), np.tile(sin64, (2, 1))], 0))
    r32 = np.zeros((2, 128, NTOK), f32)
    r32[0, 64:96] = cos32
    r32[1, 64:96] = sin32
    sh["rope32"] = r32
    return sh


_CACHE = {}


def _get_program(layers, dbg=()):
    key = (tuple(layers), tuple(dbg))
    if key not in _CACHE:
        _CACHE[key] = build_program(layers, dbg)
    return _CACHE[key]


def kernel(**inputs):
    inp = {k: np.asarray(v, dtype=np.float32) for k, v in inputs.items()}
    sh = _prep_shared(inp)
    B = inp["x"].shape[0]
    in_maps = []
    for b in range(B):
        m = dict(sh)
        m["x_all"] = np.ascontiguousarray(np.concatenate([inp["ctx"][b], inp["x"][b]], 0))
        cc = np.stack([inp["c"][b], inp["c_ctx"]], 0)
        m["cT"] = np.ascontiguousarray(cc.reshape(2, 8, 128).transpose(2, 1, 0))
        in_maps.append(m)
    nc, info = _get_program((0, 1))
    res = run_bass_kernel_spmd(nc, in_maps, core_ids=list(range(B)))
    out = np.stack([np.asarray(r["out"], dtype=np.float32) for r in res.results], 0)
    return out
```

```python
import math
import os
import numpy as np
import ml_dtypes
import concourse.bass as bass
import concourse.mybir as mybir
from concourse.bass_utils import run_bass_kernel_spmd

F32 = mybir.dt.float32
BF16 = mybir.dt.bfloat16
AF = mybir.ActivationFunctionType
ALU = mybir.AluOpType

D = 1024
DFF = 2816
NTOK = 2304
NT = 18
NCTX_T = 2
ALPHA = 4.0 ** 0.25
LN_EPS = 1e-5
RMS_EPS = 1e-6
NFC = 22
NPASS = 2
FCP = NFC // NPASS


class _Rec:
    def __init__(self):
        self.call = None

    def __getattr__(self, name):
        def f(*a, **k):
            self.call = (name, a, k)
            return self
        return f


def _record(fn):
    r = _Rec()
    fn(r)
    assert r.call is not None
    return r.call


class Sched:
    ENGS = ("pe", "act", "dve", "pool", "sp")

    def __init__(self, nc, same_engine_sync=True):
        self.nc = nc
        self.streams = {e: [] for e in self.ENGS}
        self.cnt = {e: 0 for e in self.ENGS}
        self.waited = {}
        self.lastw = {}
        self.readers = {}
        self.semcnt = {}
        self.same = same_engine_sync

    def _deps(self, reads, writes):
        deps = {}

        def add(k, v):
            if deps.get(k, 0) < v:
                deps[k] = v

        for r in reads:
            t = self.lastw.get(r)
            if t is not None:
                add(*t)
        for w in writes:
            t = self.lastw.get(w)
            if t is not None:
                add(*t)
            for k, v in self.readers.get(w, {}).items():
                add(k, v)
        return deps

    def _commit(self, tok, reads, writes):
        k, v = tok
        for r in reads:
            d = self.readers.setdefault(r, {})
            if d.get(k, 0) < v:
                d[k] = v
        for w in writes:
            self.lastw[w] = tok
            self.readers[w] = {}

    def _waits(self, eng, deps):
        waits = []
        for k, v in deps.items():
            if k == eng and (eng == "pe" or not self.same):
                continue
            if self.waited.get((eng, k), 0) >= v:
                continue
            self.waited[(eng, k)] = v
            waits.append((k, v))
        return waits

    def op(self, eng, fn, reads=(), writes=()):
        deps = self._deps(reads, writes)
        waits = self._waits(eng, deps)
        self.cnt[eng] += 1
        tok = (eng, self.cnt[eng])
        self.semcnt[eng] = self.cnt[eng]
        self.streams[eng].append((waits, _record(fn), (eng, 1)))
        self._commit(tok, reads, writes)
        return tok

    def dma(self, q, semkey, fn, reads=(), writes=()):
        deps = self._deps(reads, writes)
        prev = self.semcnt.get(semkey, 0)
        if prev and deps.get(semkey, 0) < prev:
            deps[semkey] = prev
        waits = self._waits(q, deps)
        self.semcnt[semkey] = prev + 16
        tok = (semkey, prev + 16)
        self.streams[q].append((waits, _record(fn), (semkey, 16)))
        self._commit(tok, reads, writes)
        return tok

    def wait_all(self, eng):
        waits = []
        for k, v in self.semcnt.items():
            if k == eng:
                continue
            if self.waited.get((eng, k), 0) >= v:
                continue
            self.waited[(eng, k)] = v
            waits.append((k, v))
        self.streams[eng].append((waits, None, None))

    def barrier(self):
        for e in self.ENGS:
            self.wait_all(e)

    def emit(self):
        nc = self.nc
        sems = {}
        for i, k in enumerate(self.semcnt):
            sems[k] = nc.alloc_semaphore(name=f"sm{i}")
        streams = self.streams

        def run(engname, eng):
            for waits, fn, inc in streams[engname]:
                for k, v in waits:
                    eng.wait_ge(sems[k], v)
                if fn is None:
                    continue
                name, a_, k_ = fn
                ins = getattr(eng, name)(*a_, **k_)
                if inc is not None:
                    ins.then_inc(sems[inc[0]], inc[1])

        with nc.Block() as block:
            @block.tensor
            def _(e):
                run("pe", e)

            @block.scalar
            def _(e):
                run("act", e)

            @block.vector
            def _(e):
                run("dve", e)

            @block.gpsimd
            def _(e):
                run("pool", e)

            @block.sync
            def _(e):
                run("sp", e)
        return len(sems)


class Arena:
    def __init__(self, nc, S):
        self.nc = nc
        self.S = S
        self.base = (nc.sbuf_base + 63) // 64 * 64
        self.top = nc.sbuf_top
        self.cur = self.base
        self.n = 0
        self.peak = 0

    def alloc(self, name, shape, dt):
        per = 1
        for s in shape[1:]:
            per *= s
        per *= 2 if dt == BF16 else 4
        off = self.cur
        self.cur = (off + per + 63) // 64 * 64
        assert self.cur <= self.top, f"SBUF overflow allocating {name}: {self.cur} > {self.top}"
        self.peak = max(self.peak, self.cur)
        self.n += 1
        return self.nc.alloc_sbuf_tensor_at(f"{name}_{self.n}", list(shape), dt, offset=off)

    def mark(self):
        return self.cur

    def release(self, m):
        self.S.barrier()
        self.cur = m


def build_program(layers=(0, 1), dbg=(), stop_after=None):
    nc = bass.Bass("TRN2", target_bir_lowering=False)
    S = Sched(nc)
    A = Arena(nc, S)
    first_layer, last_layer = layers[0], layers[-1]

    def din(name, shape, dt=F32):
        return nc.dram_tensor(name, list(shape), dt, kind="ExternalInput").ap()

    def dscr(name, shape, dt=F32):
        return nc.dram_tensor(name, list(shape), dt, kind="Internal").ap()

    x_all = din("x_all", [NTOK, D])
    cT_d = din("cT", [128, 8, 2])
    wmod_d = din("wmod", [2, 12, 128, 8, 512])
    bmodpp_d = din("bmod_pp", [2, 128, 48])
    bmod_d = din("bmod", [2, 6144])
    lnv_d = din("lnv", [2, 4, D])
    wgu_d = din("wgu", [2, 11, 128, 8, 512])
    convp_d = din("convp", [2, 128, NFC, 4])
    wdown_d = din("wdown", [2, NPASS, 128, FCP, D])
    w0_d = din("w0", [128, 16, 8, 128])
    wqb_d = din("wqb", [128, 8, 2, 2, 96])
    wkvb_d = din("wkvb", [128, 1024])
    qnorm_d = din("qnorm_pp", [128, 2])
    kvnorm_d = din("kvnorm_pp", [128, 1])
    sink_d = din("sink", [8])
    wout0_d = din("wout0", [128, 8, D])
    w1h_d = din("w1h", [8, 128, 5, 8, 128])
    wout1_d = din("wout1", [128, 8, D])
    lamv_d = din("lamv", [4, 64])
    subg_d = din("subg", [128])
    identf_d = din("ident_f", [128, 128])
    identb_d = din("ident_b", [128, 128], BF16)
    masks_d = din("masks", [2, 128, 512], BF16)
    rope64_d = din("rope64", [2, 128, NTOK])
    rope32_d = din("rope32", [2, 128, NTOK])
    out_d = nc.dram_tensor("out", [2048, D], F32, kind="ExternalOutput").ap()
    xres = [dscr("xres_a", [NTOK, D]), dscr("xres_b", [NTOK, D])]
    ybuf = dscr("ybuf", [NTOK, D])
    gts_d = dscr("gts", [2, 2, 2, 128, D])
    dbg_out = {}

    P2 = [nc.alloc_psum_tensor(f"pp{i}", [128, 1024], F32) for i in range(4)]

    def bank(i):
        return P2[i // 2][:, (i % 2) * 512:(i % 2 + 1) * 512]

    ident_f = A.alloc("ident_f", [128, 128], F32)
    ident_b = A.alloc("ident_b", [128, 128], BF16)
    ones_f = A.alloc("ones_f", [128, 128], F32)
    epsb = A.alloc("epsb", [128, 2], F32)
    masks = A.alloc("masks", [128, 2, 512], BF16)
    mpp2 = A.alloc("mpp", [128, 2, 4, 8, 2], F32)
    hT = A.alloc("hT", [128, 8, NTOK], BF16)
    gt_t = A.alloc("gt_t", [128, 2, D], F32)
    lng_t = A.alloc("lng_t", [128, D], F32)
    lnb_t = A.alloc("lnb_t", [128, D], F32)

    class _Stop(Exception):
        pass

    def checkpoint(name, tensors):
        if name in dbg or stop_after == name:
            S.barrier()
            for nm, (t_ap, shape, dt) in tensors.items():
                d = nc.dram_tensor("dbg_" + nm, list(shape), dt, kind="ExternalOutput").ap()
                cdma(lambda e: e.dma_start(out=d, in_=t_ap))
        if stop_after == name:
            raise _Stop()

    cq = [0]

    def cdma(fn, reads=(), writes=(), q="sp"):
        cq[0] += 1
        return S.dma(q, ("c", cq[0] % 4), fn, reads=reads, writes=writes)

    cdma(lambda e: e.dma_start(out=ident_f[:], in_=identf_d), writes=["ident_f"])
    cdma(lambda e: e.dma_start(out=ident_b[:], in_=identb_d), writes=["ident_b"])
    cdma(lambda e: e.dma_start(out=masks[:], in_=masks_d.rearrange("m p n -> p m n")), writes=["masks"])
    S.op("dve", lambda e: e.memset(ones_f[:], 1.0), writes=["ones_f"])
    S.op("dve", lambda e: e.memset(epsb[:, 0:1], LN_EPS), writes=["epsb"])
    S.op("dve", lambda e: e.memset(epsb[:, 1:2], RMS_EPS), writes=["epsb"])

    def ttiles(a, b):
        return range(a // 128, (b + 127) // 128)

    def modulation_units(l, need_c_gates, pbank_pp, pbank_gt):
        st = {}
        nvar = 2 if need_c_gates else 1
        vi_of = {0: 0, 1: 1, 3: 2, 4: 3}

        def setup():
            st["m0"] = A.mark()
            cT = st["cT"] = A.alloc("cT", [128, 8, 2], F32)
            scf = st["scf"] = A.alloc("scf", [128, 8, 2], F32)
            scb = st["scb"] = A.alloc("scb", [128, 8, 2], BF16)
            cbc = st["cbc"] = A.alloc("cbc", [128, 2, 8, 128], BF16)
            bpp = st["bpp"] = A.alloc("bpp", [128, 48], F32)
            bmb = st["bmb"] = A.alloc("bmb", [128, 2, D], F32)
            st["gtb"] = A.alloc("gtb", [128, 2, 2, D], F32)
            st["wsl"] = [A.alloc(f"wm{i}", [128, 8, 512], BF16) for i in range(3)]
            cdma(lambda e: e.dma_start(out=cT[:], in_=cT_d), writes=[("cT", l)])
            cdma(lambda e: e.dma_start(out=bpp[:], in_=bmodpp_d[l]), writes=[("bpp", l)])
            for gi, vec in enumerate((2, 5)):
                cdma(lambda e, gi=gi, vec=vec: e.dma_start(out=bmb[:, gi, :], in_=bmod_d[l, vec * 1024:(vec + 1) * 1024].partition_broadcast(128)),
                     writes=[("bmb", l, gi)])
            S.op("act", lambda e: e.activation(out=scf[:], in_=cT[:], func=AF.Silu), reads=[("cT", l)], writes=[("scf", l)])
            S.op("dve", lambda e: e.tensor_copy(out=scb[:], in_=scf[:]), reads=[("scf", l)], writes=[("scb", l)])
            for v in range(nvar):
                for k in range(8):
                    S.op("dve", lambda e, v=v, k=k: e.tensor_scalar(out=cbc[:, v, k, :], in0=ones_f[:], scalar1=scf[:, k, v:v + 1], scalar2=None, op0=ALU.mult),
                         reads=[("scf", l), "ones_f"], writes=[("cbc", l, v)])

        def block(blk, n):
            def f():
                wsl, scb, cbc, bpp, bmb, gtb = st["wsl"], st["scb"], st["cbc"], st["bpp"], st["bmb"], st["gtb"]
                sl = n % 3
                vec, half = blk // 2, blk % 2
                S.dma("pool", ("wm", sl), lambda e: e.dma_start(out=wsl[sl][:], in_=wmod_d[l, blk], max_dma_last_dim=8192), writes=[("wm", l, sl)])
                if vec in vi_of:
                    vi = vi_of[vec]
                    pb = ("pb", pbank_pp)
                    for j in range(4):
                        for k in range(8):
                            S.op("pe", lambda e, j=j, k=k: e.matmul(bank(pbank_pp)[:, j * 2:j * 2 + 2], lhsT=wsl[sl][:, k, j * 128:(j + 1) * 128],
                                                                    rhs=scb[:, k, :], start=(k == 0), stop=(k == 7)),
                                 reads=[("wm", l, sl), ("scb", l)], writes=[pb])
                    c0 = half * 4
                    psv = bank(pbank_pp)[:, 0:8].rearrange("p (j v) -> p j v", v=2)
                    for v in range(2):
                        if vec in (1, 4):
                            S.op("dve", lambda e, v=v: e.scalar_tensor_tensor(
                                out=mpp2[:, l, vi, c0:c0 + 4, v], in0=psv[:, :, v], scalar=1.0, in1=bpp[:, vec * 8 + c0:vec * 8 + c0 + 4], op0=ALU.add, op1=ALU.add),
                                reads=[pb, ("bpp", l)], writes=[("mpp", l, vi, half, v)])
                        else:
                            S.op("dve", lambda e, v=v: e.tensor_tensor(
                                out=mpp2[:, l, vi, c0:c0 + 4, v], in0=psv[:, :, v], in1=bpp[:, vec * 8 + c0:vec * 8 + c0 + 4], op=ALU.add),
                                reads=[pb, ("bpp", l)], writes=[("mpp", l, vi, half, v)])
                else:
                    gi = 0 if vec == 2 else 1
                    pb = ("pb", pbank_gt)
                    for v in range(nvar):
                        for k in range(8):
                            S.op("pe", lambda e, k=k, v=v: e.matmul(bank(pbank_gt), lhsT=cbc[:, v, k, :], rhs=wsl[sl][:, k, :], start=(k == 0), stop=(k == 7)),
                                 reads=[("wm", l, sl), ("cbc", l, v)], writes=[pb])
                        S.op("dve", lambda e, v=v: e.tensor_tensor(
                            out=gtb[:, gi, v, half * 512:(half + 1) * 512], in0=bank(pbank_gt), in1=bmb[:, gi, half * 512:(half + 1) * 512], op=ALU.add),
                            reads=[pb, ("bmb", l, gi)], writes=[("gtb", l, gi, v, half)])
            return f

        order = (0, 1, 2, 3, 6, 7, 8, 9, 4, 5, 10, 11)
        blocks = [block(blk, n) for n, blk in enumerate(order)]

        def finish():
            gtb = st["gtb"]
            for gi in range(2):
                for v in range(nvar):
                    cdma(lambda e, gi=gi, v=v: e.dma_start(out=gts_d[l, gi, v], in_=gtb[:, gi, v, :]),
                         reads=[("gtb", l, gi, v, 0), ("gtb", l, gi, v, 1)], writes=[("gts", l, gi, v)])
            checkpoint(f"mod{l}", {"mpp": (mpp2[:, l], [128, 4, 8, 2], F32), "gtb": (gtb[:], [128, 2, 2, D], F32)})
            A.release(st["m0"])

        return setup, blocks, finish

    def load_sublayer_consts(l, sub, need_c):
        for v in range(2 if need_c else 1):
            cdma(lambda e, v=v: e.dma_start(out=gt_t[:, v, :], in_=gts_d[l, sub, v]), reads=[("gts", l, sub, v)], writes=[("gt_t", v)])
        cdma(lambda e: e.dma_start(out=lng_t[:], in_=lnv_d[l, 2 * sub].partition_broadcast(128)), writes=["lng_t"])
        cdma(lambda e: e.dma_start(out=lnb_t[:], in_=lnv_d[l, 2 * sub + 1].partition_broadcast(128)), writes=["lnb_t"])

    def phase_P(l, src, tiles, sub, fillers=()):
        m0 = A.mark()
        fillers = list(fillers)
        xt = [A.alloc(f"xtP{i}", [128, D], F32) for i in range(3)]
        vi_sh, vi_sc = 2 * sub, 2 * sub + 1
        def xload(n):
            if n < len(tiles):
                t_ = tiles[n]
                sl_ = n % 3
                S.dma("sp", ("xt", sl_), lambda e: e.dma_start(out=xt[sl_][:], in_=src[t_ * 128:(t_ + 1) * 128, :]),
                      reads=[(src.tensor.name, t_)], writes=[("xtP", sl_)])
        xload(0)
        xload(1)
        for n, t in enumerate(tiles):
            sl = n % 3
            v = 1 if t < NCTX_T else 0
            xload(n + 2)
            pp = P2[n % 2]
            for k in range(8):
                S.op("pe", lambda e, k=k: e.transpose(out=pp[:, k * 128:(k + 1) * 128], in_=xt[sl][:, k * 128:(k + 1) * 128], identity=ident_f[:]),
                     reads=[("xtP", sl), "ident_f"], writes=[("pb", 2 * (n % 2) + k // 4)])
            for k in range(8):
                rk = [("pb", 2 * (n % 2) + k // 4), ("mpp", l, vi_sc, k // 4, v), ("mpp", l, vi_sh, k // 4, v)]
                S.op("act", lambda e, k=k: e.activation(out=hT[:, k, t * 128:(t + 1) * 128], in_=pp[:, k * 128:(k + 1) * 128], func=AF.Identity,
                                                        scale=mpp2[:, l, vi_sc, k, v:v + 1], bias=mpp2[:, l, vi_sh, k, v:v + 1]),
                     reads=rk, writes=[("hT", t, k)])
            if fillers:
                fillers.pop(0)()
        for f in fillers:
            f()
        A.release(m0)

    def hT_reads(a, b, k):
        return [("hT", t, k) for t in ttiles(a, b)]

    def phase_R(tiles, lhs_fn, lhs_reads_fn, nchunk, w_t, w_key, xsrc, first, last, dst, dst_row0, ytmp, fillers=()):
        m0 = A.mark()
        xt = [A.alloc(f"xtR{i}", [128, D], F32) for i in range(3)]
        NTMP = 4
        tmp = [A.alloc(f"tmpR{i}", [128, D], F32) for i in range(NTMP)]
        st = A.alloc("stR", [128, NTMP, 2, 6], F32)
        mv = A.alloc("mvR", [128, NTMP, 4], F32)
        rsrc = xsrc if first else ytmp

        def xload(n):
            if n < len(tiles):
                t_ = tiles[n]
                sl_ = n % 3
                S.dma("sp", ("xt", sl_), lambda e: e.dma_start(out=xt[sl_][:], in_=rsrc[t_ * 128:(t_ + 1) * 128, :]),
                      reads=[(rsrc.tensor.name, t_)], writes=[("xtR", sl_)])
        xload(0)
        xload(1)

        def stageA(n):
            t = tiles[n]
            sl = n % 3
            s2 = n % NTMP
            v = 1 if t < NCTX_T else 0
            xload(n + 2)
            p2i = n % 3
            pp = P2[p2i]
            for hf in range(2):
                for c in range(nchunk):
                    S.op("pe", lambda e, c=c, hf=hf: e.matmul(pp[:, hf * 512:(hf + 1) * 512], lhsT=lhs_fn(c, t), rhs=w_t[:, c, hf * 512:(hf + 1) * 512],
                                                              start=(c == 0), stop=(c == nchunk - 1)),
                         reads=lhs_reads_fn(c, t) + [w_key], writes=[("pb", 2 * p2i + hf)])
            pbk = [("pb", 2 * p2i), ("pb", 2 * p2i + 1)]
            S.op("dve", lambda e: e.tensor_tensor(out=tmp[s2][:], in0=pp[:], in1=gt_t[:, v, :], op=ALU.mult),
                 reads=pbk + [("gt_t", v)], writes=[("tmpR", s2)])
            S.op("dve", lambda e: e.scalar_tensor_tensor(out=tmp[s2][:], in0=xt[sl][:], scalar=(ALPHA if first else 1.0), in1=tmp[s2][:],
                                                         op0=ALU.mult, op1=ALU.add),
                 reads=[("xtR", sl), ("tmpR", s2)], writes=[("tmpR", s2)])
            if not last:
                S.dma("sp", ("yst", s2), lambda e: e.dma_start(out=ytmp[t * 128:(t + 1) * 128, :], in_=tmp[s2][:]),
                      reads=[("tmpR", s2)], writes=[(ytmp.tensor.name, t)])
                return
            for c in range(2):
                S.op("dve", lambda e, c=c: e.bn_stats(out=st[:, s2, c, :], in_=tmp[s2][:, c * 512:(c + 1) * 512]),
                     reads=[("tmpR", s2)], writes=[("stR", s2, c)])
            S.op("dve", lambda e: e.bn_aggr(out=mv[:, s2, 0:2], in_=st[:, s2, :, :].rearrange("p a b -> p (a b)")),
                 reads=[("stR", s2, 0), ("stR", s2, 1)], writes=[("mvR", s2)])
            S.op("act", lambda e: e.activation(out=mv[:, s2, 2:3], in_=mv[:, s2, 1:2], func=AF.Ln, bias=epsb[:, 0:1]),
                 reads=[("mvR", s2), "epsb"], writes=[("mvR", s2)])
            S.op("act", lambda e: e.activation(out=mv[:, s2, 2:3], in_=mv[:, s2, 2:3], func=AF.Exp, scale=-0.5),
                 reads=[("mvR", s2)], writes=[("mvR", s2)])

        def stageB(n):
            t = tiles[n]
            s2 = n % NTMP
            S.op("dve", lambda e: e.scalar_tensor_tensor(out=mv[:, s2, 3:4], in0=mv[:, s2, 0:1], scalar=-1.0, in1=mv[:, s2, 2:3], op0=ALU.mult, op1=ALU.mult),
                 reads=[("mvR", s2)], writes=[("mvR", s2)])
            S.op("act", lambda e: e.activation(out=tmp[s2][:], in_=tmp[s2][:], func=AF.Identity, scale=mv[:, s2, 2:3], bias=mv[:, s2, 3:4]),
                 reads=[("mvR", s2), ("tmpR", s2)], writes=[("tmpR", s2)])
            eng2 = "dve" if pool_free else "pool"
            S.op(eng2, lambda e: e.tensor_tensor(out=tmp[s2][:], in0=tmp[s2][:], in1=lng_t[:], op=ALU.mult),
                 reads=[("tmpR", s2), "lng_t"], writes=[("tmpR", s2)])
            S.op(eng2, lambda e: e.tensor_tensor(out=tmp[s2][:], in0=tmp[s2][:], in1=lnb_t[:], op=ALU.add),
                 reads=[("tmpR", s2), "lnb_t"], writes=[("tmpR", s2)])
            r0 = t * 128 - dst_row0
            if pool_free:
                S.dma("sp", ("yst", s2), lambda e: e.dma_start(out=dst[r0:r0 + 128, :], in_=tmp[s2][:]),
                      reads=[("tmpR", s2)], writes=[(dst.tensor.name, t)])
            else:
                S.dma("pool", ("ystp", s2), lambda e: e.dma_start(out=dst[r0:r0 + 128, :], in_=tmp[s2][:]),
                      reads=[("tmpR", s2)], writes=[(dst.tensor.name, t)])

        fillers = list(fillers)
        pool_free = bool(fillers)
        for n in range(len(tiles)):
            stageA(n)
            if last and n >= 1:
                stageB(n - 1)
            if fillers:
                fillers.pop(0)()
        if last:
            stageB(len(tiles) - 1)
        for f in fillers:
            f()
        A.release(m0)

    def run_attention(jobs, ET, fillers=()):
        steps = [(job, i) for job in jobs for i in range(len(job["keys"]))]
        fillers = list(fillers)
        fill_every = max(1, len(steps) // max(1, len(fillers))) if fillers else 0
        NSB = 4
        LA = NSB - 1

        def qk(si):
            job, ki = steps[si]
            j = job["keys"][ki]
            b = si % NSB
            for (c0, ncol, lhsT, rhs, view) in job["qk"](j):
                out = bank(b)[:, c0:c0 + ncol]
                if view is not None:
                    out = view(out)
                S.op("pe", lambda e, out=out, lhsT=lhsT, rhs=rhs: e.matmul(out, lhsT=lhsT, rhs=rhs, start=True, stop=True, skip_group_check=True),
                     reads=job["qk_reads"](j), writes=[("pb", b)])

        def ex(si):
            job, ki = steps[si]
            j = job["keys"][ki]
            b = si % NSB
            eb = si % len(ET)
            W = job["W"]
            o = ET[eb][:, 0:W]
            S.op("act", lambda e, o=o, b=b, W=W, job=job: e.activation(out=o, in_=bank(b)[:, 0:W], func=AF.Exp, scale=job["scale"]),
                 reads=[("pb", b)], writes=[("ET", eb)])
            mk = job["mask"](j) if job.get("mask") else None
            if mk is not None:
                S.op("dve", lambda e, o=o, mk=mk: e.tensor_tensor(out=o, in0=o, in1=mk, op=ALU.mult), reads=[("ET", eb), "masks"], writes=[("ET", eb)])

        def pv(si):
            job, ki = steps[si]
            j = job["keys"][ki]
            eb = si % len(ET)
            nkeys = len(job["keys"])
            for (ap_, key_, fib, ec0, rhs) in job["pv"](j):
                S.op("pe", lambda e, ap_=ap_, ec0=ec0, rhs=rhs, fib=fib: e.matmul(
                    ap_, lhsT=ET[eb][:, ec0:ec0 + 128], rhs=rhs, start=(ki == 0 and fib), stop=(ki == nkeys - 1), skip_group_check=True),
                    reads=[("ET", eb)] + job["pv_reads"](j), writes=[key_])
            if ki == nkeys - 1:
                return job["fin"]()
            return None

        pending = []
        for si in range(min(LA, len(steps))):
            qk(si)
        for si in range(len(steps)):
            if si + LA < len(steps):
                qk(si + LA)
            ex(si)
            pending = [(d - 1, f) for d, f in pending]
            due = [f for d, f in pending if d <= 0]
            pending = [(d, f) for d, f in pending if d > 0]
            for f in due:
                f()
            late = pv(si)
            if late is not None:
                if callable(late):
                    late = [(2, late)]
                pending.extend(late)
            if fillers and (si + 1) % fill_every == 0:
                fillers.pop(0)()
        for _, f in sorted(pending, key=lambda x: x[0]):
            f()
        for f in fillers:
            f()

    def rsqrt_small(dst, src, scale, eps_col, rkeys, wkey):
        S.op("act", lambda e: e.activation(out=dst, in_=src, func=AF.Ln, scale=scale, bias=epsb[:src.shape[0], eps_col:eps_col + 1]),
             reads=rkeys + ["epsb"], writes=[wkey])
        S.op("act", lambda e: e.activation(out=dst, in_=dst, func=AF.Exp, scale=-0.5), reads=[wkey], writes=[wkey])

    BLK_ALL = [(0, 512), (512, 1024), (1024, 1536), (1536, 2048), (2048, 2304)]
    BLK_LAT = [(256, 768), (768, 1280), (1280, 1792), (1792, 2304)]

    def layer0_attention(src, dst, mod0, mod1):
        setup0, blocks0, finish0 = mod0
        setup0()
        for f in blocks0[:4]:
            f()
        phase_P(0, src, list(range(NT)), 0, fillers=blocks0[4:])
        finish0()
        load_sublayer_consts(0, 0, True)
        checkpoint("P0", {"hT": (hT[:], [128, 8, NTOK], BF16)})
        mA = A.mark()
        qan = A.alloc("qan", [128, 2, NTOK], BF16)
        kvan = A.alloc("kvan", [128, NTOK], BF16)
        krT = A.alloc("krT", [96, NTOK], BF16)
        rope32 = A.alloc("rope32", [128, 2, NTOK], F32)
        wqb = A.alloc("wqb", [128, 8, 2, 2, 96], BF16)
        wkvb = A.alloc("wkvb", [128, 1024], BF16)
        qn_pp = A.alloc("qn_pp", [128, 2], F32)
        kvn_pp = A.alloc("kvn_pp", [128, 1], F32)
        esink = A.alloc("esink", [128, 8], F32)
        ET = [A.alloc(f"ET{i}", [128, 512], BF16) for i in range(4)]
        mB = A.mark()
        qsT = A.alloc("qsT", [128, 4, NTOK], BF16)
        ksT2 = [A.alloc(f"ksT{i}", [128, NTOK], BF16) for i in range(2)]
        Vs = A.alloc("Vs", [128, NT, 2, 65], BF16)
        mC = A.mark()
        w0 = A.alloc("w0", [128, 16, 8, 128], BF16)
        rope64 = A.alloc("rope64", [128, 2, NTOK], F32)
        t1 = [A.alloc(f"t1_{i}", [128, 512], F32) for i in range(2)]
        t2 = [A.alloc(f"t2_{i}", [128, 512], F32) for i in range(2)]
        qaf = A.alloc("qaf", [128, 2, 512], F32)
        qsq = A.alloc("qsq", [128, 2, 512], F32)
        rbc = A.alloc("rbc", [128, 512], F32)

        for r in range(2):
            cdma(lambda e, r=r: e.dma_start(out=rope32[:, r, :], in_=rope32_d[r]), writes=[("rope32", r)])
            cdma(lambda e, r=r: e.dma_start(out=rope64[:, r, :], in_=rope64_d[r]), writes=[("rope64", r)])
        cdma(lambda e: e.dma_start(out=qn_pp[:], in_=qnorm_d), writes=["qn_pp"])
        cdma(lambda e: e.dma_start(out=kvn_pp[:], in_=kvnorm_d), writes=["kvn_pp"])
        cdma(lambda e: e.dma_start(out=esink[:], in_=sink_d.partition_broadcast(128)), writes=["esink"])
        S.op("act", lambda e: e.activation(out=esink[:], in_=esink[:], func=AF.Exp), reads=["esink"], writes=["esink"])
        for g4 in range(4):
            S.dma("pool", ("w0", g4), lambda e, g4=g4: e.dma_start(out=w0[:, g4 * 4:(g4 + 1) * 4], in_=w0_d[:, g4 * 4:(g4 + 1) * 4], max_dma_last_dim=8192),
                  writes=[("w0", g4)])
        S.dma("pool", ("wq", 0), lambda e: e.dma_start(out=wqb[:], in_=wqb_d, max_dma_last_dim=8192), writes=["wqb"])
        S.dma("pool", ("wq", 1), lambda e: e.dma_start(out=wkvb[:], in_=wkvb_d, max_dma_last_dim=8192), writes=["wkvb"])
        S.op("dve", lambda e: e.memset(Vs[:, :, :, 64:65], 1.0), writes=["Vs1"])

        pbc = [0]

        def nextbank():
            pbc[0] = (pbc[0] + 1) % 8
            return pbc[0]

        def fm_proj(bk, chunk, a, b, m=128):
            for k in range(8):
                S.op("pe", lambda e, k=k: e.matmul(bank(bk)[0:m, 0:b - a], lhsT=w0[:, chunk, k, 0:m], rhs=hT[:, k, a:b], start=(k == 0), stop=(k == 7)),
                     reads=[("w0", chunk // 4)] + hT_reads(a, b, k), writes=[("pb", bk)])

        def rope_evac(bk_raw, bk_rot, tab, tabkey, p0, p1, a, b, dst_ap, wkey, i, dsts=None):
            n = b - a
            S.op("dve", lambda e: e.tensor_tensor(out=t1[i][p0:p1, 0:n], in0=bank(bk_raw)[p0:p1, 0:n], in1=tab[p0:p1, 0, a:b], op=ALU.mult),
                 reads=[("pb", bk_raw), (tabkey, 0)], writes=[("t1", i)])
            S.op("dve", lambda e: e.tensor_tensor(out=t2[i][p0:p1, 0:n], in0=bank(bk_rot)[p0:p1, 0:n], in1=tab[p0:p1, 1, a:b], op=ALU.mult),
                 reads=[("pb", bk_rot), (tabkey, 1)], writes=[("t2", i)])
            if dsts is None:
                dsts = [(p0, p1, dst_ap)]
            for (q0, q1, d_ap) in dsts:
                S.op("pool", lambda e, q0=q0, q1=q1, d_ap=d_ap: e.tensor_tensor(out=d_ap, in0=t1[i][q0:q1, 0:n], in1=t2[i][q0:q1, 0:n], op=ALU.add),
                     reads=[("t1", i), ("t2", i)], writes=[wkey])

        S.op("dve", lambda e: e.memset(ksT2[0][64:128, :], 0.0), writes=["ksTz"])
        S.op("dve", lambda e: e.memset(ksT2[1][0:64, :], 0.0), writes=["ksTz"])
        for bi, (a, b) in enumerate(BLK_ALL):
            n = b - a
            tl = list(ttiles(a, b))
            for c in range(4):
                br, bt = nextbank(), nextbank()
                fm_proj(br, 2 + c, a, b)
                fm_proj(bt, 6 + c, a, b)
                rope_evac(br, bt, rope64, "rope64", 0, 128, a, b, qsT[:, c, a:b], ("qsT", c, bi), (bi * 8 + c) % 2)
            br, bt = nextbank(), nextbank()
            fm_proj(br, 13, a, b)
            fm_proj(bt, 14, a, b)
            rope_evac(br, bt, rope64, "rope64", 0, 128, a, b, None, ("ksT", bi), 0,
                      dsts=[(0, 64, ksT2[0][0:64, a:b]), (64, 128, ksT2[1][64:128, a:b])])
            br, bt = nextbank(), nextbank()
            fm_proj(br, 11, a, b, m=96)
            fm_proj(bt, 12, a, b, m=96)
            rope_evac(br, bt, rope32, "rope32", 64, 96, a, b, krT[64:96, a:b], ("krT", bi), 1)
            bq = [nextbank(), nextbank()]
            for c in range(2):
                fm_proj(bq[c], c, a, b)
                S.op("act", lambda e, c=c, bq=bq: e.activation(out=qaf[:, c, 0:n], in_=bank(bq[c])[:, 0:n], func=AF.Copy), reads=[("pb", bq[c])], writes=[("qaf", c)])
                S.op("act", lambda e, c=c, bq=bq: e.activation(out=qsq[:, c, 0:n], in_=bank(bq[c])[:, 0:n], func=AF.Square), reads=[("pb", bq[c])], writes=[("qsq", c)])
            bs = nextbank()
            for c in range(2):
                S.op("pe", lambda e, c=c, bs=bs: e.matmul(bank(bs)[:, 0:n], lhsT=ones_f[:], rhs=qsq[:, c, 0:n], start=(c == 0), stop=(c == 1)),
                     reads=["ones_f", ("qsq", c)], writes=[("pb", bs)])
            rsqrt_small(rbc[:, 0:n], bank(bs)[:, 0:n], 1.0 / 256.0, 1, [("pb", bs)], "rbc")
            for c in range(2):
                S.op("dve", lambda e, c=c: e.scalar_tensor_tensor(out=qan[:, c, a:b], in0=qaf[:, c, 0:n], scalar=qn_pp[:, c:c + 1], in1=rbc[:, 0:n], op0=ALU.mult, op1=ALU.mult),
                     reads=[("qaf", c), "qn_pp", "rbc"], writes=[("qan", c, bi)])
            bq0 = nextbank()
            fm_proj(bq0, 10, a, b)
            S.op("act", lambda e, bq0=bq0: e.activation(out=qaf[:, 0, 0:n], in_=bank(bq0)[:, 0:n], func=AF.Copy), reads=[("pb", bq0)], writes=[("qaf", 0)])
            S.op("act", lambda e, bq0=bq0: e.activation(out=qsq[:, 0, 0:n], in_=bank(bq0)[:, 0:n], func=AF.Square), reads=[("pb", bq0)], writes=[("qsq", 0)])
            bs = nextbank()
            S.op("pe", lambda e, bs=bs: e.matmul(bank(bs)[:, 0:n], lhsT=ones_f[:], rhs=qsq[:, 0, 0:n], start=True, stop=True),
                 reads=["ones_f", ("qsq", 0)], writes=[("pb", bs)])
            rsqrt_small(rbc[:, 0:n], bank(bs)[:, 0:n], 1.0 / 128.0, 1, [("pb", bs)], "rbc")
            S.op("dve", lambda e: e.scalar_tensor_tensor(out=kvan[:, a:b], in0=qaf[:, 0, 0:n], scalar=kvn_pp[:, 0:1], in1=rbc[:, 0:n], op0=ALU.mult, op1=ALU.mult),
                 reads=[("qaf", 0), "kvn_pp", "rbc"], writes=[("kvan", bi)])
            bv = nextbank()
            for ti, t in enumerate(tl):
                for k in range(8):
                    S.op("pe", lambda e, k=k, t=t, ti=ti, bv=bv: e.matmul(bank(bv)[:, ti * 128:(ti + 1) * 128], lhsT=hT[:, k, t * 128:(t + 1) * 128], rhs=w0[:, 15, k, :],
                                                                         start=(k == 0), stop=(k == 7)),
                         reads=[("w0", 3), ("hT", t, k)], writes=[("pb", bv)])
            nt_ = len(tl)
            S.op("act", lambda e, bv=bv, t0=tl[0], nt_=nt_: e.activation(
                out=Vs[:, t0:t0 + nt_, :, 0:64], in_=bank(bv)[:, 0:nt_ * 128].rearrange("p (t g d) -> p t g d", g=2, d=64), func=AF.Copy),
                reads=[("pb", bv)], writes=[("Vs", bi)])
        checkpoint("proj0", {"qsT": (qsT[:], [128, 4, NTOK], BF16), "krT": (krT[64:96, :], [32, NTOK], BF16),
                             "qan": (qan[:], [128, 2, NTOK], BF16), "kvan": (kvan[:], [128, NTOK], BF16), "Vs": (Vs[:], [128, NT, 2, 65], BF16)})
        A.release(mC)

        otk = [A.alloc(f"otk{i}", [128, 256], BF16) for i in range(2)]
        den = [A.alloc(f"den{i}", [128, 8], F32) for i in range(2)]
        oT = hT
        _p3 = P2[3][:].bitcast(BF16)
        class _TPB:
            def __getitem__(self, idx):
                _, sl_ = idx
                a0 = sl_.start
                bk = a0 // 256
                off = bk * 1024 + (a0 - bk * 256)
                return _p3[:, off:off + (sl_.stop - sl_.start)]
        tpb = _TPB()
        jobs = []
        jn = [0]
        for g in range(2):
            for n_ in range(NT):
                if n_ < NCTX_T:
                    keys = [0, 1]
                else:
                    keys = [0, 1] + [j for j in (n_ - 1, n_, n_ + 1) if NCTX_T <= j < NT]
                aset = jn[0] % 2
                jn[0] += 1
                accb = 4 + aset

                def acc(qt, c, accb=accb):
                    return bank(accb)[:, qt * 65:(qt + 1) * 65], ("pb", accb), qt == 0

                def mask(j, n_=n_):
                    if n_ < NCTX_T:
                        return None
                    if j == n_ - 1 and j >= NCTX_T:
                        return masks[:, 0, :]
                    if j == n_ + 1:
                        return masks[:, 1, :]
                    return None

                def fin(g=g, n_=n_, accb=accb, aset=aset):
                    av = bank(accb)[:, 0:260].rearrange("p (h d) -> p h d", d=65)
                    akeys = [("pb", accb)]
                    S.op("dve", lambda e: e.tensor_tensor(out=den[aset][:, 0:4], in0=av[:, :, 64], in1=esink[:, g * 4:g * 4 + 4], op=ALU.add),
                         reads=akeys + ["esink"], writes=[("den", aset)])
                    S.op("dve", lambda e: e.reciprocal(out=den[aset][:, 4:8], in_=den[aset][:, 0:4]), reads=[("den", aset)], writes=[("den", aset)])
                    for c in range(4):
                        S.op("dve", lambda e, c=c: e.tensor_scalar(out=otk[aset][:, c * 64:(c + 1) * 64], in0=av[:, c, 0:64], scalar1=den[aset][:, 4 + c:5 + c], scalar2=None, op0=ALU.mult),
                             reads=[("pb", accb), ("den", aset)], writes=[("otk", aset, c // 2)])
                    def late():
                        for j2 in range(2):
                            slot = (aset * 2 + j2)
                            S.op("pe", lambda e, j2=j2, slot=slot: e.transpose(out=tpb[:, slot * 128:(slot + 1) * 128], in_=otk[aset][:, j2 * 128:(j2 + 1) * 128], identity=ident_b[:]),
                                 reads=[("otk", aset, j2), "ident_b"], writes=[("pb", 6 + aset)])
                        S.op("dve", lambda e: e.tensor_copy(out=oT[:, 4 + 2 * g:6 + 2 * g, n_ * 128:(n_ + 1) * 128],
                                                            in_=tpb[:, aset * 256:(aset + 1) * 256].rearrange("p (j n) -> p j n", j=2)),
                             reads=[("pb", 6 + aset)], writes=[("hT", n_, 4 + 2 * g), ("hT", n_, 5 + 2 * g)])
                    return late

                def qk_(j, g=g, n_=n_):
                    return [(0, 512, ksT2[g][:, j * 128:(j + 1) * 128], qsT[:, :, n_ * 128:(n_ + 1) * 128],
                             lambda ap: ap.rearrange("p (h n) -> p h n", h=4))]

                def pv_(j, g=g, accb=accb):
                    return [(bank(accb)[:, c * 65:(c + 1) * 65], ("pb", accb), c == 0, c * 128, Vs[:, j, g, :]) for c in range(4)]

                jobs.append(dict(
                    keys=keys, W=512, scale=64 ** -0.5,
                    qk=qk_, qk_reads=lambda j, n_=n_: [("ksT", j // 4), "ksTz"] + [("qsT", c, n_ // 4) for c in range(4)],
                    pv=pv_, pv_reads=lambda j: [("Vs", j // 4), "Vs1"],
                    mask=mask, fin=fin))
        run_attention(jobs, ET)
        checkpoint("swa0", {"oT": (hT[:], [128, 8, NTOK], BF16)})
        A.release(mB)

        Vm = A.alloc("Vm", [128, NT, 8, 65], BF16)
        qm = [A.alloc(f"qm{i}", [96, NTOK], BF16) for i in range(2)]
        km = [A.alloc(f"km{i}", [96, NTOK], BF16) for i in range(2)]
        ostg = [A.alloc(f"ostg{i}", [128, NT, 128], BF16) for i in range(2)]
        t1 = [A.alloc(f"t1m{i}", [128, 512], F32) for i in range(2)]
        t2 = [A.alloc(f"t2m{i}", [128, 512], F32) for i in range(2)]
        rden = [A.alloc(f"rden{i}", [128, 4], F32) for i in range(2)]
        S.op("dve", lambda e: e.memset(Vm[:, :, :, 64:65], 1.0), writes=["Vm1"])
        for t in range(NT):
            bv = 4 + t % 2
            S.op("pe", lambda e, t=t, bv=bv: e.matmul(bank(bv), lhsT=kvan[:, t * 128:(t + 1) * 128], rhs=wkvb[:, 512:1024], start=True, stop=True),
                 reads=[("kvan", t // 4), "wkvb"], writes=[("pb", bv)])
            S.op("dve", lambda e, t=t, bv=bv: e.tensor_copy(out=Vm[:, t, :, 0:64], in_=bank(bv).rearrange("p (h d) -> p h d", d=64)),
                 reads=[("pb", bv)], writes=[("Vm", t)])
        jobs = []
        jn = [0]
        for h in range(8):
            hb = h % 2
            def head_proj(h=h, hb=hb):
                for bi, (a, b) in enumerate(BLK_ALL):
                    n = b - a
                    br, bt, bk_ = 4, 5, 4
                    for kc in range(2):
                        S.op("pe", lambda e, kc=kc: e.matmul(bank(br)[0:96, 0:n], lhsT=wqb[:, h, 0, kc, :], rhs=qan[:, kc, a:b], start=(kc == 0), stop=(kc == 1)),
                             reads=["wqb", ("qan", kc, bi)], writes=[("pb", br)])
                    for kc in range(2):
                        S.op("pe", lambda e, kc=kc: e.matmul(bank(bt)[0:96, 0:n], lhsT=wqb[:, h, 1, kc, :], rhs=qan[:, kc, a:b], start=(kc == 0), stop=(kc == 1)),
                             reads=["wqb", ("qan", kc, bi)], writes=[("pb", bt)])
                    S.op("act", lambda e: e.activation(out=qm[hb][0:64, a:b], in_=bank(br)[0:64, 0:n], func=AF.Copy), reads=[("pb", br)], writes=[("qm", hb, bi, 0)])
                    i = bi % 2
                    S.op("dve", lambda e: e.tensor_tensor(out=t1[i][64:96, 0:n], in0=bank(br)[64:96, 0:n], in1=rope32[64:96, 0, a:b], op=ALU.mult),
                         reads=[("pb", br), ("rope32", 0)], writes=[("t1", i)])
                    S.op("dve", lambda e: e.tensor_tensor(out=t2[i][64:96, 0:n], in0=bank(bt)[64:96, 0:n], in1=rope32[64:96, 1, a:b], op=ALU.mult),
                         reads=[("pb", bt), ("rope32", 1)], writes=[("t2", i)])
                    S.op("pool", lambda e: e.tensor_tensor(out=qm[hb][64:96, a:b], in0=t1[i][64:96, 0:n], in1=t2[i][64:96, 0:n], op=ALU.add),
                         reads=[("t1", i), ("t2", i)], writes=[("qm", hb, bi, 1)])
                    S.op("pe", lambda e: e.matmul(bank(bk_)[0:64, 0:n], lhsT=wkvb[:, h * 64:(h + 1) * 64], rhs=kvan[:, a:b], start=True, stop=True),
                         reads=["wkvb", ("kvan", bi)], writes=[("pb", bk_)])
                    S.op("act", lambda e: e.activation(out=km[hb][0:64, a:b], in_=bank(bk_)[0:64, 0:n], func=AF.Copy), reads=[("pb", bk_)], writes=[("km", hb, bi, 0)])
                    S.op("pool", lambda e: e.tensor_copy(out=km[hb][64:96, a:b], in_=krT[64:96, a:b]), reads=[("krT", bi)], writes=[("km", hb, bi, 1)])

            qblocks = [(0, 256, [0, 1])] + [(a, b, list(range(NT))) for (a, b) in BLK_LAT]
            for qi, (qa_, qb_, keys) in enumerate(qblocks):
                nq = (qb_ - qa_) // 128
                aset = jn[0] % 2
                jn[0] += 1
                accb = 4 + aset if False else (6 + aset)

                def acc(qt, c, accb=accb):
                    return bank(accb)[:, qt * 65:(qt + 1) * 65], ("pb", accb), qt == 0

                def fin(h=h, hb=hb, qa_=qa_, nq=nq, accb=accb, aset=aset):
                    av = bank(accb)[:, 0:nq * 65].rearrange("p (h d) -> p h d", d=65)
                    akeys = [("pb", accb)]
                    S.op("dve", lambda e: e.reciprocal(out=rden[aset][:, 0:nq], in_=av[:, :, 64]), reads=akeys, writes=[("rden", aset)])
                    for qt in range(nq):
                        t = qa_ // 128 + qt
                        S.op("dve", lambda e, qt=qt, t=t: e.tensor_scalar(out=ostg[(h // 2) % 2][:, t, hb * 64:(hb + 1) * 64], in0=av[:, qt, 0:64], scalar1=rden[aset][:, qt:qt + 1],
                                                                      scalar2=None, op0=ALU.mult),
                             reads=[("pb", accb), ("rden", aset)], writes=[("ostg", (h // 2) % 2, t, hb)])
                    if hb == 1:
                        def late():
                            tb = P2[2][:].bitcast(BF16)[:, 1024:2048]
                            t0 = qa_ // 128
                            for qt in range(nq):
                                t = t0 + qt
                                S.op("pe", lambda e, t=t, qt=qt: e.transpose(out=tb[:, qt * 128:(qt + 1) * 128], in_=ostg[(h // 2) % 2][:, t, :], identity=ident_b[:]),
                                     reads=[("ostg", (h // 2) % 2, t, 0), ("ostg", (h // 2) % 2, t, 1), "ident_b"], writes=[("pb", 5)])
                            S.op("dve", lambda e: e.tensor_copy(out=oT[:, h // 2, t0 * 128:(t0 + nq) * 128], in_=tb[:, 0:nq * 128]),
                                 reads=[("pb", 5)], writes=[("hT", t0 + qt, h // 2) for qt in range(nq)])
                        return late
                    return None

                def qk_(j, hb=hb, qa_=qa_, qb_=qb_):
                    return [(0, qb_ - qa_, km[hb][0:96, j * 128:(j + 1) * 128], qm[hb][0:96, qa_:qb_], None)]

                def pv_(j, h=h, accb=accb, nq=nq):
                    return [(bank(accb)[:, qt * 65:(qt + 1) * 65], ("pb", accb), qt == 0, qt * 128, Vm[:, j, h, :]) for qt in range(nq)]

                jobs.append(dict(
                    pre=(head_proj if qi == 0 else None),
                    keys=keys, W=qb_ - qa_, scale=96 ** -0.5,
                    qk=qk_, qk_reads=lambda j, hb=hb: [("km", hb, j // 4, 0), ("km", hb, j // 4, 1)] + [("qm", hb, bi, r) for bi in range(5) for r in range(2)],
                    pv=pv_, pv_reads=lambda j: [("Vm", j), "Vm1"],
                    mask=None, fin=fin))
        hj = 0
        while hj < len(jobs):
            jobs[hj]["pre"]()
            run_attention(jobs[hj:hj + 5], ET)
            hj += 5
        checkpoint("mla0", {"oT": (hT[:], [128, 8, NTOK], BF16)})
        A.release(mA)

        mR = A.mark()
        wo = A.alloc("wo", [128, 8, D], BF16)
        for hf in range(2):
            S.dma("pool", ("wo", hf), lambda e, hf=hf: e.dma_start(out=wo[:, hf * 4:(hf + 1) * 4, :], in_=wout0_d[:, hf * 4:(hf + 1) * 4, :], max_dma_last_dim=8192), writes=["wo"])
        fl = []
        if mod1 is not None:
            setup1, fl, finish1 = mod1
            setup1()
        phase_R(list(range(NT)), lambda c, t: oT[:, c, t * 128:(t + 1) * 128], lambda c, t: [("hT", t, c)], 8, wo, "wo",
                src, True, True, dst, 0, None, fillers=fl)
        if mod1 is not None:
            finish1()
        A.release(mR)

    def ffn_sublayer(l, src, dst, dst_row0, tiles):
        need_c = tiles[0] < NCTX_T
        load_sublayer_consts(l, 1, need_c)
        phase_P(l, src, tiles, 1)
        tok0, tok1 = tiles[0] * 128, (tiles[-1] + 1) * 128
        blocks = [(a, min(a + 512, tok1)) for a in range(tok0, tok1, 512)]
        m0 = A.mark()
        convp = A.alloc("convp", [128, NFC, 4], F32)
        wsl = [A.alloc(f"wg{i}", [128, 8, 512], BF16) for i in range(3)]
        wdn = A.alloc("wdn", [128, FCP, D], BF16)
        hid = A.alloc("hid", [128, FCP, NTOK], BF16)
        cdma(lambda e: e.dma_start(out=convp[:], in_=convp_d[l]), writes=["convp"])
        def gcol(g):
            return g + 1 if g < 256 else g + 3
        for ps in range(NPASS):
            m1 = A.mark()
            G = [A.alloc(f"G{i}", [128, NTOK + 4], F32) for i in range(2)]
            T1 = [A.alloc(f"T1_{i}", [128, NTOK + 4], F32) for i in range(2)]
            for i in range(2):
                for cpad in (0, 257, 2307):
                    w = 2 if cpad == 257 else 1
                    S.op("dve", lambda e, i=i, cpad=cpad, w=w: e.memset(G[i][:, cpad:cpad + w], 0.0), writes=[("Gpad", i)])
            for ci in range(FCP):
                fc = ps * FCP + ci
                blk11 = fc // 2
                sub = fc % 2
                sl = blk11 % 3
                if sub == 0 or ci == 0:
                    S.dma("pool", ("wg", sl), lambda e, blk11=blk11, sl=sl: e.dma_start(out=wsl[sl][:], in_=wgu_d[l, blk11], max_dma_last_dim=8192),
                          writes=[("wg", sl)])
                if ci == 2:
                    S.dma("pool", ("wdn", 0), lambda e, ps=ps: e.dma_start(out=wdn[:], in_=wdown_d[l, ps], max_dma_last_dim=8192), writes=["wdn"])
                wv = wsl[sl][:].rearrange("p k (g f) -> p k g f", g=2)
                gi = ci % 2
                c0, c1 = gcol(tok0), gcol(tok1 - 1) + 1
                for bi, (a, b) in enumerate(blocks):
                    n = b - a
                    st_ = (ci * len(blocks) + bi) % 4
                    bg, bu = 2 * st_, 2 * st_ + 1
                    for k in range(8):
                        S.op("pe", lambda e, k=k, bg=bg, wv=wv: e.matmul(bank(bg)[:, 0:n], lhsT=wv[:, k, 0, sub * 128:(sub + 1) * 128], rhs=hT[:, k, a:b], start=(k == 0), stop=(k == 7)),
                             reads=[("wg", sl)] + hT_reads(a, b, k), writes=[("pb", bg)])
                    for k in range(8):
                        S.op("pe", lambda e, k=k, bu=bu, wv=wv: e.matmul(bank(bu)[:, 0:n], lhsT=wv[:, k, 1, sub * 128:(sub + 1) * 128], rhs=hT[:, k, a:b], start=(k == 0), stop=(k == 7)),
                             reads=[("wg", sl)] + hT_reads(a, b, k), writes=[("pb", bu)])
                    segs = []
                    if a < 256 < b:
                        segs = [(a, 256), (256, b)]
                    else:
                        segs = [(a, b)]
                    for (sa, sb) in segs:
                        S.op("act", lambda e, sa=sa, sb=sb, bg=bg, gi=gi: e.activation(out=G[gi][:, gcol(sa):gcol(sa) + sb - sa], in_=bank(bg)[:, sa - a:sb - a], func=AF.Copy),
                             reads=[("pb", bg)], writes=[("G", gi, bi)])
                    S.op("dve", lambda e, bu=bu, ci=ci: e.tensor_copy(out=hid[:, ci, a:b], in_=bank(bu)[:, 0:n]), reads=[("pb", bu)], writes=[("hid", ci, bi)])
                gk = [("G", gi, bi) for bi in range(len(blocks))] + [("Gpad", gi)]
                S.op("act", lambda e, gi=gi, fc=fc: e.activation(out=T1[gi][:, c0:c1], in_=G[gi][:, c0:c1], func=AF.Identity, scale=convp[:, fc, 1:2], bias=convp[:, fc, 3:4]),
                     reads=gk + ["convp"], writes=[("T1", gi)])
                S.op("dve", lambda e, gi=gi, fc=fc: e.scalar_tensor_tensor(out=T1[gi][:, c0:c1], in0=G[gi][:, c0 - 1:c1 - 1], scalar=convp[:, fc, 0:1], in1=T1[gi][:, c0:c1], op0=ALU.mult, op1=ALU.add),
                     reads=gk + ["convp", ("T1", gi)], writes=[("T1", gi)])
                S.op("dve", lambda e, gi=gi, fc=fc: e.scalar_tensor_tensor(out=T1[gi][:, c0:c1], in0=G[gi][:, c0 + 1:c1 + 1], scalar=convp[:, fc, 2:3], in1=T1[gi][:, c0:c1], op0=ALU.mult, op1=ALU.add),
                     reads=gk + ["convp", ("T1", gi)], writes=[("T1", gi)])
                S.op("act", lambda e, gi=gi: e.activation(out=T1[gi][:, c0:c1], in_=T1[gi][:, c0:c1], func=AF.Silu), reads=[("T1", gi)], writes=[("T1", gi)])
                segs = [(tok0, 256), (256, tok1)] if tok0 < 256 else [(tok0, tok1)]
                for (sa, sb) in segs:
                    S.op("dve", lambda e, sa=sa, sb=sb, gi=gi, ci=ci: e.tensor_tensor(out=hid[:, ci, sa:sb], in0=hid[:, ci, sa:sb], in1=T1[gi][:, gcol(sa):gcol(sa) + sb - sa], op=ALU.mult),
                         reads=[("T1", gi)] + [("hid", ci, bi) for bi in range(len(blocks))], writes=[("hid", ci, bi) for bi in range(len(blocks))])
            A.release(m1)
            phase_R(tiles, lambda c, t: hid[:, c, t * 128:(t + 1) * 128],
                    lambda c, t: [("hid", c, bi) for bi in range(len(blocks)) if blocks[bi][0] <= t * 128 < blocks[bi][1]],
                    FCP, wdn, "wdn", src, ps == 0, ps == NPASS - 1, dst, dst_row0, ybuf)
        A.release(m0)

    def layer1_attention(src, dst):
        load_sublayer_consts(1, 0, False)
        lambda_init = 0.8 - 0.6 * math.exp(-0.3 * 1)
        mA = A.mark()
        oT = A.alloc("oT1", [128, 8, 2048], BF16)
        rope64 = A.alloc("rope64b", [128, 2, NTOK], F32)
        ET = [A.alloc(f"ETb{i}", [128, 512], BF16) for i in range(4)]
        t1 = [A.alloc(f"t1b{i}", [128, 256], F32) for i in range(2)]
        t2 = [A.alloc(f"t2b{i}", [128, 256], F32) for i in range(2)]
        wo = A.alloc("wo1", [128, 8, D], BF16)
        gsub = A.alloc("gsub", [128, 128], F32)
        lamt = A.alloc("lamt", [128, 4, 64], F32)
        lams = A.alloc("lams", [128, 8], F32)
        accs = [A.alloc(f"accs{i}", [128, 4, 129], F32) for i in range(2)]
        rr = [A.alloc(f"rr{i}", [128, 8], F32) for i in range(2)]
        uu = [A.alloc(f"uu{i}", [128, 128], F32) for i in range(2)]
        avv = [A.alloc(f"avv{i}", [128, 128], F32) for i in range(2)]
        junk = A.alloc("junk", [128, 128], BF16)
        otk = [A.alloc(f"otkb{i}", [128, 128], BF16) for i in range(4)]
        mH = A.mark()
        V1 = [A.alloc(f"V1_{i}", [128, NT, 130], BF16) for i in range(2)]
        qT = [A.alloc(f"qT1_{i}", [128, 2048], BF16) for i in range(2)]
        kT0 = [A.alloc(f"kT0_{i}", [128, NTOK], BF16) for i in range(2)]
        kT1 = [A.alloc(f"kT1_{i}", [128, NTOK], BF16) for i in range(2)]
        w1 = [A.alloc(f"w1h_{i}", [128, 5, 8, 128], BF16) for i in range(2)]

        for r in range(2):
            cdma(lambda e, r=r: e.dma_start(out=rope64[:, r, :], in_=rope64_d[r]), writes=[("rope64", r)])
        cdma(lambda e: e.dma_start(out=gsub[:], in_=subg_d.partition_broadcast(128)), writes=["gsub"])
        S.op("dve", lambda e: e.tensor_scalar(out=gsub[:], in0=gsub[:], scalar1=(1.0 - lambda_init), scalar2=None, op0=ALU.mult), reads=["gsub"], writes=["gsub"])
        for i in range(4):
            cdma(lambda e, i=i: e.dma_start(out=lamt[:, i, :], in_=lamv_d[i].partition_broadcast(128)), writes=[("lamt", i)])
        for i in range(2):
            S.op("dve", lambda e, i=i: e.tensor_tensor(out=lamt[:, 2 * i, :], in0=lamt[:, 2 * i, :], in1=lamt[:, 2 * i + 1, :], op=ALU.mult),
                 reads=[("lamt", 2 * i), ("lamt", 2 * i + 1)], writes=[("lamt", 2 * i)])
            S.op("dve", lambda e, i=i: e.reduce_sum(out=lams[:, i:i + 1], in_=lamt[:, 2 * i, :], axis=mybir.AxisListType.X), reads=[("lamt", 2 * i)], writes=["lams"])
        S.op("act", lambda e: e.activation(out=lams[:, 2:4], in_=lams[:, 0:2], func=AF.Exp), reads=["lams"], writes=["lams"])
        S.op("dve", lambda e: e.scalar_tensor_tensor(out=lams[:, 4:5], in0=lams[:, 3:4], scalar=-lambda_init, in1=lams[:, 2:3], op0=ALU.add, op1=ALU.subtract),
             reads=["lams"], writes=["lams"])
        for i in range(2):
            S.op("dve", lambda e, i=i: e.memset(V1[i][:, :, 128:129], 1.0), writes=[("V11", i)])
            S.op("dve", lambda e, i=i: e.memset(kT0[i][64:128, :], 0.0), writes=[("kTz", i)])
            S.op("dve", lambda e, i=i: e.memset(kT1[i][0:64, :], 0.0), writes=[("kTz", i)])
        for hf in range(2):
            S.dma("pool", ("wo", hf), lambda e, hf=hf: e.dma_start(out=wo[:, hf * 4:(hf + 1) * 4, :], in_=wout1_d[:, hf * 4:(hf + 1) * 4, :], max_dma_last_dim=8192), writes=["wo"])

        tpb = P2[3][:].bitcast(BF16)[:, 0:1024]
        PB = 7
        jn = [0]
        uc = [0]

        def proj_units(h, banks=(7,)):
            hb = h % 2
            units = []

            def load_w():
                S.dma("pool", ("w1", hb), lambda e: e.dma_start(out=w1[hb][:], in_=w1h_d[h], max_dma_last_dim=8192), writes=[("w1", hb)])

            def qk_unit(chunk, a, b, dsts):
                def f():
                    n = b - a
                    i = uc[0] % 2
                    PB = banks[uc[0] % len(banks)]
                    uc[0] += 1
                    for r in range(2):
                        for k in range(8):
                            S.op("pe", lambda e, k=k, r=r: e.matmul(bank(PB)[:, r * 256:r * 256 + n], lhsT=w1[hb][:, chunk + r, k, :], rhs=hT[:, k, a:b], start=(k == 0), stop=(k == 7)),
                                 reads=[("w1", hb)] + hT_reads(a, b, k), writes=[("pb", PB)])
                    S.op("dve", lambda e: e.tensor_tensor(out=t1[i][:, 0:n], in0=bank(PB)[:, 0:n], in1=rope64[:, 0, a:b], op=ALU.mult),
                         reads=[("pb", PB), ("rope64", 0)], writes=[("t1", i)])
                    S.op("dve", lambda e: e.tensor_tensor(out=t2[i][:, 0:n], in0=bank(PB)[:, 256:256 + n], in1=rope64[:, 1, a:b], op=ALU.mult),
                         reads=[("pb", PB), ("rope64", 1)], writes=[("t2", i)])
                    for (p0, p1, dst_ap, wkey) in dsts:
                        S.op("pool", lambda e, p0=p0, p1=p1, dst_ap=dst_ap: e.tensor_tensor(out=dst_ap, in0=t1[i][p0:p1, 0:n], in1=t2[i][p0:p1, 0:n], op=ALU.add),
                             reads=[("t1", i), ("t2", i)], writes=[wkey])
                return f

            def v_unit(t0):
                def f():
                    PB = banks[uc[0] % len(banks)]
                    uc[0] += 1
                    tl = list(range(t0, min(t0 + 2, NT)))
                    for ti, t in enumerate(tl):
                        for k in range(8):
                            S.op("pe", lambda e, k=k, t=t, ti=ti: e.matmul(bank(PB)[:, ti * 128:(ti + 1) * 128], lhsT=hT[:, k, t * 128:(t + 1) * 128], rhs=w1[hb][:, 4, k, :],
                                                                          start=(k == 0), stop=(k == 7)),
                                 reads=[("w1", hb), ("hT", t, k)], writes=[("pb", PB)])
                    nt_ = len(tl)
                    S.op("dve", lambda e: e.tensor_copy(out=V1[hb][:, t0:t0 + nt_, 0:128], in_=bank(PB)[:, 0:nt_ * 128].rearrange("p (t d) -> p t d", d=128)),
                         reads=[("pb", PB)], writes=[("V1", hb, t0 // 2)])
                return f

            for u in range(9):
                a = u * 256
                units.append(qk_unit(2, a, a + 256, [(0, 64, kT0[hb][0:64, a:a + 256], ("kT", hb, 0, u)), (64, 128, kT1[hb][64:128, a:a + 256], ("kT", hb, 1, u))]))
            for u in range(9):
                units.append(v_unit(2 * u))
            for u in range(8):
                a = 256 + u * 256
                units.append(qk_unit(0, a, a + 256, [(0, 128, qT[hb][:, u * 256:(u + 1) * 256], ("qT", hb, u))]))
            return load_w, units

        def head_jobs(h):
            hb = h % 2
            jobs = []
            for qb in range(8):
                qa = qb * 256
                aset = jn[0] % 2
                jn[0] += 1

                def accinfo(qt, c):
                    idx = qt * 2 + c
                    bk = 4 + idx // 3
                    sl_ = idx % 3
                    return bank(bk)[:, sl_ * 129:(sl_ + 1) * 129], ("pb", bk), sl_ == 0

                def qk_(j, qa=qa):
                    return [(0, 256, kT0[hb][:, j * 128:(j + 1) * 128], qT[hb][:, qa:qa + 256], None),
                            (256, 256, kT1[hb][:, j * 128:(j + 1) * 128], qT[hb][:, qa:qa + 256], None)]

                def pv_(j):
                    ops = []
                    for qt in range(2):
                        for c in range(2):
                            ap_, key_, fib = accinfo(qt, c)
                            ops.append((ap_, key_, fib, c * 256 + qt * 128, V1[hb][:, j, 0:129]))
                    return ops

                def fin(qa=qa, aset=aset):
                    ac = accs[aset]
                    S.op("act", lambda e: e.activation(out=ac[:, 0:3, :], in_=bank(4)[:, 0:3 * 129].rearrange("p (a d) -> p a d", d=129), func=AF.Copy),
                         reads=[("pb", 4)], writes=[("accs", aset, 0)])
                    S.op("act", lambda e: e.activation(out=ac[:, 3, :], in_=bank(5)[:, 0:129], func=AF.Copy), reads=[("pb", 5)], writes=[("accs", aset, 1)])
                    for qt in range(2):
                        i = qt
                        ak = [("accs", aset, 0), ("accs", aset, 1)]
                        S.op("dve", lambda e, qt=qt, i=i: e.reciprocal(out=rr[i][:, 0:2], in_=ac[:, qt * 2:qt * 2 + 2, 128]), reads=ak, writes=[("rr", i)])
                        S.op("dve", lambda e, i=i: e.tensor_tensor(out=rr[i][:, 2:3], in0=rr[i][:, 1:2], in1=lams[:, 4:5], op=ALU.mult), reads=[("rr", i), "lams"], writes=[("rr", i)])
                        S.op("dve", lambda e, qt=qt, i=i: e.tensor_scalar(out=uu[i][:], in0=ac[:, qt * 2 + 1, 0:128], scalar1=rr[i][:, 2:3], scalar2=None, op0=ALU.mult),
                             reads=ak + [("rr", i)], writes=[("uu", i)])
                        S.op("dve", lambda e, qt=qt, i=i: e.scalar_tensor_tensor(out=avv[i][:], in0=ac[:, qt * 2, 0:128], scalar=rr[i][:, 0:1], in1=uu[i][:], op0=ALU.mult, op1=ALU.add),
                             reads=ak + [("rr", i), ("uu", i)], writes=[("avv", i)])

                    def late0():
                        for qt in range(2):
                            i = qt
                            S.op("act", lambda e, i=i: e.activation(out=junk[:], in_=avv[i][:], func=AF.Square, accum_out=rr[i][:, 3:4]), reads=[("avv", i)], writes=["junk", ("rr", i)])
                            rsqrt_small(rr[i][:, 4:5], rr[i][:, 3:4], 1.0 / 128.0, 1, [("rr", i)], ("rr", i))

                    def late1():
                        for qt in range(2):
                            i = qt
                            oi = aset * 2 + qt
                            S.op("dve", lambda e, i=i, oi=oi: e.scalar_tensor_tensor(out=otk[oi][:], in0=avv[i][:], scalar=rr[i][:, 4:5], in1=gsub[:], op0=ALU.mult, op1=ALU.mult),
                                 reads=[("avv", i), ("rr", i), "gsub"], writes=[("otk", oi)])

                    def late2():
                        for qt in range(2):
                            oi = aset * 2 + qt
                            S.op("pe", lambda e, oi=oi: e.transpose(out=tpb[:, oi * 128:(oi + 1) * 128], in_=otk[oi][:], identity=ident_b[:]),
                                 reads=[("otk", oi), "ident_b"], writes=[("pb", 6)])
                        S.op("dve", lambda e: e.tensor_copy(out=oT[:, h, qa:qa + 256], in_=tpb[:, aset * 256:aset * 256 + 256]),
                             reads=[("pb", 6)], writes=[("oT", h, qa // 128), ("oT", h, qa // 128 + 1)])
                    return [(7, late0), (11, late1), (14, late2)]

                jobs.append(dict(
                    keys=list(range(NT)), W=512, scale=64 ** -0.5,
                    qk=qk_, qk_reads=lambda j, qb=qb: [("kT", hb, 0, j // 2), ("kT", hb, 1, j // 2), ("kTz", hb), ("qT", hb, qb)],
                    pv=pv_, pv_reads=lambda j: [("V1", hb, j // 2), ("V11", hb)],
                    mask=None, fin=fin))
            return jobs

        lw, units = proj_units(0, banks=(4, 5, 6, 7))
        lw()
        ku, vu, qu = units[0:9], units[9:18], units[18:26]
        sched = []
        for t in range(NT):
            fs = []
            if t % 2 == 1:
                fs.append(ku[(t - 1) // 2])
                fs.append(vu[(t - 1) // 2])
                if t >= 3:
                    fs.append(qu[(t - 3) // 2])
            sched.append(lambda fs=fs: [f() for f in fs])
        phase_P(1, src, list(range(NT)), 0, fillers=sched)
        for h in range(8):
            fl = []
            if h + 1 < 8:
                lw, fl = proj_units(h + 1)
                lw()
            run_attention(head_jobs(h), ET, fl)
        A.release(mH)

        lat = list(range(NCTX_T, NT))
        phase_R(lat, lambda c, t: oT[:, c, (t - NCTX_T) * 128:(t - NCTX_T + 1) * 128], lambda c, t: [("oT", c, t - NCTX_T)], 8, wo, "wo",
                src, True, True, dst, 0, None)
        A.release(mA)

    cur_src = x_all
    dbg_x = None
    if stop_after is not None:
        dbg_x = nc.dram_tensor("dbg_x", [NTOK, D], F32, kind="ExternalOutput").ap()
    try:
      for l in layers:
          if l == 0:
              mod0 = modulation_units(0, True, 6, 7)
              mod1 = modulation_units(1, False, 6, 7) if 1 in layers else None
              if stop_after == "attn0":
                  layer0_attention(cur_src, dbg_x, mod0, mod1)
                  break
              layer0_attention(cur_src, xres[0], mod0, mod1)
              if stop_after == "ffn0":
                  ffn_sublayer(0, xres[0], dbg_x, 0, list(range(NT)))
                  break
              ffn_sublayer(0, xres[0], xres[1], 0, list(range(NT)))
              cur_src = xres[1]
          else:
              if 0 not in layers:
                  su, bl, fi = modulation_units(1, False, 6, 7)
                  su()
                  for f in bl:
                      f()
                  fi()
              if stop_after == "attn1":
                  layer1_attention(cur_src, dbg_x)
                  break
              layer1_attention(cur_src, xres[0])
              ffn_sublayer(1, xres[0], out_d, 256, list(range(NCTX_T, NT)))
    except _Stop:
        pass
    S.barrier()
    nsem = S.emit()
    info = dict(nsem=nsem, peak=A.peak - A.base, cnt=dict(S.cnt))
    return nc, info


def _rope_tables(dim):
    rows = 2048 // 64
    row = np.repeat(np.arange(rows), 64).astype(np.float32)
    col = np.tile(np.arange(64), rows).astype(np.float32)
    q = dim // 4
    freqs = (np.float32(10000.0) ** (-np.arange(q, dtype=np.float32) / np.float32(q))).astype(np.float32)
    ar = row[:, None] * freqs
    ac = col[:, None] * freqs
    cos = np.concatenate([np.cos(ar), np.cos(ar), np.cos(ac), np.cos(ac)], -1).astype(np.float32)
    sin = np.concatenate([np.sin(ar), np.sin(ar), np.sin(ac), np.sin(ac)], -1).astype(np.float32)
    sign = np.concatenate([-np.ones(q), np.ones(q), -np.ones(q), np.ones(q)]).astype(np.float32)
    perm = np.concatenate([np.arange(q, 2 * q), np.arange(0, q), np.arange(3 * q, 4 * q), np.arange(2 * q, 3 * q)])
    cosT = np.ones((dim, NTOK), np.float32)
    sinT = np.zeros((dim, NTOK), np.float32)
    cosT[:, 256:] = cos.T
    sinT[:, 256:] = (sin * sign[None, :]).T
    return cosT, sinT, perm


def _prep_shared(inp):
    f32 = np.float32
    cos64, sin64, perm64 = _rope_tables(64)
    cos32, sin32, perm32 = _rope_tables(32)
    sh = {}
    w_mod = inp["w_mod"]
    sh["wmod"] = np.ascontiguousarray(w_mod.reshape(2, 8, 128, 12, 512).transpose(0, 3, 2, 1, 4))
    sh["bmod_pp"] = np.ascontiguousarray(inp["b_mod"].reshape(2, 6, 8, 128).transpose(0, 3, 1, 2).reshape(2, 128, 48))
    sh["bmod"] = np.ascontiguousarray(inp["b_mod"])
    sh["lnv"] = np.ascontiguousarray(np.stack([inp["ln1_g"], inp["ln1_b"], inp["ln2_g"], inp["ln2_b"]], 1))
    g = inp["ffn_w_gate"].reshape(2, 8, 128, 11, 256).transpose(0, 3, 2, 1, 4)
    u = inp["ffn_w_up"].reshape(2, 8, 128, 11, 256).transpose(0, 3, 2, 1, 4)
    sh["wgu"] = np.ascontiguousarray(np.stack([g, u], 4).reshape(2, 11, 128, 8, 512))
    cw = inp["ffn_conv_w"].reshape(2, 3, NFC, 128).transpose(0, 3, 2, 1)
    cb = inp["ffn_conv_b"].reshape(2, NFC, 128).transpose(0, 2, 1)[..., None]
    sh["convp"] = np.ascontiguousarray(np.concatenate([cw, cb], -1))
    sh["wdown"] = np.ascontiguousarray(inp["ffn_w_down"].reshape(2, NPASS, FCP, 128, D).transpose(0, 1, 3, 2, 4))
    W = inp["ab_w_in"][0]
    z = lambda n: np.zeros((D, n), f32)
    chunks = [W[:, 0:128], W[:, 128:256]]
    qs = W[:, 256:768].reshape(D, 8, 64)
    for c in range(4):
        chunks.append(np.concatenate([qs[:, c], qs[:, 4 + c]], 1))
    for c in range(4):
        chunks.append(np.concatenate([qs[:, c][:, perm64], qs[:, 4 + c][:, perm64]], 1))
    chunks.append(W[:, 768:896])
    kr = W[:, 896:928]
    chunks.append(np.concatenate([z(64), kr, z(32)], 1))
    chunks.append(np.concatenate([z(64), kr[:, perm32], z(32)], 1))
    ks = W[:, 928:1056].reshape(D, 2, 64)
    chunks.append(np.concatenate([ks[:, 0], ks[:, 1]], 1))
    chunks.append(np.concatenate([ks[:, 0][:, perm64], ks[:, 1][:, perm64]], 1))
    chunks.append(W[:, 1056:1184])
    w0 = np.stack(chunks, 0).reshape(16, 8, 128, 128).transpose(2, 0, 1, 3)
    sh["w0"] = np.ascontiguousarray(w0)
    Wq = inp["mla_w_qb"][0].reshape(256, 8, 96)
    raw = Wq
    rot = np.concatenate([np.zeros((256, 8, 64), f32), Wq[:, :, 64:96][:, :, perm32]], 2)
    wqb = np.stack([raw, rot], 0)
    wqb = wqb.reshape(2, 2, 128, 8, 96).transpose(2, 3, 0, 1, 4)
    sh["wqb"] = np.ascontiguousarray(wqb)
    Wkv = inp["mla_w_kvb"][0].reshape(128, 8, 128)
    sh["wkvb"] = np.ascontiguousarray(np.concatenate([Wkv[:, :, 0:64].reshape(128, 512), Wkv[:, :, 64:128].reshape(128, 512)], 1))
    sh["qnorm_pp"] = np.ascontiguousarray(inp["mla_q_norm"][0].reshape(2, 128).T)
    sh["kvnorm_pp"] = np.ascontiguousarray(inp["mla_kv_norm"][0].reshape(128, 1))
    sh["sink"] = np.ascontiguousarray(inp["swa_sink"][0])
    sh["wout0"] = np.ascontiguousarray(inp["ab_w_out"][0].reshape(8, 128, D).transpose(1, 0, 2))
    W1 = inp["diff_w_in"][0]
    perm128 = np.concatenate([perm64, 64 + perm64])
    heads = []
    for h in range(8):
        q = W1[:, h * 128:(h + 1) * 128]
        k = W1[:, 1024 + h * 128:1024 + (h + 1) * 128]
        v = W1[:, 2048 + h * 128:2048 + (h + 1) * 128]
        hw = np.stack([q, q[:, perm128], k, k[:, perm128], v], 0)
        heads.append(hw.reshape(5, 8, 128, 128).transpose(2, 0, 1, 3))
    sh["w1h"] = np.ascontiguousarray(np.stack(heads, 0))
    sh["wout1"] = np.ascontiguousarray(inp["diff_w_out"][0].reshape(8, 128, D).transpose(1, 0, 2))
    sh["lamv"] = np.ascontiguousarray(np.stack([inp["diff_lam_q1"][0], inp["diff_lam_k1"][0], inp["diff_lam_q2"][0], inp["diff_lam_k2"][0]], 0))
    sh["subg"] = np.ascontiguousarray(inp["diff_subln_g"][0])
    sh["ident_f"] = np.eye(128, dtype=f32)
    sh["ident_b"] = np.eye(128, dtype=f32).astype(ml_dtypes.bfloat16)
    jj = np.arange(128)[:, None]
    ii = np.arange(128)[None, :]
    mP = (jj >= ii).astype(f32)
    mN = (jj <= ii).astype(f32)
    sh["masks"] = np.stack([np.tile(mP, (1, 4)), np.tile(mN, (1, 4))], 0).astype(ml_dtypes.bfloat16)
    sh["rope64"] = np.ascontiguousarray(np.stack([np.tile(cos64, (2, 1)), np.tile(sin64, (2, 1))], 0))
    r32 = np.zeros((2, 128, NTOK), f32)
    r32[0, 64:96] = cos32
    r32[1, 64:96] = sin32
    sh["rope32"] = r32
    return sh


_CACHE = {}


def _get_program(layers, dbg=()):
    key = (tuple(layers), tuple(dbg))
    if key not in _CACHE:
        _CACHE[key] = build_program(layers, dbg)
    return _CACHE[key]


def kernel(**inputs):
    inp = {k: np.asarray(v, dtype=np.float32) for k, v in inputs.items()}
    sh = _prep_shared(inp)
    B = inp["x"].shape[0]
    in_maps = []
    for b in range(B):
        m = dict(sh)
        m["x_all"] = np.ascontiguousarray(np.concatenate([inp["ctx"][b], inp["x"][b]], 0))
        cc = np.stack([inp["c"][b], inp["c_ctx"]], 0)
        m["cT"] = np.ascontiguousarray(cc.reshape(2, 8, 128).transpose(2, 1, 0))
        in_maps.append(m)
    nc, info = _get_program((0, 1))
    res = run_bass_kernel_spmd(nc, in_maps, core_ids=list(range(B)))
    out = np.stack([np.asarray(r["out"], dtype=np.float32) for r in res.results], 0)
    return out
```
